# Optimizing a Trainium2 kernel written in Bass

```python
import jax, jax.numpy as jnp
from jax import lax
import numpy as np

D_MODEL = 1024
BATCH = 8
SEQ = 2048
DEPTH = 2
DEC_BATCH = 8
DEC_SEQ = 64
PAST_LEN = 2048

CHUNK = 64
EPS = 1e-6
N_BRANCH = 3
POOL_WINDOWS = (2, 4, 8, 16)
POOL_GROUPS = 4
POOL_GW = D_MODEL // 8
D_A = POOL_GROUPS * POOL_GW
POOL_HIST = 16 - 1
SGU_LEN = 128
SGU_GROUPS = 4
D_B = D_MODEL // 2
SGU_GW = D_B // SGU_GROUPS
GLA_HEADS = 4
GLA_DK = D_MODEL // 2 // GLA_HEADS
GLA_DV = D_MODEL // GLA_HEADS
D_CK = GLA_HEADS * GLA_DK
D_CV = GLA_HEADS * GLA_DV
GLA_RANK = 16
GLA_NORMALIZER = 16.0
GLA_BLOCK = CHUNK // 4
SPLIT_SIZES = (D_A, D_A, D_B, D_B, D_B, D_CK, D_CK, D_CV, D_CV, GLA_RANK, N_BRANCH * D_MODEL)
D_IN = 2 * D_A + 3 * D_B + 2 * D_CK + 2 * D_CV + GLA_RANK + N_BRANCH * D_MODEL

kernel_name = "hybrid_pool_sgu_gla_streaming_step"


def rmsnorm(x, g):
    xf = x.astype(jnp.float32)
    y = xf * lax.rsqrt(jnp.mean(xf * xf, axis=-1, keepdims=True) + EPS)
    return (y * g.astype(jnp.float32)).astype(x.dtype)


def layernorm(x, g):
    xf = x.astype(jnp.float32)
    mu = jnp.mean(xf, axis=-1, keepdims=True)
    xc = xf - mu
    y = xc * lax.rsqrt(jnp.mean(xc * xc, axis=-1, keepdims=True) + EPS)
    return (y * g.astype(jnp.float32)).astype(x.dtype)


def split_cols(z):
    idx = [int(i) for i in np.cumsum(SPLIT_SIZES)[:-1]]
    return jnp.split(z, idx, axis=-1)


def ada_modulation(c, w, b):
    mod = (jax.nn.silu(c) @ w + b)[:, None, :]
    shift, scale, gate = jnp.split(mod, 3, axis=-1)
    return shift, scale, gate


def pool_mixer(a, hist, pos0, pool_w, pool_scale):
    B, T, _ = a.shape
    full = jnp.concatenate([hist, a], axis=1)
    ff = full.astype(jnp.float32)
    cs = jnp.concatenate([jnp.zeros((B, 1, D_A), jnp.float32), jnp.cumsum(ff, axis=1)], axis=1)
    end = cs[:, POOL_HIST + 1:]
    pos = pos0 + jnp.arange(T)
    means = []
    for gi, w in enumerate(POOL_WINDOWS):
        sl = slice(gi * POOL_GW, (gi + 1) * POOL_GW)
        start = cs[:, POOL_HIST + 1 - w: POOL_HIST + 1 - w + T, sl]
        cnt = jnp.minimum(w, pos + 1).astype(jnp.float32)[None, :, None]
        means.append((end[..., sl] - start) / cnt)
    d = (jnp.concatenate(means, axis=-1) - ff[:, POOL_HIST:]).astype(a.dtype)
    d = d.reshape(B, T, POOL_GROUPS, POOL_GW)
    y = jnp.einsum('btgc,gcd->btgd', d, pool_w).reshape(B, T, D_A) * pool_scale
    new_hist = full[:, -POOL_HIST:]
    return y, new_hist


def sgu_mixer(u, v, norm_g, w_s, b_s):
    B, T, _ = v.shape
    L = min(T, SGU_LEN)
    nc = T // L
    vn = layernorm(v, norm_g)
    vr = vn.reshape(B, nc, L, SGU_GROUPS, SGU_GW)
    wm = jnp.tril(w_s[:, :L, :L])
    bias = b_s[:, :L].T[None, None, :, :, None]
    s = jnp.einsum('gij,bnjgc->bnigc', wm, vr) + bias
    return u * s.reshape(B, T, D_B), vn


def gla_scan(q, k, v, log_a, s0):
    B, T = q.shape[0], q.shape[1]
    pad = (-T) % GLA_BLOCK
    padt = lambda t: jnp.pad(t.astype(jnp.float32), ((0, 0), (0, pad), (0, 0), (0, 0)))

    def to_blocks(t):
        Tp, H, X = t.shape[1], t.shape[2], t.shape[3]
        return t.reshape(B, Tp // GLA_BLOCK, GLA_BLOCK, H, X).transpose(1, 0, 3, 2, 4)

    qb, kb, vb, gb = (to_blocks(padt(t)) for t in (q, k, v, log_a))
    mask = jnp.tril(jnp.ones((GLA_BLOCK, GLA_BLOCK), bool))[:, :, None]

    def step(S, inp):
        qi, ki, vi, gi = inp
        bc = jnp.cumsum(gi, axis=2)
        diff = bc[:, :, :, None, :] - bc[:, :, None, :, :]
        decay = jnp.where(mask, jnp.exp(jnp.where(mask, diff, 0.0)), 0.0)
        att = jnp.einsum('bhid,bhjd,bhijd->bhij', qi, ki, decay)
        o = jnp.einsum('bhij,bhjv->bhiv', att, vi) + jnp.einsum('bhid,bhdv->bhiv', qi * jnp.exp(bc), S)
        blast = bc[:, :, -1]
        kdec = ki * jnp.exp(blast[:, :, None, :] - bc)
        S = jnp.exp(blast)[..., None] * S + jnp.einsum('bhjd,bhjv->bhdv', kdec, vi)
        return S, o

    s_fin, ob = lax.scan(step, s0.astype(jnp.float32), (qb, kb, vb, gb))
    o = ob.transpose(1, 0, 3, 2, 4).reshape(B, T + pad, GLA_HEADS, GLA_DV)[:, :T]
    return o.astype(v.dtype), s_fin.astype(s0.dtype)


def mixer_layer(x, c, pool_hist, gla_state, pos0, lp):
    (ada_w, ada_b, pre_g, post_g, w_in, pool_w, pool_scale, sgu_g, sgu_w, sgu_b,
     wa2, ba, gla_g, w_oa, w_ob, w_oc, w_out) = lp
    B, T, _ = x.shape
    shift, scale, gate = ada_modulation(c, ada_w, ada_b)
    h = rmsnorm(x, pre_g) * (1.0 + scale) + shift
    z = h @ w_in
    a, g_a, u, v_b, g_b, q, k, v_c, g_c, z_lr, g_m = split_cols(z)
    y_a, new_hist = pool_mixer(a, pool_hist, pos0, pool_w, pool_scale)
    y_a = y_a * jax.nn.silu(g_a)
    y_b, v_rows = sgu_mixer(u, v_b, sgu_g, sgu_w, sgu_b)
    y_b = y_b * jax.nn.silu(g_b)
    log_a = jax.nn.log_sigmoid((z_lr @ wa2 + ba).astype(jnp.float32)) / GLA_NORMALIZER
    heads = lambda t, d: t.reshape(B, T, GLA_HEADS, d)
    o, new_state = gla_scan(heads(q, GLA_DK) * (GLA_DK ** -0.5), heads(k, GLA_DK),
                            heads(v_c, GLA_DV), heads(log_a, GLA_DK), gla_state)
    y_c = rmsnorm(o, gla_g).reshape(B, T, D_CV) * jax.nn.silu(g_c)
    gm = jax.nn.sigmoid(g_m)
    m = (gm[..., :D_MODEL] * (y_a @ w_oa)
         + gm[..., D_MODEL:2 * D_MODEL] * (y_b @ w_ob)
         + gm[..., 2 * D_MODEL:] * (y_c @ w_oc))
    out = m @ w_out
    x = x + gate * rmsnorm(out, post_g)
    return x, new_hist, new_state, v_rows


def setup_inputs(seed: int = 0) -> dict:
    key = jax.random.key(seed)
    ks = jax.random.split(key, 26)
    nrm = lambda k, shape, s: jax.random.normal(k, shape, jnp.float32) * s
    D = D_MODEL
    return {
        "x_prompt": nrm(ks[0], (BATCH, SEQ, D), 1.0),
        "x_sample": nrm(ks[1], (DEC_BATCH, DEC_SEQ, D), 1.0),
        "state_pool": nrm(ks[2], (DEPTH, DEC_BATCH, POOL_HIST, D_A), 1.0),
        "state_gla": nrm(ks[3], (DEPTH, DEC_BATCH, GLA_HEADS, GLA_DK, GLA_DV), 1.0),
        "c_prompt": nrm(ks[4], (BATCH, D), 1.0),
        "c_sample": nrm(ks[5], (DEC_BATCH, D), 1.0),
        "ada_w": nrm(ks[6], (DEPTH, D, 3 * D), 0.3 * D ** -0.5),
        "ada_b": nrm(ks[7], (DEPTH, 3 * D), 0.1),
        "pre_norm_g": 1.0 + nrm(ks[8], (DEPTH, D), 0.05),
        "post_norm_g": 1.0 + nrm(ks[9], (DEPTH, D), 0.05),
        "w_in": nrm(ks[10], (DEPTH, D, D_IN), D ** -0.5),
        "pool_w": nrm(ks[11], (DEPTH, POOL_GROUPS, POOL_GW, POOL_GW), POOL_GW ** -0.5),
        "pool_scale": 1.0 + nrm(ks[12], (DEPTH, D_A), 0.1),
        "sgu_norm_g": 1.0 + nrm(ks[13], (DEPTH, D_B), 0.05),
        "sgu_w": nrm(ks[14], (DEPTH, SGU_GROUPS, SGU_LEN, SGU_LEN), SGU_LEN ** -0.5),
        "sgu_b": 1.0 + nrm(ks[15], (DEPTH, SGU_GROUPS, SGU_LEN), 0.1),
        "gla_wa2": nrm(ks[16], (DEPTH, GLA_RANK, D_CK), GLA_RANK ** -0.5),
        "gla_ba": nrm(ks[17], (DEPTH, D_CK), 0.1),
        "gla_norm_g": 1.0 + nrm(ks[18], (DEPTH, GLA_DV), 0.05),
        "w_oa": nrm(ks[19], (DEPTH, D_A, D), D_A ** -0.5),
        "w_ob": nrm(ks[20], (DEPTH, D_B, D), D_B ** -0.5),
        "w_oc": nrm(ks[21], (DEPTH, D_CV, D), D_CV ** -0.5),
        "w_out": nrm(ks[22], (DEPTH, D, D), D ** -0.5),
    }


def reference(x_prompt, x_sample, state_pool, state_gla, c_prompt, c_sample,
              ada_w, ada_b, pre_norm_g, post_norm_g, w_in, pool_w, pool_scale,
              sgu_norm_g, sgu_w, sgu_b, gla_wa2, gla_ba, gla_norm_g,
              w_oa, w_ob, w_oc, w_out):
    params = (ada_w, ada_b, pre_norm_g, post_norm_g, w_in, pool_w, pool_scale,
              sgu_norm_g, sgu_w, sgu_b, gla_wa2, gla_ba, gla_norm_g,
              w_oa, w_ob, w_oc, w_out)
    xp, xs = x_prompt, x_sample
    bp = xp.shape[0]
    pool_p, gla_p, pool_s, gla_s, sgu_s = [], [], [], [], []
    for l in range(DEPTH):
        lp = tuple(p[l] for p in params)
        hist0 = jnp.zeros((bp, POOL_HIST, D_A), xp.dtype)
        s0 = jnp.zeros((bp, GLA_HEADS, GLA_DK, GLA_DV), state_gla.dtype)
        xp, hp, sp, _ = mixer_layer(xp, c_prompt, hist0, s0, 0, lp)
        xs, hs, ss, vs = mixer_layer(xs, c_sample, state_pool[l], state_gla[l], PAST_LEN, lp)
        pool_p.append(hp)
        gla_p.append(sp)
        pool_s.append(hs)
        gla_s.append(ss)
        sgu_s.append(vs)
    new_pool_prompt = jnp.stack(pool_p)
    new_gla_prompt = jnp.stack(gla_p)
    new_pool_sample = jnp.stack(pool_s)
    new_gla_sample = jnp.stack(gla_s)
    new_sgu_v_sample = jnp.stack(sgu_s)
    return (xp, xs, new_pool_prompt, new_gla_prompt, new_pool_sample, new_gla_sample, new_sgu_v_sample)
```

```python
import numpy as np
from contextlib import ExitStack
import concourse.bass as bass
import concourse.mybir as mybir
from concourse.bass_utils import run_bass_kernel_spmd

F32 = mybir.dt.float32
BF16 = mybir.dt.bfloat16
AF = mybir.ActivationFunctionType
ALU = mybir.AluOpType

D = 1024
TP = 2048
TS = 64
NT = TP + TS
DEPTH = 2
D_IN = 8720
EPS = 1e-6
C_A, C_GA, C_U, C_VB, C_GB, C_Q, C_K, C_VC, C_GC, C_LR, C_GM = (
    0, 512, 1024, 1536, 2048, 2560, 3072, 3584, 4608, 5632, 5648)
POOL_W = (2, 4, 8, 16)

DEBUG = False


class Res:
    __slots__ = ("w", "r", "name")

    def __init__(self, name=""):
        self.w = None
        self.r = []
        self.name = name


class Sched:
    ENGS = ("pe", "act", "dve", "pool", "sp")
    NDMA = 8

    def __init__(self):
        self.ops = {e: [] for e in self.ENGS}
        self.count = {e: 0 for e in self.ENGS}
        self.known = {e: {} for e in self.ENGS}
        self.dma_n = {e: 0 for e in self.ENGS}
        self.dma_tokens = []

    def _waits(self, eng, reads, writes, extra=()):
        toks = list(extra)
        for r in reads:
            if r.w is not None:
                toks.append(r.w)
        for w in writes:
            if w.w is not None:
                toks.append(w.w)
            toks.extend(w.r)
        kn = self.known[eng]
        waits = {}
        for (sem, val) in toks:
            if sem == ("e", eng) and eng == "pe":
                continue
            if kn.get(sem, 0) >= val:
                continue
            if waits.get(sem, 0) < val:
                waits[sem] = val
        for sem, val in waits.items():
            kn[sem] = val
        return list(waits.items())

    def _commit(self, tok, reads, writes):
        for r in reads:
            r.r.append(tok)
        for w in writes:
            w.w = tok
            w.r = []

    def op(self, eng, fn, reads=(), writes=()):
        waits = self._waits(eng, reads, writes)
        self.count[eng] += 1
        tok = (("e", eng), self.count[eng])
        self.ops[eng].append((waits, fn, tok, 1))
        self._commit(tok, reads, writes)
        return tok

    def dma(self, eng, out, in_, reads=(), writes=(), **kw):
        n = self.dma_n[eng]
        self.dma_n[eng] += 1
        sem = ("d", eng, n % self.NDMA)
        val = 16 * (n // self.NDMA + 1)
        extra = [(sem, val - 16)] if val > 16 else []
        waits = self._waits(eng, reads, writes, extra)
        tok = (sem, val)

        def fn(e, out=out, in_=in_, kw=kw):
            return e.dma_start(out=out, in_=in_, **kw)
        self.ops[eng].append((waits, fn, tok, 16))
        self._commit(tok, reads, writes)
        self.dma_tokens.append(tok)
        return tok

    def barrier(self):
        toks = [(("e", e), self.count[e]) for e in self.ENGS if self.count[e] > 0]
        last = {}
        for (sem, val) in self.dma_tokens:
            if last.get(sem, 0) < val:
                last[sem] = val
        toks += list(last.items())
        for e in self.ENGS:
            kn = self.known[e]
            waits = []
            for (sem, val) in toks:
                if sem == ("e", e) and e == "pe":
                    continue
                if kn.get(sem, 0) < val:
                    kn[sem] = val
                    waits.append((sem, val))
            if waits:
                self.ops[e].append((waits, None, None, 0))

    def final_wait(self, eng="sp"):
        last = {}
        for (sem, val) in self.dma_tokens:
            if last.get(sem, 0) < val:
                last[sem] = val
        self.ops[eng].append((list(last.items()), None, None, 0))

    def sem_keys(self):
        keys = [("e", e) for e in self.ENGS]
        for e in self.ENGS:
            if self.dma_n[e] > 0:
                keys += [("d", e, i) for i in range(min(self.NDMA, self.dma_n[e]))]
        return keys

    def emit(self, eng, engine, sems):
        for (waits, fn, tok, inc) in self.ops[eng]:
            for (sem, val) in waits:
                engine.wait_ge(sems[sem], val)
            if fn is not None:
                ins = fn(engine)
                ins.then_inc(sems[tok[0]], inc)


class Mem:
    live = []

    @classmethod
    def claim(cls, start, end, res_list):
        seed = {}
        keep = []
        for (s0, e0, rl) in cls.live:
            if s0 < end and start < e0:
                for r in rl:
                    toks = list(r.r)
                    if r.w is not None:
                        toks.append(r.w)
                    for (sem, val) in toks:
                        if seed.get(sem, 0) < val:
                            seed[sem] = val
                if not (start <= s0 and e0 <= end):
                    keep.append((s0, e0, rl))
            else:
                keep.append((s0, e0, rl))
        keep.append((start, end, res_list))
        cls.live = keep
        for r in res_list:
            r.w = None
            r.r = list(seed.items())


class Bump:
    def __init__(self, arena, start, end):
        self.arena = arena
        self.p = start
        self.end = end

    def alloc(self, shape, dt, name=""):
        esz = 4 if dt == F32 else 2
        n = 1
        for s in shape[1:]:
            n *= s
        nbytes = (n * esz + 31) // 32 * 32
        assert self.p + nbytes <= self.end, f"arena overflow for {name}: {self.p}+{nbytes}>{self.end}"
        o4 = self.p // 4
        v = self.arena[:, o4:o4 + nbytes // 4]
        if dt != F32:
            v = v.bitcast(dt)
        v = v[:, 0:n]
        if len(shape) == 3:
            v = v.rearrange("p (a b) -> p a b", b=shape[2])
        elif len(shape) == 4:
            v = v.rearrange("p (a b c) -> p a b c", b=shape[2], c=shape[3])
        if shape[0] < 128:
            v = v[0:shape[0]]
        res = Res(name)
        Mem.claim(self.p, self.p + nbytes, [res])
        self.p += nbytes
        return v, res


def build_program():
    nc = bass.Bass("TRN2", target_bir_lowering=False)
    S = Sched()
    Mem.live = []

    def din(name, shape):
        return nc.dram_tensor(name, list(shape), F32, kind="ExternalInput").ap()

    def dout(name, shape):
        return nc.dram_tensor(name, list(shape), F32, kind="ExternalOutput").ap()

    xin = din("xin", [NT, D])
    c2 = din("c2", [2, D])
    spool = din("spool", [DEPTH, 15, 512])
    sgla = din("sgla", [DEPTH, 4, 128, 256])
    ada_w = din("ada_w", [DEPTH, D, 3 * D])
    ada_b = din("ada_b", [DEPTH, 3 * D])
    pre_g = din("pre_g", [DEPTH, D])
    post_g = din("post_g", [DEPTH, D])
    w_in = din("w_in", [DEPTH, D, D_IN])
    pool_w = din("pool_w", [DEPTH, 4, 128, 128])
    pool_scale = din("pool_scale", [DEPTH, 512])
    sgu_g = din("sgu_g", [DEPTH, 512])
    sgu_w = din("sgu_w", [DEPTH, 4, 128, 128])
    sgu_b = din("sgu_b", [DEPTH, 512])
    gla_wa2 = din("gla_wa2", [DEPTH, 16, 512])
    gla_ba = din("gla_ba", [DEPTH, 512])
    gla_g = din("gla_g", [DEPTH, 256])
    w_oa = din("w_oa", [DEPTH, 512, D])
    w_ob = din("w_ob", [DEPTH, 512, D])
    w_oc = din("w_oc", [DEPTH, 1024, D])
    w_out = din("w_out", [DEPTH, D, D])
    k_ident = din("k_ident", [128, 128])
    k_masku = din("k_masku", [128, 128])
    k_ucum = din("k_ucum", [128, 128])
    k_invc = din("k_invc", [128, 64])

    y = dout("y", [NT, D])
    o_pool = dout("o_pool", [DEPTH, 2, 15, 512])
    o_gla = dout("o_gla", [DEPTH, 2, 4, 128, 256])
    o_sgu = dout("o_sgu", [DEPTH, TS, 512])
    x1s = nc.dram_tensor("x1s", [NT, D], F32, kind="Internal").ap()
    dbg = {}
    if DEBUG:
        dbg["hT"] = dout("d_hT", [128, 8 * NT])
        dbg["ycat"] = dout("d_ycat", [128, 16 * NT])
        dbg["mT"] = dout("d_mT", [128, 8 * NT])

    ARENA_B = 210944
    with ExitStack() as es:
        arena = es.enter_context(nc.sbuf_tensor("arena", [128, ARENA_B // 4], F32))
        banks = [es.enter_context(nc.psum_tensor(f"bank{i}", [128, 512], F32)) for i in range(8)]
        bank_res = [Res(f"bank{i}") for i in range(8)]
        bank_i = [0]

        nrot = [8]
        acc_i = [0]

        held = set()

        def psum(acc=False, hold=False):
            if acc:
                i = 6 + acc_i[0] % 2
                acc_i[0] += 1
            else:
                for _ in range(nrot[0] + 1):
                    i = bank_i[0] % nrot[0]
                    bank_i[0] += 1
                    if i not in held:
                        break
                else:
                    raise RuntimeError("no free PSUM bank")
            if hold:
                held.add(i)
            return banks[i], bank_res[i]

        def punhold(res):
            held.discard(bank_res.index(res))

        def pbf(bank):
            return bank[:, :].bitcast(BF16)

        HT_B, YC_B, MT_B = 8 * NT * 2, 16 * NT * 2, 8 * NT * 2
        R_HT = (0, HT_B)
        R_YC = (HT_B, HT_B + YC_B)
        R_MT = (HT_B + YC_B, HT_B + YC_B + MT_B)
        R_CONST = (R_MT[1], R_MT[1] + 14336)
        R_WA = (R_CONST[1], R_CONST[1] + 51200)
        R_SCR = (R_WA[1], ARENA_B)
        assert R_SCR[1] - R_SCR[0] == 10240

        def resident(rg, k):
            o4 = rg[0] // 4
            return arena[:, o4:o4 + (rg[1] - rg[0]) // 4].bitcast(BF16).rearrange("p (k t) -> p k t", k=k)
        hT = resident(R_HT, 8)
        ycat = resident(R_YC, 16)
        mT = resident(R_MT, 8)
        NTILE = 17
        r_hT_t = [Res(f"hT{t}") for t in range(NTILE)]
        r_yc_t = [[Res(f"yc{c}_{t}") for t in range(NTILE)] for c in range(16)]
        r_mT_t = [Res(f"mT{t}") for t in range(NTILE)]

        bc = Bump(arena, *R_CONST)
        ident_f, r_identf = bc.alloc([128, 128], F32, "ident_f")
        ident_b, r_identb = bc.alloc([128, 128], BF16, "ident_b")
        masku, r_masku = bc.alloc([128, 128], F32, "masku")
        ucum, r_ucum = bc.alloc([128, 128], F32, "ucum")
        invc, r_invc = bc.alloc([128, 64], F32, "invc")
        ones_f, r_onesf = bc.alloc([128, 128], F32, "ones_f")
        ones_b, r_ones = bc.alloc([1, 128], BF16, "ones_b")
        scT, r_scT = bc.alloc([128, 8, 2], BF16, "scT")
        pool_wb, r_poolw = bc.alloc([128, 4, 128], BF16, "pool_wb")
        WT, r_WT = bc.alloc([128, 4, 128], BF16, "WT")
        pscale, r_pscale = bc.alloc([128, 4], F32, "pscale")
        sgug_bc, r_sgug = bc.alloc([128, 512], F32, "sgug_bc")
        glag_bc, r_glag = bc.alloc([128, 256], F32, "glag_bc")
        sgub_b, r_sgub = bc.alloc([1, 512], BF16, "sgub_b")
        wa_hi, r_wahi = bc.alloc([17, 512], BF16, "wa_hi")
        wa_lo, r_walo = bc.alloc([17, 512], BF16, "wa_lo")
        ucum_b, r_ucumb = bc.alloc([128, 128], BF16, "ucum_b")
        modcs = [bc.alloc([128, 24, 2], F32, f"modc{i}") for i in range(2)]
        epsc, r_epsc = bc.alloc([128, 1], F32, "epsc")
        onec, r_onec = bc.alloc([128, 1], F32, "onec")

        class Chunk:
            __slots__ = ("ap", "res", "closed", "iv")

            def __init__(self):
                self.ap = None
                self.res = None
                self.closed = False
                self.iv = None

        wa_p = [R_WA[0]]
        wa_chunks = []
        wa_pending = []

        def enqueue(nbytes, emit_fn):
            ch = Chunk()
            wa_pending.append((ch, nbytes, emit_fn))
            return ch

        def pump():
            while wa_pending:
                ch, nbytes, emit_fn = wa_pending[0]
                opens = [c.iv for c in wa_chunks if not c.closed]

                def fits(p0):
                    if p0 + nbytes > R_WA[1]:
                        return False
                    return all(not (iv[0] < p0 + nbytes and p0 < iv[1]) for iv in opens)
                cands = [wa_p[0], R_WA[0]] + sorted(iv[1] for iv in opens)
                p = next((c for c in cands if fits(c)), None)
                if p is None:
                    break
                wa_pending.pop(0)
                res = Res("wa")
                Mem.claim(p, p + nbytes, [res])
                ch.iv = (p, p + nbytes)
                ch.res = res
                flat = arena[:, p // 4:(p + nbytes) // 4].bitcast(BF16)
                ch.ap = emit_fn(flat, res)
                wa_chunks[:] = [c for c in wa_chunks if not (c.closed and p <= c.iv[0] and c.iv[1] <= p + nbytes)]
                wa_chunks.append(ch)
                wa_p[0] = p + nbytes

        def need(ch):
            if ch.ap is None:
                pump()
            assert ch.ap is not None, "weight chunk not resident: weight arena too small for this order"
            return ch.ap, ch.res

        def close(ch):
            ch.closed = True
            pump()

        def wload(dram_cols, ncols, k=8):
            def emit(flat, res, dram_cols=dram_cols, ncols=ncols, k=k):
                v = flat[:, 0:k * ncols].rearrange("p (k n) -> p k n", n=ncols)
                for c0_ in range(0, ncols, 512):
                    c1_ = min(ncols, c0_ + 512)
                    S.dma("pool", v[:, :, c0_:c1_], dram_cols[:, c0_:c1_].rearrange("(k p) n -> p k n", p=128), [], [res])
                return v
            return enqueue(k * ncols * 2, emit)

        def wload_multi(nbytes, shape_fn, parts):
            def emit(flat, res, shape_fn=shape_fn, parts=parts):
                v = shape_fn(flat)
                for dst_fn, src in parts:
                    S.dma("pool", dst_fn(v), src.rearrange("(k p) n -> p k n", p=128), [], [res])
                return v
            return enqueue(nbytes, emit)

        WQ = [dict() for _ in range(DEPTH)]

        def enq_ada(l_):
            WQ[l_]["ada"] = [wload(ada_w[l_][:, cg * 512:(cg + 1) * 512], 512) for cg in range(6)]

        enq_ada(0)
        for l_ in range(DEPTH):
            wl_ = w_in[l_]
            q = WQ[l_]
            q["a"] = wload(wl_[:, C_A:C_A + 512], 512)
            q["ga"] = wload(wl_[:, C_GA:C_GA + 512], 512)
            q["vb"] = wload(wl_[:, C_VB:C_VB + 512], 512)
            q["u"] = wload(wl_[:, C_U:C_U + 512], 512)
            q["gb"] = wload(wl_[:, C_GB:C_GB + 512], 512)
            q["zl"] = wload(wl_[:, C_LR:C_LR + 16], 16)
            q["q"] = wload(wl_[:, C_Q:C_Q + 512], 512)
            q["k"] = wload(wl_[:, C_K:C_K + 512], 512)
            q["vc"] = wload(wl_[:, C_VC:C_VC + 1024], 1024)
            q["gc"] = [wload(wl_[:, C_GC:C_GC + 512], 512)]
            if l_ + 1 < DEPTH:
                enq_ada(l_ + 1)
            q["gc"].append(wload(wl_[:, C_GC + 512:C_GC + 1024], 512))
            q["wo"] = []
            q["wgm"] = []
            for dp in range(4):
                cs = slice(dp * 256, (dp + 1) * 256)
                q["wo"].append(wload_multi(
                    16 * 256 * 2, lambda f: f.rearrange("p (k n) -> p k n", n=256),
                    [(lambda v: v[:, 0:4, :], w_oa[l_][:, cs]), (lambda v: v[:, 4:8, :], w_ob[l_][:, cs]),
                     (lambda v: v[:, 8:16, :], w_oc[l_][:, cs])]))
                q["wgm"].append(wload_multi(
                    8 * 3 * 256 * 2, lambda f: f.rearrange("p (k i n) -> p k i n", i=3, n=256),
                    [((lambda v, i=i: v[:, :, i, :]),
                      wl_[:, C_GM + i * 1024 + dp * 256:C_GM + i * 1024 + (dp + 1) * 256]) for i in range(3)]))
            q["wout"] = wload(w_out[l_], 1024)

        def mm(out, pairs, reads, writes):
            def fn(pe, out=out, pairs=pairs):
                n = len(pairs)
                for i, (l, r) in enumerate(pairs):
                    ins = pe.matmul(out, lhsT=l, rhs=r, start=(i == 0), stop=(i == n - 1))
                return ins
            S.op("pe", fn, reads, writes)

        def mm_groups(groups, reads, writes):
            def fn(pe, groups=groups):
                for out, pairs in groups:
                    n = len(pairs)
                    for i, (l, r) in enumerate(pairs):
                        ins = pe.matmul(out, lhsT=l, rhs=r, start=(i == 0), stop=(i == n - 1))
                return ins
            S.op("pe", fn, reads, writes)

        def transposes(items, ident, reads, writes):
            def fn(pe, items=items):
                for out, in_, idn in items:
                    ins = pe.transpose(out=out, in_=in_, identity=idn)
                return ins
            S.op("pe", fn, reads, writes)

        def act(out, in_, func, reads, writes, **kw):
            S.op("act", lambda a, out=out, in_=in_, func=func, kw=kw: a.activation(out=out, in_=in_, func=func, **kw),
                 reads, writes)

        def dve_tt(out, in0, in1, op, reads, writes):
            S.op("dve", lambda v, o=out, a=in0, b=in1, op=op: v.tensor_tensor(out=o, in0=a, in1=b, op=op), reads, writes)

        def dve_ts(out, in0, s1, s2, op0, op1, reads, writes):
            S.op("dve", lambda v, o=out, a=in0, s1=s1, s2=s2, op0=op0, op1=op1:
                 v.tensor_scalar(out=o, in0=a, scalar1=s1, scalar2=s2, op0=op0, op1=op1), reads, writes)

        def dve_stt(out, in0, scalar, in1, op0, op1, reads, writes):
            S.op("dve", lambda v, o=out, a=in0, s=scalar, b=in1, op0=op0, op1=op1:
                 v.scalar_tensor_tensor(out=o, in0=a, scalar=s, in1=b, op0=op0, op1=op1), reads, writes)

        def dve_copy(out, in_, reads, writes):
            S.op("dve", lambda v, o=out, i=in_: v.tensor_copy(out=o, in_=i), reads, writes)

        def rstd_from_ms(ms, out, r_ms, r_out, scale=1.0):
            act(out, ms, AF.Ln, [r_ms, r_epsc], [r_out], bias=epsc[0:ms.shape[0], :], scale=scale)
            act(out, out, AF.Exp, [r_out], [r_out], scale=-0.5)

        class Pipe:
            def __init__(self, stages, n, filler=None, order=None):
                self.stages = stages
                self.n = n
                self.ns = len(stages)
                self.order = order if order is not None else list(reversed(range(self.ns)))
                self.filler = filler
                self.k = 0

            def step(self):
                if self.k >= self.n + self.ns - 1:
                    return False
                for si in self.order:
                    i = self.k - si
                    if 0 <= i < self.n:
                        self.stages[si](i)
                if self.filler is not None and self.k >= self.n - 1:
                    self.filler()
                self.k += 1
                return True

            def run(self):
                while self.step():
                    pass

        def pipeline(stages, n, filler=None, order=None):
            Pipe(stages, n, filler, order).run()

        S.dma("sp", ident_f, k_ident, [], [r_identf])
        S.dma("pool", ident_b, k_ident, [], [r_identb])
        S.dma("sp", masku, k_masku, [], [r_masku])
        S.dma("sp", ucum, k_ucum, [], [r_ucum])
        S.dma("pool", ucum_b, k_ucum, [], [r_ucumb])
        S.dma("sp", invc, k_invc, [], [r_invc])
        S.op("dve", lambda v: v.memset(ones_f, 1.0), [], [r_onesf])
        S.op("dve", lambda v: v.memset(ones_b, 1.0), [], [r_ones])
        S.op("dve", lambda v: v.memset(epsc, EPS), [], [r_epsc])
        S.op("dve", lambda v: v.memset(onec, 1.0), [], [r_onec])

        bs = Bump(arena, *R_SCR)
        crow, r_crow = bs.alloc([2, 1024], F32, "crow")
        S.dma("sp", crow, c2, [], [r_crow])
        act(crow, crow, AF.Silu, [r_crow], [r_crow])
        pb, pr = psum()
        transposes([(pb[:, 2 * k:2 * k + 2], crow[0:2, k * 128:(k + 1) * 128], ident_f[0:2, 0:2]) for k in range(8)],
                   None, [r_crow, r_identf], [pr])
        dve_copy(scT, pb[:, 0:16].rearrange("p (k s) -> p k s", s=2), [pr], [r_scT])

        pump()
        TILES = [(t * 128, 128) for t in range(16)] + [(TP, TS)]
        r_x1s = [Res(f"x1s{t}") for t in range(17)]
        BLOCKS = [(0, 512, [0, 1, 2, 3]), (512, 512, [4, 5, 6, 7]), (1024, 512, [8, 9, 10, 11]),
                  (1536, 512, [12, 13, 14, 15]), (TP, TS, [16])]

        def small_params(l, region):
            bsm = Bump(arena, *region)
            stg, r_stg = bsm.alloc([128, 4, 128], F32, "stg")
            S.dma("pool", pool_wb, pool_w[l].rearrange("g c d -> c g d"), [], [r_poolw])
            S.dma("sp", pscale, pool_scale[l].rearrange("(g p) -> p g", p=128), [], [r_pscale],
                  allow_slow_non_contiguous=True)
            S.dma("sp", sgug_bc, sgu_g[l].partition_broadcast(128), [], [r_sgug])
            S.dma("sp", glag_bc, gla_g[l].partition_broadcast(128), [], [r_glag])
            S.dma("pool", sgub_b, sgu_b[l].rearrange("(o n) -> o n", o=1), [], [r_sgub])
            wa2f, r_wa2f = bsm.alloc([17, 512], F32, "wa2f")
            S.dma("sp", wa2f[0:16, :], gla_wa2[l], [], [r_wa2f])
            S.dma("sp", wa2f[16:17, :], gla_ba[l].rearrange("(o n) -> o n", o=1), [r_wa2f], [r_wa2f])
            dve_copy(wa_hi, wa2f, [r_wa2f], [r_wahi])
            dve_tt(wa_lo, wa2f, wa_hi, ALU.subtract, [r_wa2f, r_wahi], [r_walo])
            S.dma("sp", stg, sgu_w[l].rearrange("g i j -> i g j"), [], [r_stg])
            pb, pr = psum()
            transposes([(pb[:, g * 128:(g + 1) * 128], stg[:, g, :], ident_f) for g in range(4)], None,
                       [r_stg, r_identf], [pr])
            dve_tt(WT, pb[:, :].rearrange("p (g i) -> p g i", i=128),
                   masku.unsqueeze(1).broadcast_to([128, 4, 128]), ALU.mult, [pr, r_masku], [r_WT])

        def ada(l, region, run=True):
            WL = WQ[l]
            modc, r_modc = modcs[l % 2]
            ba = Bump(arena, *region)
            mod, r_mod = ba.alloc([2, 3072], F32, "mod")
            preg2, r_preg2 = ba.alloc([2, 1024], F32, "preg2")
            postg2, r_postg2 = ba.alloc([2, 1024], F32, "postg2")
            S.dma("sp", mod, ada_b[l].partition_broadcast(2), [], [r_mod])
            S.dma("sp", preg2, pre_g[l].partition_broadcast(2), [], [r_preg2])
            S.dma("sp", postg2, post_g[l].partition_broadcast(2), [], [r_postg2])

            def to_cols(j0, j1):
                pb, pr = psum()
                transposes([(pb[:, 2 * j:2 * j + 2], mod[0:2, j * 128:(j + 1) * 128], ident_f[0:2, 0:2])
                            for j in range(j0, j1)], None, [r_mod, r_identf], [pr])
                dve_copy(modc[:, j0:j1, :], pb[:, 2 * j0:2 * j1].rearrange("p (j s) -> p j s", s=2), [pr], [r_modc])

            def group(cg):
                wv, wr = need(WL["ada"][cg])
                pb, pr = psum()
                mm(pb[0:2, :], [(scT[:, k, :], wv[:, k, :]) for k in range(8)], [r_scT, wr], [pr])
                dve_tt(mod[:, cg * 512:(cg + 1) * 512], pb[0:2, :], mod[:, cg * 512:(cg + 1) * 512], ALU.add,
                       [pr, r_mod], [r_mod])
                close(WL["ada"][cg])
                if cg == 3:
                    dve_stt(mod[:, 1024:2048], mod[:, 1024:2048], 1.0, preg2, ALU.add, ALU.mult,
                            [r_mod, r_preg2], [r_mod])
                    to_cols(0, 16)
                if cg == 5:
                    dve_tt(mod[:, 2048:3072], mod[:, 2048:3072], postg2, ALU.mult, [r_mod, r_postg2], [r_mod])
                    to_cols(16, 24)
            steps = [(lambda cg=cg: group(cg)) for cg in range(6)]
            if run:
                for st in steps:
                    st()
            return steps

        def p0_stages(l, bump, get_x, stat_bump, pool_xn=False, raw=False):
            modc, r_modc = modcs[l % 2]
            xnb = [bump.alloc([128, 1024], BF16, f"xnb{i}") for i in range(2)]
            tmod, r_tmod = bump.alloc([128, 8, 128], F32, "tmod")
            junk, r_junk = bump.alloc([128, 1024], BF16, "junk")
            st0 = [stat_bump.alloc([128, 4], F32, f"st0{i}") for i in range(4)]

            def sq(t):
                c0, R = TILES[t]
                xa, xr = get_x(t)
                sa, sr = st0[t % 4]
                if raw:
                    S.op("dve", lambda v, o=junk[0:R, :], a=xa[0:R, :], acc=sa[0:R, 0:1]:
                         v.scalar_tensor_tensor(out=o, in0=a, scalar=1.0, in1=a, op0=ALU.mult, op1=ALU.mult,
                                                accum_out=acc), [xr], [r_junk, sr])
                else:
                    act(junk[0:R, :], xa[0:R, :], AF.Square, [xr], [r_junk, sr], accum_out=sa[0:R, 0:1])

            def sc(t):
                c0, R = TILES[t]
                sa, sr = st0[t % 4]
                pass

            def nrm(t):
                c0, R = TILES[t]
                xa, xr = get_x(t)
                sa, sr = st0[t % 4]
                xn, xnr = xnb[t % 2]
                rstd_from_ms(sa[0:R, 0:1], sa[0:R, 2:3], sr, sr, scale=1.0 / D)
                if pool_xn:
                    S.op("pool", lambda g_, o=xn[0:R, :], a=xa[0:R, :], sc_=sa[0:R, 2:3]:
                         g_.tensor_scalar(out=o, in0=a, scalar1=sc_, scalar2=None, op0=ALU.mult), [xr, sr], [xnr])
                else:
                    act(xn[0:R, :], xa[0:R, :], AF.Copy, [xr, sr], [xnr], scale=sa[0:R, 2:3])

            def tp(t):
                c0, R = TILES[t]
                xn, xnr = xnb[t % 2]
                pb, pr = psum(hold=True)
                pv = pbf(pb).rearrange("p (k t) -> p k t", t=128)
                transposes([(pv[:, k, 0:R], xn[0:R, k * 128:(k + 1) * 128], ident_b[0:R, 0:R]) for k in range(8)],
                           None, [xnr, r_identb], [pr])
                ctx0[t] = (pv, pr)

            def md(t):
                c0, R = TILES[t]
                sq_ = 0 if t < 16 else 1
                pv, pr = ctx0.pop(t)
                if raw:
                    dve_copy(hT[:, :, c0:c0 + R], pv[:, :, 0:R], [pr], [r_hT_t[t]])
                    punhold(pr)
                    return
                dve_tt(tmod[:, :, 0:R], pv[:, :, 0:R], modc[:, 8:16, sq_:sq_ + 1].broadcast_to([128, 8, R]), ALU.mult,
                       [pr, r_modc], [r_tmod])
                dve_tt(hT[:, :, c0:c0 + R], tmod[:, :, 0:R], modc[:, 0:8, sq_:sq_ + 1].broadcast_to([128, 8, R]),
                       ALU.add, [r_tmod, r_modc], [r_hT_t[t]])
                punhold(pr)

            ctx0 = {}
            return [sq, nrm, tp, md]

        for l in range(DEPTH):
            x_src = xin if l == 0 else x1s
            x_dst = x1s if l == 0 else y
            wl = w_in[l]
            WL = WQ[l]
            modcL, r_modcL = modcs[l % 2]

            if l == 0:
                small_params(0, R_SCR)
                Mem.claim(R_HT[0], R_HT[1], r_hT_t)
                ada_steps = ada(0, (R_YC[0], R_YC[0] + 24576), run=False)
                b0 = Bump(arena, R_YC[0] + 24576, R_MT[1])
                xt = [b0.alloc([128, 1024], F32, f"xt{i}") for i in range(8)]

                def p0_ld(t):
                    c0, R = TILES[t]
                    xa, xr = xt[t % 8]
                    S.dma("sp", xa[0:R, :], x_src[c0:c0 + R, :], [r_x1s[t]], [xr])

                def p0_nop(t):
                    pass

                P0 = Pipe([p0_ld, p0_nop] + p0_stages(0, b0, lambda t: xt[t % 8], b0, raw=True), NTILE)
                while P0.step():
                    if P0.k in (4, 8, 12, 16, 19, 22):
                        ada_steps.pop(0)()
                while ada_steps:
                    ada_steps.pop(0)()
                modc0, r_modc0 = modcs[0]
                for bi, (c0, W, tl) in enumerate(BLOCKS):
                    sq0 = 0 if bi < 4 else 1
                    rr = [r_hT_t[t] for t in tl]
                    for k in range(8):
                        if k % 2 == 0:
                            dve_ts(hT[:, k, c0:c0 + W], hT[:, k, c0:c0 + W], modc0[:, 8 + k, sq0:sq0 + 1],
                                   modc0[:, k, sq0:sq0 + 1], ALU.mult, ALU.add, rr + [r_modc0], rr)
                        else:
                            act(hT[:, k, c0:c0 + W], hT[:, k, c0:c0 + W], AF.Identity, rr + [r_modc0], rr,
                                scale=modc0[:, 8 + k, sq0:sq0 + 1], bias=modc0[:, k, sq0:sq0 + 1])

            Mem.claim(R_YC[0], R_YC[1], [r for rl in r_yc_t for r in rl])
            bA = Bump(arena, *R_MT)
            abuf = [bA.alloc([128, 15 + 512], F32, f"abuf{g}") for g in range(4)]
            tA = [bA.alloc([128, 15 + 512], F32, f"tA{i}") for i in range(3)]
            dT = [bA.alloc([128, 512], BF16, f"dT{i}") for i in range(4)]
            sga = [bA.alloc([128, 512], F32, f"sga{i}") for i in range(4)]
            fx, r_fx = bA.alloc([128, 16], F32, "fx")
            hrow, r_hrow = bA.alloc([16, 512], F32, "hrow")
            orow, r_orow = bA.alloc([16, 512], F32, "orow")
            wa_v, wa_r = need(WL["a"])
            wg_v, wg_r = need(WL["ga"])
            ctxA = {}

            def pA_s0(it):
                bi, g = divmod(it, 4)
                c0, W, tl = BLOCKS[bi]
                rh = [r_hT_t[t] for t in tl]
                if bi == 0 and g == 0:
                    for gg in range(4):
                        S.op("dve", lambda v, o=abuf[gg][0][:, 0:15]: v.memset(o, 0.0), [], [abuf[gg][1]])
                if bi == 4 and g == 0:
                    S.dma("sp", hrow[0:15, :], spool[l], [], [r_hrow])
                    pb, pr = psum()
                    transposes([(pb[:, gg * 16:gg * 16 + 15], hrow[0:15, gg * 128:(gg + 1) * 128], ident_f[0:15, 0:15])
                                for gg in range(4)], None, [r_hrow, r_identf], [pr])
                    for gg in range(4):
                        dve_copy(abuf[gg][0][:, 0:15], pb[:, gg * 16:gg * 16 + 15], [pr], [abuf[gg][1]])
                ab, ar = abuf[g]
                w = POOL_W[g]
                pa, par = psum()
                mm(pa[:, 0:W], [(wa_v[:, k, g * 128:(g + 1) * 128], hT[:, k, c0:c0 + W]) for k in range(8)],
                   [wa_r] + rh, [par])
                act(ab[:, 15:15 + W], pa[:, 0:W], AF.Copy, [par], [ar])
                pg, pgr = psum()
                mm(pg[:, 0:W], [(wg_v[:, k, g * 128:(g + 1) * 128], hT[:, k, c0:c0 + W]) for k in range(8)],
                   [wg_r] + rh, [pgr])
                sg_, sgr = sga[it % 4]
                act(sg_[:, 0:W], pg[:, 0:W], AF.Silu, [pgr], [sgr])
                src, srcr = ab, ar
                sh = 1
                lo = 1
                ti = 0
                while sh < w:
                    dst, dstr = tA[(it + ti) % 3]
                    dve_tt(dst[:, lo:15 + W], src[:, lo:15 + W], src[:, lo - sh:15 + W - sh], ALU.add,
                           [srcr], [dstr])
                    src, srcr = dst, dstr
                    sh *= 2
                    lo += sh
                    ti += 1
                d_, dr = dT[it % 4]
                dve_stt(d_[:, 0:W], src[:, 15:15 + W], 1.0 / w, ab[:, 15:15 + W], ALU.mult, ALU.subtract,
                        [srcr, ar], [dr])
                if bi == 0:
                    dve_tt(fx[:, 0:15], src[:, 15:30], invc[:, g * 16:g * 16 + 15], ALU.mult, [srcr, r_invc], [r_fx])
                    dve_tt(d_[:, 0:15], fx[:, 0:15], ab[:, 15:30], ALU.subtract, [r_fx, ar], [dr])
                if bi < 3:
                    dve_copy(ab[:, 0:15], ab[:, W:W + 15], [ar], [ar])

            def pA_s1(it):
                bi, g = divmod(it, 4)
                c0, W, tl = BLOCKS[bi]
                sq = 0 if bi < 4 else 1
                sg_, sgr = sga[it % 4]
                d_, dr = dT[it % 4]
                py, pyr = psum()
                mm(py[:, 0:W], [(pool_wb[:, g, :], d_[:, 0:W])], [r_poolw, dr], [pyr])
                dve_stt(ycat[:, g, c0:c0 + W], py[:, 0:W], pscale[:, g:g + 1], sg_[:, 0:W], ALU.mult, ALU.mult,
                        [pyr, r_pscale, sgr], [r_yc_t[g][t] for t in tl])
                if bi in (3, 4) and g == 3:
                    pho, phr = psum()
                    transposes([(pho[0:15, gg * 128:(gg + 1) * 128], abuf[gg][0][:, W:W + 15], ident_f)
                                for gg in range(4)], None, [abuf[gg][1] for gg in range(4)] + [r_identf], [phr])
                    act(orow[0:15, :], pho[0:15, :], AF.Copy, [phr], [r_orow])
                    S.dma("sp", o_pool[l, sq], orow[0:15, :], [r_orow], [])

            pipeline([pA_s0, (lambda it: None), pA_s1], 20)
            close(WL["a"])
            close(WL["ga"])

            bB = Bump(arena, *R_MT)
            vnb_p0 = bB.p
            vnb, r_vnb = bB.alloc([128, 17, 512], BF16, "vnb")
            r_vnb_t = [Res(f"vnb{t}") for t in range(NTILE)]
            Mem.claim(vnb_p0, bB.p, r_vnb_t)
            vt = [bB.alloc([128, 512], F32, f"vt{i}") for i in range(2)]
            ssb = [bB.alloc([128, 512], F32, f"ssb{i}") for i in range(2)]
            sgb = [bB.alloc([128, 512], F32, f"sgb{i}") for i in range(2)]
            tB = [bB.alloc([128, 512], F32, f"tB{i}") for i in range(2)]
            bBs = Bump(arena, *R_SCR)
            stB = [bBs.alloc([128, 16], F32, f"stB{i}") for i in range(3)]
            wv_v, wv_r = need(WL["vb"])
            ctxB = {}

            def pB_s0(t):
                c0, R = TILES[t]
                pvb, pvr = psum(hold=True)
                ctxB[t] = (pvb, pvr)
                mm(pvb[0:R, :], [(hT[:, k, c0:c0 + R], wv_v[:, k, :]) for k in range(8)], [wv_r, r_hT_t[t]], [pvr])
                st_, str_ = stB[t % 3]
                S.op("dve", lambda v, o=st_[0:R, 0:6], i=pvb[0:R, :]: v.bn_stats(out=o, in_=i), [pvr], [str_])
                S.op("dve", lambda v, o=st_[0:R, 6:8], i=st_[0:R, 0:6]: v.bn_aggr(out=o, in_=i), [str_], [str_])
                rstd_from_ms(st_[0:R, 7:8], st_[0:R, 8:9], str_, str_)

            def pB_s1(t):
                c0, R = TILES[t]
                pvb, pvr = ctxB.pop(t)
                st_, str_ = stB[t % 3]
                v_, vr_ = vt[t % 2]
                act(st_[0:R, 9:10], st_[0:R, 6:7], AF.Identity, [str_], [str_], scale=st_[0:R, 8:9])
                act(st_[0:R, 9:10], st_[0:R, 9:10], AF.Identity, [str_], [str_], scale=-1.0)
                act(v_[0:R, :], pvb[0:R, :], AF.Identity, [pvr, str_], [vr_], scale=st_[0:R, 8:9], bias=st_[0:R, 9:10])
                punhold(pvr)
                if t == 16:
                    dve_tt(v_[0:R, :], v_[0:R, :], sgug_bc[0:R, :], ALU.mult, [vr_, r_sgug], [vr_])
                    S.dma("sp", o_sgu[l], v_[0:R, :], [vr_], [])
                    act(vnb[0:R, t, :], v_[0:R, :], AF.Copy, [vr_], [r_vnb_t[t]])
                else:
                    S.op("pool", lambda g_, o=vnb[0:R, t, :], a=v_[0:R, :], b=sgug_bc[0:R, :]:
                         g_.tensor_tensor(out=o, in0=a, in1=b, op=ALU.mult), [vr_, r_sgug], [r_vnb_t[t]])

            pipeline([pB_s0, pB_s1], NTILE)
            close(WL["vb"])
            wu_v, wu_r = need(WL["u"])
            wgb_v, wgb_r = need(WL["gb"])
            step = 0
            for bi, (c0, W, tl) in enumerate(BLOCKS):
                rh = [r_hT_t[t] for t in tl]
                for g in range(4):
                    ps_, psr = psum()
                    groups = []
                    for i, t in enumerate(tl):
                        R = TILES[t][1]
                        groups.append((ps_[:, i * 128:i * 128 + R],
                                       [(vnb[0:R, t, g * 128:(g + 1) * 128], WT[0:R, g, 0:R]),
                                        (ones_b[0:1, :], sgub_b[0:1, g * 128:g * 128 + R])]))
                    mm_groups(groups, [r_vnb_t[t] for t in tl] + [r_WT, r_ones, r_sgub], [psr])
                    pu, pur = psum()
                    mm(pu[:, 0:W], [(wu_v[:, k, g * 128:(g + 1) * 128], hT[:, k, c0:c0 + W]) for k in range(8)],
                       [wu_r] + rh, [pur])
                    pg, pgr = psum()
                    mm(pg[:, 0:W], [(wgb_v[:, k, g * 128:(g + 1) * 128], hT[:, k, c0:c0 + W]) for k in range(8)],
                       [wgb_r] + rh, [pgr])
                    s_, sr_ = ssb[step % 2]
                    g_, gr_ = sgb[step % 2]
                    t_, tr_ = tB[step % 2]
                    act(s_[:, 0:W], ps_[:, 0:W], AF.Copy, [psr], [sr_])
                    act(g_[:, 0:W], pg[:, 0:W], AF.Silu, [pgr], [gr_])
                    dve_tt(t_[:, 0:W], pu[:, 0:W], s_[:, 0:W], ALU.mult, [pur, sr_], [tr_])
                    dve_tt(ycat[:, 4 + g, c0:c0 + W], t_[:, 0:W], g_[:, 0:W], ALU.mult, [tr_, gr_],
                           [r_yc_t[4 + g][t] for t in tl])
                    step += 1
            close(WL["u"])
            close(WL["gb"])

            wzl_v, wzl_r = need(WL["zl"])
            wq_v, wq_r = need(WL["q"])
            wk_v, wk_r = need(WL["k"])
            wvc_v, wvc_r = need(WL["vc"])
            bCs = Bump(arena, *R_MT)
            lsb = [bCs.alloc([128, 512], F32, f"lsb{i}") for i in range(1)]
            lhb = [bCs.alloc([128, 512], BF16, f"lhb{i}") for i in range(2)]
            llb = [bCs.alloc([128, 512], BF16, f"llb{i}") for i in range(2)]
            eq = [bCs.alloc([128, 4, 128], F32, f"eq{i}") for i in range(2)]
            ek = [bCs.alloc([128, 4, 128], F32, f"ek{i}") for i in range(2)]
            qtl = [bCs.alloc([128, 4, 128], BF16, f"qtl{i}") for i in range(3)]
            ktl = [bCs.alloc([128, 4, 128], BF16, f"ktl{i}") for i in range(2)]
            ktok = [bCs.alloc([128, 512], BF16, f"ktok{i}") for i in range(2)]
            attm = [bCs.alloc([128, 4, 128], BF16, f"attm{i}") for i in range(2)]
            vb = [bCs.alloc([128, 1024], BF16, f"vb{i}") for i in range(2)]
            ycb = [bCs.alloc([128, 1024], BF16, f"ycb{i}") for i in range(2)]
            bCt = Bump(arena, *R_SCR)
            zhb = [bCt.alloc([17, 128], BF16, f"zhb{i}") for i in range(2)]
            zlb = [bCt.alloc([17, 128], BF16, f"zlb{i}") for i in range(2)]
            ebt = [bCt.alloc([128, 4], F32, f"ebt{i}") for i in range(5)]
            stC = [bCt.alloc([128, 16], F32, f"stC{i}") for i in range(2)]
            junkc, r_junkc = bCt.alloc([128, 256], BF16, "junkc")
            Sf, r_Sf = bCt.alloc([128, 4, 256], F32, "Sf")
            Sbb = [bCt.alloc([128, 4, 256], BF16, f"Sb{i}") for i in range(2)]
            QS = 128.0 ** -0.5
            for zz, zr in zhb:
                S.op("dve", lambda v, o=zz: v.memset(o, 1.0), [], [zr])
            for zz, zr in zlb:
                S.op("dve", lambda v, o=zz: v.memset(o, 0.0), [], [zr])
            S.op("dve", lambda v: v.memset(Sf, 0.0), [], [r_Sf])
            S.op("dve", lambda v, o=Sbb[1][0]: v.memset(o, 0.0), [], [Sbb[1][1]])
            cC = {}

            def rhC(t):
                return [r_hT_t[t]]

            def pC_s0(t):
                c0, R = TILES[t]
                pz, pzr = psum()
                mm(pz[0:16, 0:R], [(wzl_v[:, k, :], hT[:, k, c0:c0 + R]) for k in range(8)], [wzl_r] + rhC(t), [pzr])
                zh_, zhr_ = zhb[t % 2]
                zl_, zlr_ = zlb[t % 2]
                act(zh_[0:16, 0:R], pz[0:16, 0:R], AF.Copy, [pzr], [zhr_])
                dve_tt(zl_[0:16, 0:R], pz[0:16, 0:R], zh_[0:16, 0:R], ALU.subtract, [pzr, zhr_], [zlr_])

            def pC_s1(t):
                c0, R = TILES[t]
                zh_, zhr_ = zhb[t % 2]
                zl_, zlr_ = zlb[t % 2]
                pp, ppr = psum()
                mm(pp[0:R, :], [(zh_[0:17, 0:R], wa_hi[0:17, :]), (zh_[0:17, 0:R], wa_lo[0:17, :]),
                                (zl_[0:17, 0:R], wa_hi[0:17, :])], [zhr_, zlr_, r_wahi, r_walo], [ppr])
                l_, lr_ = lsb[0]
                lh_, lhr_ = lhb[t % 2]
                ll_, llr_ = llb[t % 2]
                act(l_[0:R, :], pp[0:R, :], AF.Exp, [ppr], [lr_], scale=-1.0)
                act(l_[0:R, :], l_[0:R, :], AF.Ln, [lr_, r_onec], [lr_], bias=onec[0:R, :], scale=1.0)
                dve_copy(lh_[0:R, :], l_[0:R, :], [lr_], [lhr_])
                dve_tt(ll_[0:R, :], l_[0:R, :], lh_[0:R, :], ALU.subtract, [lr_, lhr_], [llr_])

            def pC_s2(t):
                c0, R = TILES[t]
                lh_, lhr_ = lhb[t % 2]
                ll_, llr_ = llb[t % 2]
                pbc, pbcr = psum()
                pbc3 = pbc[:, :].rearrange("p (h i) -> p h i", i=128)
                mm_groups([(pbc3[:, h, 0:R], [(lh_[0:R, h * 128:(h + 1) * 128], ucum_b[0:R, 0:R]),
                                              (ll_[0:R, h * 128:(h + 1) * 128], ucum_b[0:R, 0:R])]) for h in range(4)],
                          [lhr_, llr_, r_ucumb], [pbcr])
                eq_, eqr = eq[t % 2]
                ek_, ekr = ek[t % 2]
                eb_, ebr = ebt[t % 5]
                act(eq_[:, :, 0:R], pbc3[:, :, 0:R], AF.Exp, [pbcr], [eqr])
                act(ek_[:, :, 0:R], pbc3[:, :, 0:R], AF.Exp, [pbcr], [ekr], scale=-1.0)
                act(eb_[:, :], pbc3[:, :, R - 1], AF.Exp, [pbcr], [ebr])

            def pC_s3(t):
                c0, R = TILES[t]
                eq_, eqr = eq[t % 2]
                ek_, ekr = ek[t % 2]
                pq, pqr = psum()
                pq3 = pq[:, :].rearrange("p (h i) -> p h i", i=128)
                mm_groups([(pq3[:, h, 0:R], [(wq_v[:, k, h * 128:(h + 1) * 128], hT[:, k, c0:c0 + R]) for k in range(8)])
                           for h in range(4)], [wq_r] + rhC(t), [pqr])
                pk, pkr = psum()
                pk3 = pk[:, :].rearrange("p (h i) -> p h i", i=128)
                mm_groups([(pk3[:, h, 0:R], [(wk_v[:, k, h * 128:(h + 1) * 128], hT[:, k, c0:c0 + R]) for k in range(8)])
                           for h in range(4)], [wk_r] + rhC(t), [pkr])
                q_, qr_ = qtl[t % 3]
                k_, kr_ = ktl[t % 2]
                dve_stt(q_[:, :, 0:R], pq3[:, :, 0:R], QS, eq_[:, :, 0:R], ALU.mult, ALU.mult, [pqr, eqr], [qr_])
                dve_tt(k_[:, :, 0:R], pk3[:, :, 0:R], ek_[:, :, 0:R], ALU.mult, [pkr, ekr], [kr_])

            def pC_s4(t):
                c0, R = TILES[t]
                q_, qr_ = qtl[t % 3]
                k_, kr_ = ktl[t % 2]
                pkt, pktr = psum()
                pkt3 = pbf(pkt)[:, 0:512].rearrange("p (h d) -> p h d", d=128)
                transposes([(pkt3[0:R, h, :], k_[:, h, 0:R], ident_b) for h in range(4)], None, [kr_, r_identb], [pktr])
                kt_, ktr_ = ktok[t % 2]
                act(kt_[0:R, :], pbf(pkt)[0:R, 0:512], AF.Copy, [pktr], [ktr_])
                pat, patr = psum()
                pat3 = pat[:, :].rearrange("p (h i) -> p h i", i=128)
                mm_groups([(pat3[0:R, h, 0:R], [(k_[:, h, 0:R], q_[:, h, 0:R])]) for h in range(4)], [kr_, qr_], [patr])
                at_, atr_ = attm[t % 2]
                dve_tt(at_[0:R, :, 0:R], pat3[0:R, :, 0:R], masku[0:R, 0:R].unsqueeze(1).broadcast_to([R, 4, R]),
                       ALU.mult, [patr, r_masku], [atr_])
                v_, vr_ = vb[t % 2]
                for hf in range(2):
                    pv_, pvr_ = psum()
                    mm(pv_[0:R, :], [(hT[:, k, c0:c0 + R], wvc_v[:, k, hf * 512:(hf + 1) * 512]) for k in range(8)],
                       [wvc_r] + rhC(t), [pvr_])
                    act(v_[0:R, hf * 512:(hf + 1) * 512], pv_[0:R, :], AF.Copy, [pvr_], [vr_])

            def pC_s5(t):
                c0, R = TILES[t]
                sq = 0 if t < 16 else 1
                eb_, ebr = ebt[t % 5]
                q_, qr_ = qtl[t % 3]
                kt_, ktr_ = ktok[t % 2]
                at_, atr_ = attm[t % 2]
                v_, vr_ = vb[t % 2]
                Sprev, r_Sprev = Sbb[(t + 1) % 2]
                Snew, r_Snew = Sbb[t % 2]
                if t == 16:
                    S.dma("sp", Sf, sgla[l].rearrange("h d v -> d h v"), [], [r_Sf])
                    act(Sprev, Sf, AF.Copy, [r_Sf], [r_Sprev])
                st_, str_ = stC[t % 2]
                yc_, ycr_ = ycb[t % 2]
                pos = []
                for hf in range(2):
                    po, por = psum(acc=True)
                    mm_groups([(po[0:R, (h % 2) * 256:(h % 2 + 1) * 256],
                                [(at_[0:R, h, 0:R], v_[0:R, h * 256:(h + 1) * 256]),
                                 (q_[:, h, 0:R], Sprev[:, h, :])]) for h in (2 * hf, 2 * hf + 1)],
                              [atr_, vr_, qr_, r_Sprev], [por])
                    pos.append((po, por))
                for hf in range(2):
                    pkv, pkvr = psum()
                    mm_groups([(pkv[:, (h % 2) * 256:(h % 2 + 1) * 256],
                                [(kt_[0:R, h * 128:(h + 1) * 128], v_[0:R, h * 256:(h + 1) * 256])])
                               for h in (2 * hf, 2 * hf + 1)], [ktr_, vr_], [pkvr])
                    for h in (2 * hf, 2 * hf + 1):
                        dve_ts(Sf[:, h, :], Sf[:, h, :], eb_[:, h:h + 1], None, ALU.mult, ALU.bypass, [r_Sf, ebr], [r_Sf])
                        dve_stt(Sf[:, h, :], pkv[:, (h % 2) * 256:(h % 2 + 1) * 256], eb_[:, h:h + 1], Sf[:, h, :],
                                ALU.mult, ALU.add, [pkvr, ebr, r_Sf], [r_Sf])
                if t in (15, 16):
                    S.dma("sp", o_gla[l, sq].rearrange("h d v -> d h v"), Sf, [r_Sf], [])
                if t < 15:
                    dve_copy(Snew, Sf, [r_Sf], [r_Snew])
                for h in range(4):
                    po, por = pos[h // 2]
                    act(junkc[0:R, :], po[0:R, (h % 2) * 256:(h % 2 + 1) * 256], AF.Square, [por], [r_junkc, str_],
                        accum_out=st_[0:R, h:h + 1])
                rstd_from_ms(st_[0:R, 0:4], st_[0:R, 8:12], str_, str_, scale=1.0 / 256)
                for h in range(4):
                    po, por = pos[h // 2]
                    dve_stt(yc_[0:R, h * 256:(h + 1) * 256], po[0:R, (h % 2) * 256:(h % 2 + 1) * 256],
                            st_[0:R, 8 + h:9 + h], glag_bc[0:R, :], ALU.mult, ALU.mult, [por, str_, r_glag], [ycr_])

            def pC_s6(t):
                c0, R = TILES[t]
                yc_, ycr_ = ycb[t % 2]
                pyt, pytr = psum()
                pyt3 = pbf(pyt).rearrange("p (c t) -> p c t", t=128)
                transposes([(pyt3[:, c, 0:R], yc_[0:R, c * 128:(c + 1) * 128], ident_b[0:R, 0:R]) for c in range(8)],
                           None, [ycr_, r_identb], [pytr])
                act(ycat[:, 8:16, c0:c0 + R], pyt3[:, :, 0:R], AF.Copy, [pytr], [r_yc_t[8 + c][t] for c in range(8)])

            def c2_item(hf, bi, cc):
                def emit():
                    c0, W, tl = BLOCKS[bi]
                    c = hf * 4 + cc
                    wg_v, wg_r = need(WL["gc"][hf])
                    pg, pgr = psum()
                    mm(pg[:, 0:W], [(wg_v[:, k, cc * 128:(cc + 1) * 128], hT[:, k, c0:c0 + W]) for k in range(8)],
                       [wg_r] + [r_hT_t[t] for t in tl], [pgr])
                    act(pg[:, 0:W], pg[:, 0:W], AF.Silu, [pgr], [pgr])
                    rr = [r_yc_t[8 + c][t] for t in tl]
                    dve_tt(ycat[:, 8 + c, c0:c0 + W], ycat[:, 8 + c, c0:c0 + W], pg[:, 0:W], ALU.mult, rr + [pgr], rr)
                return emit
            c2_items = [[c2_item(hf, bi, cc) for bi in range(5) for cc in range(4)] for hf in range(2)]

            def c1_filler():
                for _ in range(2):
                    if len(c2_items[0]) > 8:
                        c2_items[0].pop(0)()

            nrot[0] = 6
            pipeline([pC_s0, pC_s1, pC_s2, pC_s3, pC_s4, pC_s5, pC_s6], NTILE, filler=c1_filler,
                     order=[4, 3, 2, 1, 0, 6, 5])
            nrot[0] = 8
            for nm in ("zl", "q", "k", "vc"):
                close(WL[nm])
            for hf in range(2):
                while c2_items[hf]:
                    c2_items[hf].pop(0)()
                close(WL["gc"][hf])
                if hf == 0 and l + 1 < DEPTH:
                    small_params(l + 1, (R_MT[0] + 24576, R_MT[1]))
                    ada(l + 1, (R_MT[0], R_MT[0] + 24576))
            if DEBUG and l == 0:
                S.barrier()
                S.dma("pool", dbg["hT"], hT.rearrange("p k t -> p (k t)"), [], [])
                S.dma("pool", dbg["ycat"], ycat.rearrange("p k t -> p (k t)"), [], [])
                S.barrier()

            Mem.claim(R_MT[0], R_MT[1], r_mT_t)
            bM = Bump(arena, *R_SCR)
            gsg = [bM.alloc([128, 512], F32, f"gsg{i}") for i in range(3)]
            tM = [bM.alloc([128, 512], F32, f"tM{i}") for i in range(2)]
            KOFF = (0, 4, 8)
            KN = (4, 4, 8)
            for dp in range(4):
                wo_, wor_ = need(WL["wo"][dp])
                wg_, wgr_ = need(WL["wgm"][dp])
                for dd in range(2):
                    dm = dp * 2 + dd
                    ds = slice(dd * 128, (dd + 1) * 128)
                    for bi, (c0, W, tl) in enumerate(BLOCKS):
                        rh = [r_hT_t[t] for t in tl]
                        for i in range(3):
                            pg, pgr = psum()
                            mm(pg[:, 0:W], [(wg_[:, k, i, ds], hT[:, k, c0:c0 + W]) for k in range(8)], [wgr_] + rh, [pgr])
                            act(gsg[i][0][:, 0:W], pg[:, 0:W], AF.Sigmoid, [pgr], [gsg[i][1]])
                        for i in range(3):
                            pp_, ppr_ = psum()
                            ry = [r_yc_t[KOFF[i] + kk][t] for kk in range(KN[i]) for t in tl]
                            mm(pp_[:, 0:W], [(wo_[:, KOFF[i] + kk, ds], ycat[:, KOFF[i] + kk, c0:c0 + W])
                                             for kk in range(KN[i])], [wor_] + ry, [ppr_])
                            if i == 0:
                                dve_tt(tM[0][0][:, 0:W], pp_[:, 0:W], gsg[0][0][:, 0:W], ALU.mult,
                                       [ppr_, gsg[0][1]], [tM[0][1]])
                            elif i == 1:
                                dve_tt(tM[1][0][:, 0:W], pp_[:, 0:W], gsg[1][0][:, 0:W], ALU.mult,
                                       [ppr_, gsg[1][1]], [tM[1][1]])
                                dve_tt(tM[0][0][:, 0:W], tM[0][0][:, 0:W], tM[1][0][:, 0:W], ALU.add,
                                       [tM[0][1], tM[1][1]], [tM[0][1]])
                            else:
                                dve_tt(tM[1][0][:, 0:W], pp_[:, 0:W], gsg[2][0][:, 0:W], ALU.mult,
                                       [ppr_, gsg[2][1]], [tM[1][1]])
                                dve_tt(mT[:, dm, c0:c0 + W], tM[0][0][:, 0:W], tM[1][0][:, 0:W], ALU.add,
                                       [tM[0][1], tM[1][1]], [r_mT_t[t] for t in tl])
                close(WL["wo"][dp])
                close(WL["wgm"][dp])
            if DEBUG and l == 0:
                S.dma("pool", dbg["mT"], mT.rearrange("p k t -> p (k t)"), [], [])
                S.barrier()

            fuse = l + 1 < DEPTH
            if fuse:
                Mem.claim(R_HT[0], R_HT[1], r_hT_t)
            bO = Bump(arena, *R_YC)
            bOs = Bump(arena, *R_SCR)
            wout_v, wout_r = need(WL["wout"])
            G_bc, r_Gbc = bO.alloc([128, 1024], F32, "G_bc")
            idg, r_idg = bO.alloc([128, 8, 128], F32, "idg")
            NXO = 9 if fuse else 6
            xo = [bO.alloc([128, 1024], F32, f"xo{i}") for i in range(NXO)]
            tO = [bO.alloc([128, 1024], F32, f"tO{i}") for i in range(2)]
            stO = [bOs.alloc([128, 8], F32, f"stO{i}") for i in range(5)]
            junko, r_junko = bOs.alloc([128, 512], BF16, "junko")
            cO = {}

            def pO_ld(t):
                c0, R = TILES[t]
                xa, xr = xo[t % NXO]
                S.dma("sp", xa[0:R, :], x_src[c0:c0 + R, :], [r_x1s[t]], [xr])

            def pO_nop(t):
                pass

            def pO_mm(t):
                c0, R = TILES[t]
                sa, sr = stO[t % 5]
                pos = []
                for hf in range(2):
                    po, por = psum(hold=True)
                    mm(po[0:R, :], [(mT[:, k, c0:c0 + R], wout_v[:, k, hf * 512:(hf + 1) * 512]) for k in range(8)],
                       [r_mT_t[t], wout_r], [por])
                    act(junko[0:R, :], po[0:R, :], AF.Square, [por], [r_junko, sr], accum_out=sa[0:R, hf:hf + 1])
                    pos.append((po, por))
                cO[t] = pos

            def pO_sc(t):
                c0, R = TILES[t]
                sa, sr = stO[t % 5]
                act(sa[0:R, 2:3], sa[0:R, 1:2], AF.Identity, [sr, r_epsc], [sr], scale=1.0 / D, bias=epsc[0:R, :])
                act(sa[0:R, 4:5], sa[0:R, 0:1], AF.Ln, [sr], [sr], scale=1.0 / D, bias=sa[0:R, 2:3])
                act(sa[0:R, 4:5], sa[0:R, 4:5], AF.Exp, [sr], [sr], scale=-0.5)

            def pO_gt(t):
                c0, R = TILES[t]
                if t in (0, 16):
                    sq = 0 if t < 16 else 1
                    for k in range(8):
                        dve_ts(idg[:, k, :], ident_f, modcL[:, 16 + k, sq:sq + 1], None, ALU.mult, ALU.bypass,
                               [r_identf, r_modcL], [r_idg])
                    for hf in range(2):
                        pb, pr = psum()
                        mm_groups([(pb[:, kk * 128:(kk + 1) * 128], [(ones_f, idg[:, hf * 4 + kk, :])]) for kk in range(4)],
                                  [r_onesf, r_idg], [pr])
                        act(G_bc[:, hf * 512:(hf + 1) * 512], pb[:, :], AF.Copy, [pr], [r_Gbc])
                ta, tr = tO[t % 2]
                sa, sr = stO[t % 5]
                pos = cO.pop(t)
                for hf in range(2):
                    po, por = pos[hf]
                    hs = slice(hf * 512, (hf + 1) * 512)
                    dve_stt(ta[0:R, hs], po[0:R, :], sa[0:R, 4:5], G_bc[0:R, hs], ALU.mult, ALU.mult,
                            [por, sr, r_Gbc], [tr])
                    punhold(por)

            def pO_add(t):
                c0, R = TILES[t]
                xa, xr = xo[t % NXO]
                ta, tr = tO[t % 2]
                S.op("pool", lambda g, o=xa[0:R, :], a=xa[0:R, :], b=ta[0:R, :]:
                     g.tensor_tensor(out=o, in0=a, in1=b, op=ALU.add), [xr, tr], [xr])
                S.dma("sp", x_dst[c0:c0 + R, :], xa[0:R, :], [xr], [r_x1s[t]])

            stages = [pO_ld, pO_nop, pO_mm, pO_sc, pO_gt, pO_add]
            if fuse:
                stages += p0_stages(l + 1, bO, lambda t: xo[t % NXO], bOs)
            pipeline(stages, NTILE)
            close(WL["wout"])

        S.final_wait("sp")

        keys = S.sem_keys()
        sems = {}
        for kx in keys:
            sems[kx] = es.enter_context(nc.semaphore("s_" + "_".join(str(p) for p in kx)))
        with nc.Block() as block:
            @block.sync
            def _(e):
                S.emit("sp", e, sems)

            @block.gpsimd
            def _(e):
                S.emit("pool", e, sems)

            @block.scalar
            def _(e):
                S.emit("act", e, sems)

            @block.vector
            def _(e):
                S.emit("dve", e, sems)

            @block.tensor
            def _(e):
                S.emit("pe", e, sems)
    return nc


def _consts():
    j = np.arange(128)[:, None]
    i = np.arange(128)[None, :]
    masku = (j <= i).astype(np.float32)
    ucum = (masku * (-1.0 / 16.0)).astype(np.float32)
    invc = np.ones((128, 64), np.float32)
    for g, w in enumerate(POOL_W):
        for t in range(16):
            invc[:, g * 16 + t] = 1.0 / min(w, t + 1)
    return {"k_ident": np.eye(128, dtype=np.float32), "k_masku": masku, "k_ucum": ucum, "k_invc": invc}


_NC_CACHE = {}


def kernel(x_prompt, x_sample, state_pool, state_gla, c_prompt, c_sample,
           ada_w, ada_b, pre_norm_g, post_norm_g, w_in, pool_w, pool_scale,
           sgu_norm_g, sgu_w, sgu_b, gla_wa2, gla_ba, gla_norm_g,
           w_oa, w_ob, w_oc, w_out):
    f = lambda a: np.ascontiguousarray(np.asarray(a, dtype=np.float32))
    x_prompt, x_sample, state_pool, state_gla, c_prompt, c_sample = map(
        f, (x_prompt, x_sample, state_pool, state_gla, c_prompt, c_sample))
    shared = {
        "ada_w": f(ada_w), "ada_b": f(ada_b), "pre_g": f(pre_norm_g), "post_g": f(post_norm_g), "w_in": f(w_in),
        "pool_w": f(pool_w), "pool_scale": f(pool_scale), "sgu_g": f(sgu_norm_g), "sgu_w": f(sgu_w),
        "sgu_b": f(sgu_b).reshape(DEPTH, 512), "gla_wa2": f(gla_wa2), "gla_ba": f(gla_ba), "gla_g": f(gla_norm_g),
        "w_oa": f(w_oa), "w_ob": f(w_ob), "w_oc": f(w_oc), "w_out": f(w_out),
    }
    shared.update(_consts())
    n = 8
    in_maps = []
    for b in range(n):
        m = dict(shared)
        m["xin"] = np.ascontiguousarray(np.concatenate([x_prompt[b], x_sample[b]], axis=0))
        m["c2"] = np.ascontiguousarray(np.stack([c_prompt[b], c_sample[b]], axis=0))
        m["spool"] = np.ascontiguousarray(state_pool[:, b])
        m["sgla"] = np.ascontiguousarray(state_gla[:, b])
        in_maps.append(m)
    if "nc" not in _NC_CACHE:
        _NC_CACHE["nc"] = build_program()
    nc = _NC_CACHE["nc"]
    res = run_bass_kernel_spmd(nc, in_maps, core_ids=list(range(n)))
    rs = res.results
    if DEBUG:
        kernel.dbg = rs
    y_prompt = np.stack([rs[b]["y"][:TP] for b in range(n)], axis=0)
    y_sample = np.stack([rs[b]["y"][TP:] for b in range(n)], axis=0)
    pool_p = np.stack([rs[b]["o_pool"][:, 0] for b in range(n)], axis=1)
    pool_s = np.stack([rs[b]["o_pool"][:, 1] for b in range(n)], axis=1)
    gla_p = np.stack([rs[b]["o_gla"][:, 0] for b in range(n)], axis=1)
    gla_s = np.stack([rs[b]["o_gla"][:, 1] for b in range(n)], axis=1)
    sgu_s = np.stack([rs[b]["o_sgu"] for b in range(n)], axis=1)
    return (y_prompt.astype(np.float32), y_sample.astype(np.float32), pool_p.astype(np.float32),
            gla_p.astype(np.float32), pool_s.astype(np.float32), gla_s.astype(np.float32),
            sgu_s.astype(np.float32))
```

```python
import numpy as np
from contextlib import ExitStack
import concourse.bass as bass
import concourse.mybir as mybir
from concourse.bass_utils import run_bass_kernel_spmd

F32 = mybir.dt.float32
BF16 = mybir.dt.bfloat16
AF = mybir.ActivationFunctionType
ALU = mybir.AluOpType

D = 1024
TP = 2048
TS = 64
NT = TP + TS
DEPTH = 2
D_IN = 8720
EPS = 1e-6
C_A, C_GA, C_U, C_VB, C_GB, C_Q, C_K, C_VC, C_GC, C_LR, C_GM = (
    0, 512, 1024, 1536, 2048, 2560, 3072, 3584, 4608, 5632, 5648)
POOL_W = (2, 4, 8, 16)

DEBUG = False


class Res:
    __slots__ = ("w", "r", "name")

    def __init__(self, name=""):
        self.w = None
        self.r = []
        self.name = name


class Sched:
    ENGS = ("pe", "act", "dve", "pool", "sp")
    NDMA = 8

    def __init__(self):
        self.ops = {e: [] for e in self.ENGS}
        self.count = {e: 0 for e in self.ENGS}
        self.known = {e: {} for e in self.ENGS}
        self.dma_n = {e: 0 for e in self.ENGS}
        self.dma_tokens = []

    def _waits(self, eng, reads, writes, extra=()):
        toks = list(extra)
        for r in reads:
            if r.w is not None:
                toks.append(r.w)
        for w in writes:
            if w.w is not None:
                toks.append(w.w)
            toks.extend(w.r)
        kn = self.known[eng]
        waits = {}
        for (sem, val) in toks:
            if sem == ("e", eng) and eng == "pe":
                continue
            if kn.get(sem, 0) >= val:
                continue
            if waits.get(sem, 0) < val:
                waits[sem] = val
        for sem, val in waits.items():
            kn[sem] = val
        return list(waits.items())

    def _commit(self, tok, reads, writes):
        for r in reads:
            r.r.append(tok)
        for w in writes:
            w.w = tok
            w.r = []

    def op(self, eng, fn, reads=(), writes=()):
        waits = self._waits(eng, reads, writes)
        self.count[eng] += 1
        tok = (("e", eng), self.count[eng])
        self.ops[eng].append((waits, fn, tok, 1))
        self._commit(tok, reads, writes)
        return tok

    def dma(self, eng, out, in_, reads=(), writes=(), **kw):
        n = self.dma_n[eng]
        self.dma_n[eng] += 1
        sem = ("d", eng, n % self.NDMA)
        val = 16 * (n // self.NDMA + 1)
        extra = [(sem, val - 16)] if val > 16 else []
        waits = self._waits(eng, reads, writes, extra)
        tok = (sem, val)

        def fn(e, out=out, in_=in_, kw=kw):
            return e.dma_start(out=out, in_=in_, **kw)
        self.ops[eng].append((waits, fn, tok, 16))
        self._commit(tok, reads, writes)
        self.dma_tokens.append(tok)
        return tok

    def barrier(self):
        toks = [(("e", e), self.count[e]) for e in self.ENGS if self.count[e] > 0]
        last = {}
        for (sem, val) in self.dma_tokens:
            if last.get(sem, 0) < val:
                last[sem] = val
        toks += list(last.items())
        for e in self.ENGS:
            kn = self.known[e]
            waits = []
            for (sem, val) in toks:
                if sem == ("e", e) and e == "pe":
                    continue
                if kn.get(sem, 0) < val:
                    kn[sem] = val
                    waits.append((sem, val))
            if waits:
                self.ops[e].append((waits, None, None, 0))

    def final_wait(self, eng="sp"):
        last = {}
        for (sem, val) in self.dma_tokens:
            if last.get(sem, 0) < val:
                last[sem] = val
        self.ops[eng].append((list(last.items()), None, None, 0))

    def sem_keys(self):
        keys = [("e", e) for e in self.ENGS]
        for e in self.ENGS:
            if self.dma_n[e] > 0:
                keys += [("d", e, i) for i in range(min(self.NDMA, self.dma_n[e]))]
        return keys

    def emit(self, eng, engine, sems):
        for (waits, fn, tok, inc) in self.ops[eng]:
            for (sem, val) in waits:
                engine.wait_ge(sems[sem], val)
            if fn is not None:
                ins = fn(engine)
                ins.then_inc(sems[tok[0]], inc)


class Mem:
    live = []

    @classmethod
    def claim(cls, start, end, res_list):
        seed = {}
        keep = []
        for (s0, e0, rl) in cls.live:
            if s0 < end and start < e0:
                for r in rl:
                    toks = list(r.r)
                    if r.w is not None:
                        toks.append(r.w)
                    for (sem, val) in toks:
                        if seed.get(sem, 0) < val:
                            seed[sem] = val
                if not (start <= s0 and e0 <= end):
                    keep.append((s0, e0, rl))
            else:
                keep.append((s0, e0, rl))
        keep.append((start, end, res_list))
        cls.live = keep
        for r in res_list:
            r.w = None
            r.r = list(seed.items())


class Bump:
    def __init__(self, arena, start, end):
        self.arena = arena
        self.p = start
        self.end = end

    def alloc(self, shape, dt, name=""):
        esz = 4 if dt == F32 else 2
        n = 1
        for s in shape[1:]:
            n *= s
        nbytes = (n * esz + 31) // 32 * 32
        assert self.p + nbytes <= self.end, f"arena overflow for {name}: {self.p}+{nbytes}>{self.end}"
        o4 = self.p // 4
        v = self.arena[:, o4:o4 + nbytes // 4]
        if dt != F32:
            v = v.bitcast(dt)
        v = v[:, 0:n]
        if len(shape) == 3:
            v = v.rearrange("p (a b) -> p a b", b=shape[2])
        elif len(shape) == 4:
            v = v.rearrange("p (a b c) -> p a b c", b=shape[2], c=shape[3])
        if shape[0] < 128:
            v = v[0:shape[0]]
        res = Res(name)
        Mem.claim(self.p, self.p + nbytes, [res])
        self.p += nbytes
        return v, res


def build_program():
    nc = bass.Bass("TRN2", target_bir_lowering=False)
    S = Sched()
    Mem.live = []

    def din(name, shape):
        return nc.dram_tensor(name, list(shape), F32, kind="ExternalInput").ap()

    def dout(name, shape):
        return nc.dram_tensor(name, list(shape), F32, kind="ExternalOutput").ap()

    xin = din("xin", [NT, D])
    c2 = din("c2", [2, D])
    spool = din("spool", [DEPTH, 15, 512])
    sgla = din("sgla", [DEPTH, 4, 128, 256])
    ada_w = din("ada_w", [DEPTH, D, 3 * D])
    ada_b = din("ada_b", [DEPTH, 3 * D])
    pre_g = din("pre_g", [DEPTH, D])
    post_g = din("post_g", [DEPTH, D])
    w_in = din("w_in", [DEPTH, D, D_IN])
    pool_w = din("pool_w", [DEPTH, 4, 128, 128])
    pool_scale = din("pool_scale", [DEPTH, 512])
    sgu_g = din("sgu_g", [DEPTH, 512])
    sgu_w = din("sgu_w", [DEPTH, 4, 128, 128])
    sgu_b = din("sgu_b", [DEPTH, 512])
    gla_wa2 = din("gla_wa2", [DEPTH, 16, 512])
    gla_ba = din("gla_ba", [DEPTH, 512])
    gla_g = din("gla_g", [DEPTH, 256])
    w_oa = din("w_oa", [DEPTH, 512, D])
    w_ob = din("w_ob", [DEPTH, 512, D])
    w_oc = din("w_oc", [DEPTH, 1024, D])
    w_out = din("w_out", [DEPTH, D, D])
    k_ident = din("k_ident", [128, 128])
    k_masku = din("k_masku", [128, 128])
    k_ucum = din("k_ucum", [128, 128])
    k_invc = din("k_invc", [128, 64])

    y = dout("y", [NT, D])
    o_pool = dout("o_pool", [DEPTH, 2, 15, 512])
    o_gla = dout("o_gla", [DEPTH, 2, 4, 128, 256])
    o_sgu = dout("o_sgu", [DEPTH, TS, 512])
    x1s = nc.dram_tensor("x1s", [NT, D], F32, kind="Internal").ap()
    dbg = {}
    if DEBUG:
        dbg["hT"] = dout("d_hT", [128, 8 * NT])
        dbg["ycat"] = dout("d_ycat", [128, 16 * NT])
        dbg["mT"] = dout("d_mT", [128, 8 * NT])

    ARENA_B = 210944
    with ExitStack() as es:
        arena = es.enter_context(nc.sbuf_tensor("arena", [128, ARENA_B // 4], F32))
        banks = [es.enter_context(nc.psum_tensor(f"bank{i}", [128, 512], F32)) for i in range(8)]
        bank_res = [Res(f"bank{i}") for i in range(8)]
        bank_i = [0]

        nrot = [8]
        acc_i = [0]

        held = set()

        def psum(acc=False, hold=False):
            if acc:
                i = 6 + acc_i[0] % 2
                acc_i[0] += 1
            else:
                for _ in range(nrot[0] + 1):
                    i = bank_i[0] % nrot[0]
                    bank_i[0] += 1
                    if i not in held:
                        break
                else:
                    raise RuntimeError("no free PSUM bank")
            if hold:
                held.add(i)
            return banks[i], bank_res[i]

        def punhold(res):
            held.discard(bank_res.index(res))

        def pbf(bank):
            return bank[:, :].bitcast(BF16)

        HT_B, YC_B, MT_B = 8 * NT * 2, 16 * NT * 2, 8 * NT * 2
        R_HT = (0, HT_B)
        R_YC = (HT_B, HT_B + YC_B)
        R_MT = (HT_B + YC_B, HT_B + YC_B + MT_B)
        R_CONST = (R_MT[1], R_MT[1] + 14336)
        R_WA = (R_CONST[1], R_CONST[1] + 51200)
        R_SCR = (R_WA[1], ARENA_B)
        assert R_SCR[1] - R_SCR[0] == 10240

        def resident(rg, k):
            o4 = rg[0] // 4
            return arena[:, o4:o4 + (rg[1] - rg[0]) // 4].bitcast(BF16).rearrange("p (k t) -> p k t", k=k)
        hT = resident(R_HT, 8)
        ycat = resident(R_YC, 16)
        mT = resident(R_MT, 8)
        NTILE = 17
        r_hT_t = [Res(f"hT{t}") for t in range(NTILE)]
        r_yc_t = [[Res(f"yc{c}_{t}") for t in range(NTILE)] for c in range(16)]
        r_mT_t = [Res(f"mT{t}") for t in range(NTILE)]

        bc = Bump(arena, *R_CONST)
        ident_f, r_identf = bc.alloc([128, 128], F32, "ident_f")
        ident_b, r_identb = bc.alloc([128, 128], BF16, "ident_b")
        masku, r_masku = bc.alloc([128, 128], F32, "masku")
        ucum, r_ucum = bc.alloc([128, 128], F32, "ucum")
        invc, r_invc = bc.alloc([128, 64], F32, "invc")
        ones_f, r_onesf = bc.alloc([128, 128], F32, "ones_f")
        ones_b, r_ones = bc.alloc([1, 128], BF16, "ones_b")
        scT, r_scT = bc.alloc([128, 8, 2], BF16, "scT")
        pool_wb, r_poolw = bc.alloc([128, 4, 128], BF16, "pool_wb")
        WT, r_WT = bc.alloc([128, 4, 128], BF16, "WT")
        pscale, r_pscale = bc.alloc([128, 4], F32, "pscale")
        sgug_bc, r_sgug = bc.alloc([128, 512], F32, "sgug_bc")
        glag_bc, r_glag = bc.alloc([128, 256], F32, "glag_bc")
        sgub_b, r_sgub = bc.alloc([1, 512], BF16, "sgub_b")
        wa_hi, r_wahi = bc.alloc([17, 512], BF16, "wa_hi")
        wa_lo, r_walo = bc.alloc([17, 512], BF16, "wa_lo")
        ucum_b, r_ucumb = bc.alloc([128, 128], BF16, "ucum_b")
        modcs = [bc.alloc([128, 24, 2], F32, f"modc{i}") for i in range(2)]
        epsc, r_epsc = bc.alloc([128, 1], F32, "epsc")
        onec, r_onec = bc.alloc([128, 1], F32, "onec")

        class Chunk:
            __slots__ = ("ap", "res", "closed", "iv")

            def __init__(self):
                self.ap = None
                self.res = None
                self.closed = False
                self.iv = None

        wa_p = [R_WA[0]]
        wa_chunks = []
        wa_pending = []

        def enqueue(nbytes, emit_fn):
            ch = Chunk()
            wa_pending.append((ch, nbytes, emit_fn))
            return ch

        def pump():
            while wa_pending:
                ch, nbytes, emit_fn = wa_pending[0]
                opens = [c.iv for c in wa_chunks if not c.closed]

                def fits(p0):
                    if p0 + nbytes > R_WA[1]:
                        return False
                    return all(not (iv[0] < p0 + nbytes and p0 < iv[1]) for iv in opens)
                cands = [wa_p[0], R_WA[0]] + sorted(iv[1] for iv in opens)
                p = next((c for c in cands if fits(c)), None)
                if p is None:
                    break
                wa_pending.pop(0)
                res = Res("wa")
                Mem.claim(p, p + nbytes, [res])
                ch.iv = (p, p + nbytes)
                ch.res = res
                flat = arena[:, p // 4:(p + nbytes) // 4].bitcast(BF16)
                ch.ap = emit_fn(flat, res)
                wa_chunks[:] = [c for c in wa_chunks if not (c.closed and p <= c.iv[0] and c.iv[1] <= p + nbytes)]
                wa_chunks.append(ch)
                wa_p[0] = p + nbytes

        def need(ch):
            if ch.ap is None:
                pump()
            assert ch.ap is not None, "weight chunk not resident: weight arena too small for this order"
            return ch.ap, ch.res

        def close(ch):
            ch.closed = True
            pump()

        def wload(dram_cols, ncols, k=8):
            def emit(flat, res, dram_cols=dram_cols, ncols=ncols, k=k):
                v = flat[:, 0:k * ncols].rearrange("p (k n) -> p k n", n=ncols)
                for c0_ in range(0, ncols, 512):
                    c1_ = min(ncols, c0_ + 512)
                    S.dma("pool", v[:, :, c0_:c1_], dram_cols[:, c0_:c1_].rearrange("(k p) n -> p k n", p=128), [], [res])
                return v
            return enqueue(k * ncols * 2, emit)

        def wload_multi(nbytes, shape_fn, parts):
            def emit(flat, res, shape_fn=shape_fn, parts=parts):
                v = shape_fn(flat)
                for dst_fn, src in parts:
                    S.dma("pool", dst_fn(v), src.rearrange("(k p) n -> p k n", p=128), [], [res])
                return v
            return enqueue(nbytes, emit)

        WQ = [dict() for _ in range(DEPTH)]

        def enq_ada(l_):
            WQ[l_]["ada"] = [wload(ada_w[l_][:, cg * 512:(cg + 1) * 512], 512) for cg in range(6)]

        enq_ada(0)
        for l_ in range(DEPTH):
            wl_ = w_in[l_]
            q = WQ[l_]
            q["a"] = wload(wl_[:, C_A:C_A + 512], 512)
            q["ga"] = wload(wl_[:, C_GA:C_GA + 512], 512)
            q["vb"] = wload(wl_[:, C_VB:C_VB + 512], 512)
            q["u"] = wload(wl_[:, C_U:C_U + 512], 512)
            q["gb"] = wload(wl_[:, C_GB:C_GB + 512], 512)
            q["zl"] = wload(wl_[:, C_LR:C_LR + 16], 16)
            q["q"] = wload(wl_[:, C_Q:C_Q + 512], 512)
            q["k"] = wload(wl_[:, C_K:C_K + 512], 512)
            q["vc"] = wload(wl_[:, C_VC:C_VC + 1024], 1024)
            q["gc"] = [wload(wl_[:, C_GC:C_GC + 512], 512)]
            if l_ + 1 < DEPTH:
                enq_ada(l_ + 1)
            q["gc"].append(wload(wl_[:, C_GC + 512:C_GC + 1024], 512))
            q["wo"] = []
            q["wgm"] = []
            for dp in range(4):
                cs = slice(dp * 256, (dp + 1) * 256)
                q["wo"].append(wload_multi(
                    16 * 256 * 2, lambda f: f.rearrange("p (k n) -> p k n", n=256),
                    [(lambda v: v[:, 0:4, :], w_oa[l_][:, cs]), (lambda v: v[:, 4:8, :], w_ob[l_][:, cs]),
                     (lambda v: v[:, 8:16, :], w_oc[l_][:, cs])]))
                q["wgm"].append(wload_multi(
                    8 * 3 * 256 * 2, lambda f: f.rearrange("p (k i n) -> p k i n", i=3, n=256),
                    [((lambda v, i=i: v[:, :, i, :]),
                      wl_[:, C_GM + i * 1024 + dp * 256:C_GM + i * 1024 + (dp + 1) * 256]) for i in range(3)]))
            q["wout"] = wload(w_out[l_], 1024)

        def mm(out, pairs, reads, writes):
            def fn(pe, out=out, pairs=pairs):
                n = len(pairs)
                for i, (l, r) in enumerate(pairs):
                    ins = pe.matmul(out, lhsT=l, rhs=r, start=(i == 0), stop=(i == n - 1))
                return ins
            S.op("pe", fn, reads, writes)

        def mm_groups(groups, reads, writes):
            def fn(pe, groups=groups):
                for out, pairs in groups:
                    n = len(pairs)
                    for i, (l, r) in enumerate(pairs):
                        ins = pe.matmul(out, lhsT=l, rhs=r, start=(i == 0), stop=(i == n - 1))
                return ins
            S.op("pe", fn, reads, writes)

        def transposes(items, ident, reads, writes):
            def fn(pe, items=items):
                for out, in_, idn in items:
                    ins = pe.transpose(out=out, in_=in_, identity=idn)
                return ins
            S.op("pe", fn, reads, writes)

        def act(out, in_, func, reads, writes, **kw):
            S.op("act", lambda a, out=out, in_=in_, func=func, kw=kw: a.activation(out=out, in_=in_, func=func, **kw),
                 reads, writes)

        def dve_tt(out, in0, in1, op, reads, writes):
            S.op("dve", lambda v, o=out, a=in0, b=in1, op=op: v.tensor_tensor(out=o, in0=a, in1=b, op=op), reads, writes)

        def dve_ts(out, in0, s1, s2, op0, op1, reads, writes):
            S.op("dve", lambda v, o=out, a=in0, s1=s1, s2=s2, op0=op0, op1=op1:
                 v.tensor_scalar(out=o, in0=a, scalar1=s1, scalar2=s2, op0=op0, op1=op1), reads, writes)

        def dve_stt(out, in0, scalar, in1, op0, op1, reads, writes):
            S.op("dve", lambda v, o=out, a=in0, s=scalar, b=in1, op0=op0, op1=op1:
                 v.scalar_tensor_tensor(out=o, in0=a, scalar=s, in1=b, op0=op0, op1=op1), reads, writes)

        def dve_copy(out, in_, reads, writes):
            S.op("dve", lambda v, o=out, i=in_: v.tensor_copy(out=o, in_=i), reads, writes)

        def rstd_from_ms(ms, out, r_ms, r_out, scale=1.0):
            act(out, ms, AF.Ln, [r_ms, r_epsc], [r_out], bias=epsc[0:ms.shape[0], :], scale=scale)
            act(out, out, AF.Exp, [r_out], [r_out], scale=-0.5)

        class Pipe:
            def __init__(self, stages, n, filler=None, order=None):
                self.stages = stages
                self.n = n
                self.ns = len(stages)
                self.order = order if order is not None else list(reversed(range(self.ns)))
                self.filler = filler
                self.k = 0

            def step(self):
                if self.k >= self.n + self.ns - 1:
                    return False
                for si in self.order:
                    i = self.k - si
                    if 0 <= i < self.n:
                        self.stages[si](i)
                if self.filler is not None and self.k >= self.n - 1:
                    self.filler()
                self.k += 1
                return True

            def run(self):
                while self.step():
                    pass

        def pipeline(stages, n, filler=None, order=None):
            Pipe(stages, n, filler, order).run()

        S.dma("sp", ident_f, k_ident, [], [r_identf])
        S.dma("pool", ident_b, k_ident, [], [r_identb])
        S.dma("sp", masku, k_masku, [], [r_masku])
        S.dma("sp", ucum, k_ucum, [], [r_ucum])
        S.dma("pool", ucum_b, k_ucum, [], [r_ucumb])
        S.dma("sp", invc, k_invc, [], [r_invc])
        S.op("dve", lambda v: v.memset(ones_f, 1.0), [], [r_onesf])
        S.op("dve", lambda v: v.memset(ones_b, 1.0), [], [r_ones])
        S.op("dve", lambda v: v.memset(epsc, EPS), [], [r_epsc])
        S.op("dve", lambda v: v.memset(onec, 1.0), [], [r_onec])

        bs = Bump(arena, *R_SCR)
        crow, r_crow = bs.alloc([2, 1024], F32, "crow")
        S.dma("sp", crow, c2, [], [r_crow])
        act(crow, crow, AF.Silu, [r_crow], [r_crow])
        pb, pr = psum()
        transposes([(pb[:, 2 * k:2 * k + 2], crow[0:2, k * 128:(k + 1) * 128], ident_f[0:2, 0:2]) for k in range(8)],
                   None, [r_crow, r_identf], [pr])
        dve_copy(scT, pb[:, 0:16].rearrange("p (k s) -> p k s", s=2), [pr], [r_scT])

        pump()
        TILES = [(t * 128, 128) for t in range(16)] + [(TP, TS)]
        r_x1s = [Res(f"x1s{t}") for t in range(17)]
        BLOCKS = [(0, 512, [0, 1, 2, 3]), (512, 512, [4, 5, 6, 7]), (1024, 512, [8, 9, 10, 11]),
                  (1536, 512, [12, 13, 14, 15]), (TP, TS, [16])]

        def small_params(l, region):
            bsm = Bump(arena, *region)
            stg, r_stg = bsm.alloc([128, 4, 128], F32, "stg")
            S.dma("pool", pool_wb, pool_w[l].rearrange("g c d -> c g d"), [], [r_poolw])
            S.dma("sp", pscale, pool_scale[l].rearrange("(g p) -> p g", p=128), [], [r_pscale],
                  allow_slow_non_contiguous=True)
            S.dma("sp", sgug_bc, sgu_g[l].partition_broadcast(128), [], [r_sgug])
            S.dma("sp", glag_bc, gla_g[l].partition_broadcast(128), [], [r_glag])
            S.dma("pool", sgub_b, sgu_b[l].rearrange("(o n) -> o n", o=1), [], [r_sgub])
            wa2f, r_wa2f = bsm.alloc([17, 512], F32, "wa2f")
            S.dma("sp", wa2f[0:16, :], gla_wa2[l], [], [r_wa2f])
            S.dma("sp", wa2f[16:17, :], gla_ba[l].rearrange("(o n) -> o n", o=1), [r_wa2f], [r_wa2f])
            dve_copy(wa_hi, wa2f, [r_wa2f], [r_wahi])
            dve_tt(wa_lo, wa2f, wa_hi, ALU.subtract, [r_wa2f, r_wahi], [r_walo])
            S.dma("sp", stg, sgu_w[l].rearrange("g i j -> i g j"), [], [r_stg])
            pb, pr = psum()
            transposes([(pb[:, g * 128:(g + 1) * 128], stg[:, g, :], ident_f) for g in range(4)], None,
                       [r_stg, r_identf], [pr])
            dve_tt(WT, pb[:, :].rearrange("p (g i) -> p g i", i=128),
                   masku.unsqueeze(1).broadcast_to([128, 4, 128]), ALU.mult, [pr, r_masku], [r_WT])

        def ada(l, region, run=True):
            WL = WQ[l]
            modc, r_modc = modcs[l % 2]
            ba = Bump(arena, *region)
            mod, r_mod = ba.alloc([2, 3072], F32, "mod")
            preg2, r_preg2 = ba.alloc([2, 1024], F32, "preg2")
            postg2, r_postg2 = ba.alloc([2, 1024], F32, "postg2")
            S.dma("sp", mod, ada_b[l].partition_broadcast(2), [], [r_mod])
            S.dma("sp", preg2, pre_g[l].partition_broadcast(2), [], [r_preg2])
            S.dma("sp", postg2, post_g[l].partition_broadcast(2), [], [r_postg2])

            def to_cols(j0, j1):
                pb, pr = psum()
                transposes([(pb[:, 2 * j:2 * j + 2], mod[0:2, j * 128:(j + 1) * 128], ident_f[0:2, 0:2])
                            for j in range(j0, j1)], None, [r_mod, r_identf], [pr])
                dve_copy(modc[:, j0:j1, :], pb[:, 2 * j0:2 * j1].rearrange("p (j s) -> p j s", s=2), [pr], [r_modc])

            def group(cg):
                wv, wr = need(WL["ada"][cg])
                pb, pr = psum()
                mm(pb[0:2, :], [(scT[:, k, :], wv[:, k, :]) for k in range(8)], [r_scT, wr], [pr])
                dve_tt(mod[:, cg * 512:(cg + 1) * 512], pb[0:2, :], mod[:, cg * 512:(cg + 1) * 512], ALU.add,
                       [pr, r_mod], [r_mod])
                close(WL["ada"][cg])
                if cg == 3:
                    dve_stt(mod[:, 1024:2048], mod[:, 1024:2048], 1.0, preg2, ALU.add, ALU.mult,
                            [r_mod, r_preg2], [r_mod])
                    to_cols(0, 16)
                if cg == 5:
                    dve_tt(mod[:, 2048:3072], mod[:, 2048:3072], postg2, ALU.mult, [r_mod, r_postg2], [r_mod])
                    to_cols(16, 24)
            steps = [(lambda cg=cg: group(cg)) for cg in range(6)]
            if run:
                for st in steps:
                    st()
            return steps

        def p0_stages(l, bump, get_x, stat_bump, pool_xn=False, raw=False):
            modc, r_modc = modcs[l % 2]
            xnb = [bump.alloc([128, 1024], BF16, f"xnb{i}") for i in range(2)]
            tmod, r_tmod = bump.alloc([128, 8, 128], F32, "tmod")
            junk, r_junk = bump.alloc([128, 1024], BF16, "junk")
            st0 = [stat_bump.alloc([128, 4], F32, f"st0{i}") for i in range(4)]

            def sq(t):
                c0, R = TILES[t]
                xa, xr = get_x(t)
                sa, sr = st0[t % 4]
                if raw:
                    S.op("dve", lambda v, o=junk[0:R, :], a=xa[0:R, :], acc=sa[0:R, 0:1]:
                         v.scalar_tensor_tensor(out=o, in0=a, scalar=1.0, in1=a, op0=ALU.mult, op1=ALU.mult,
                                                accum_out=acc), [xr], [r_junk, sr])
                else:
                    act(junk[0:R, :], xa[0:R, :], AF.Square, [xr], [r_junk, sr], accum_out=sa[0:R, 0:1])

            def sc(t):
                c0, R = TILES[t]
                sa, sr = st0[t % 4]
                pass

            def nrm(t):
                c0, R = TILES[t]
                xa, xr = get_x(t)
                sa, sr = st0[t % 4]
                xn, xnr = xnb[t % 2]
                rstd_from_ms(sa[0:R, 0:1], sa[0:R, 2:3], sr, sr, scale=1.0 / D)
                if pool_xn:
                    S.op("pool", lambda g_, o=xn[0:R, :], a=xa[0:R, :], sc_=sa[0:R, 2:3]:
                         g_.tensor_scalar(out=o, in0=a, scalar1=sc_, scalar2=None, op0=ALU.mult), [xr, sr], [xnr])
                else:
                    act(xn[0:R, :], xa[0:R, :], AF.Copy, [xr, sr], [xnr], scale=sa[0:R, 2:3])

            def tp(t):
                c0, R = TILES[t]
                xn, xnr = xnb[t % 2]
                pb, pr = psum(hold=True)
                pv = pbf(pb).rearrange("p (k t) -> p k t", t=128)
                transposes([(pv[:, k, 0:R], xn[0:R, k * 128:(k + 1) * 128], ident_b[0:R, 0:R]) for k in range(8)],
                           None, [xnr, r_identb], [pr])
                ctx0[t] = (pv, pr)

            def md(t):
                c0, R = TILES[t]
                sq_ = 0 if t < 16 else 1
                pv, pr = ctx0.pop(t)
                if raw:
                    dve_copy(hT[:, :, c0:c0 + R], pv[:, :, 0:R], [pr], [r_hT_t[t]])
                    punhold(pr)
                    return
                dve_tt(tmod[:, :, 0:R], pv[:, :, 0:R], modc[:, 8:16, sq_:sq_ + 1].broadcast_to([128, 8, R]), ALU.mult,
                       [pr, r_modc], [r_tmod])
                dve_tt(hT[:, :, c0:c0 + R], tmod[:, :, 0:R], modc[:, 0:8, sq_:sq_ + 1].broadcast_to([128, 8, R]),
                       ALU.add, [r_tmod, r_modc], [r_hT_t[t]])
                punhold(pr)

            ctx0 = {}
            return [sq, nrm, tp, md]

        for l in range(DEPTH):
            x_src = xin if l == 0 else x1s
            x_dst = x1s if l == 0 else y
            wl = w_in[l]
            WL = WQ[l]
            modcL, r_modcL = modcs[l % 2]

            if l == 0:
                small_params(0, R_SCR)
                Mem.claim(R_HT[0], R_HT[1], r_hT_t)
                ada_steps = ada(0, (R_YC[0], R_YC[0] + 24576), run=False)
                b0 = Bump(arena, R_YC[0] + 24576, R_MT[1])
                xt = [b0.alloc([128, 1024], F32, f"xt{i}") for i in range(8)]

                def p0_ld(t):
                    c0, R = TILES[t]
                    xa, xr = xt[t % 8]
                    S.dma("sp", xa[0:R, :], x_src[c0:c0 + R, :], [r_x1s[t]], [xr])

                def p0_nop(t):
                    pass

                P0 = Pipe([p0_ld, p0_nop] + p0_stages(0, b0, lambda t: xt[t % 8], b0, raw=True), NTILE)
                while P0.step():
                    if P0.k in (4, 8, 12, 16, 19, 22):
                        ada_steps.pop(0)()
                while ada_steps:
                    ada_steps.pop(0)()
                modc0, r_modc0 = modcs[0]
                for bi, (c0, W, tl) in enumerate(BLOCKS):
                    sq0 = 0 if bi < 4 else 1
                    rr = [r_hT_t[t] for t in tl]
                    for k in range(8):
                        if k % 2 == 0:
                            dve_ts(hT[:, k, c0:c0 + W], hT[:, k, c0:c0 + W], modc0[:, 8 + k, sq0:sq0 + 1],
                                   modc0[:, k, sq0:sq0 + 1], ALU.mult, ALU.add, rr + [r_modc0], rr)
                        else:
                            act(hT[:, k, c0:c0 + W], hT[:, k, c0:c0 + W], AF.Identity, rr + [r_modc0], rr,
                                scale=modc0[:, 8 + k, sq0:sq0 + 1], bias=modc0[:, k, sq0:sq0 + 1])

            Mem.claim(R_YC[0], R_YC[1], [r for rl in r_yc_t for r in rl])
            bA = Bump(arena, *R_MT)
            abuf = [bA.alloc([128, 15 + 512], F32, f"abuf{g}") for g in range(4)]
            tA = [bA.alloc([128, 15 + 512], F32, f"tA{i}") for i in range(3)]
            dT = [bA.alloc([128, 512], BF16, f"dT{i}") for i in range(4)]
            sga = [bA.alloc([128, 512], F32, f"sga{i}") for i in range(4)]
            fx, r_fx = bA.alloc([128, 16], F32, "fx")
            hrow, r_hrow = bA.alloc([16, 512], F32, "hrow")
            orow, r_orow = bA.alloc([16, 512], F32, "orow")
            wa_v, wa_r = need(WL["a"])
            wg_v, wg_r = need(WL["ga"])
            ctxA = {}

            def pA_s0(it):
                bi, g = divmod(it, 4)
                c0, W, tl = BLOCKS[bi]
                rh = [r_hT_t[t] for t in tl]
                if bi == 0 and g == 0:
                    for gg in range(4):
                        S.op("dve", lambda v, o=abuf[gg][0][:, 0:15]: v.memset(o, 0.0), [], [abuf[gg][1]])
                if bi == 4 and g == 0:
                    S.dma("sp", hrow[0:15, :], spool[l], [], [r_hrow])
                    pb, pr = psum()
                    transposes([(pb[:, gg * 16:gg * 16 + 15], hrow[0:15, gg * 128:(gg + 1) * 128], ident_f[0:15, 0:15])
                                for gg in range(4)], None, [r_hrow, r_identf], [pr])
                    for gg in range(4):
                        dve_copy(abuf[gg][0][:, 0:15], pb[:, gg * 16:gg * 16 + 15], [pr], [abuf[gg][1]])
                ab, ar = abuf[g]
                w = POOL_W[g]
                pa, par = psum()
                mm(pa[:, 0:W], [(wa_v[:, k, g * 128:(g + 1) * 128], hT[:, k, c0:c0 + W]) for k in range(8)],
                   [wa_r] + rh, [par])
                act(ab[:, 15:15 + W], pa[:, 0:W], AF.Copy, [par], [ar])
                pg, pgr = psum()
                mm(pg[:, 0:W], [(wg_v[:, k, g * 128:(g + 1) * 128], hT[:, k, c0:c0 + W]) for k in range(8)],
                   [wg_r] + rh, [pgr])
                sg_, sgr = sga[it % 4]
                act(sg_[:, 0:W], pg[:, 0:W], AF.Silu, [pgr], [sgr])
                src, srcr = ab, ar
                sh = 1
                lo = 1
                ti = 0
                while sh < w:
                    dst, dstr = tA[(it + ti) % 3]
                    dve_tt(dst[:, lo:15 + W], src[:, lo:15 + W], src[:, lo - sh:15 + W - sh], ALU.add,
                           [srcr], [dstr])
                    src, srcr = dst, dstr
                    sh *= 2
                    lo += sh
                    ti += 1
                d_, dr = dT[it % 4]
                dve_stt(d_[:, 0:W], src[:, 15:15 + W], 1.0 / w, ab[:, 15:15 + W], ALU.mult, ALU.subtract,
                        [srcr, ar], [dr])
                if bi == 0:
                    dve_tt(fx[:, 0:15], src[:, 15:30], invc[:, g * 16:g * 16 + 15], ALU.mult, [srcr, r_invc], [r_fx])
                    dve_tt(d_[:, 0:15], fx[:, 0:15], ab[:, 15:30], ALU.subtract, [r_fx, ar], [dr])
                if bi < 3:
                    dve_copy(ab[:, 0:15], ab[:, W:W + 15], [ar], [ar])

            def pA_s1(it):
                bi, g = divmod(it, 4)
                c0, W, tl = BLOCKS[bi]
                sq = 0 if bi < 4 else 1
                sg_, sgr = sga[it % 4]
                d_, dr = dT[it % 4]
                py, pyr = psum()
                mm(py[:, 0:W], [(pool_wb[:, g, :], d_[:, 0:W])], [r_poolw, dr], [pyr])
                dve_stt(ycat[:, g, c0:c0 + W], py[:, 0:W], pscale[:, g:g + 1], sg_[:, 0:W], ALU.mult, ALU.mult,
                        [pyr, r_pscale, sgr], [r_yc_t[g][t] for t in tl])
                if bi in (3, 4) and g == 3:
                    pho, phr = psum()
                    transposes([(pho[0:15, gg * 128:(gg + 1) * 128], abuf[gg][0][:, W:W + 15], ident_f)
                                for gg in range(4)], None, [abuf[gg][1] for gg in range(4)] + [r_identf], [phr])
                    act(orow[0:15, :], pho[0:15, :], AF.Copy, [phr], [r_orow])
                    S.dma("sp", o_pool[l, sq], orow[0:15, :], [r_orow], [])

            pipeline([pA_s0, (lambda it: None), pA_s1], 20)
            close(WL["a"])
            close(WL["ga"])

            bB = Bump(arena, *R_MT)
            vnb_p0 = bB.p
            vnb, r_vnb = bB.alloc([128, 17, 512], BF16, "vnb")
            r_vnb_t = [Res(f"vnb{t}") for t in range(NTILE)]
            Mem.claim(vnb_p0, bB.p, r_vnb_t)
            vt = [bB.alloc([128, 512], F32, f"vt{i}") for i in range(2)]
            ssb = [bB.alloc([128, 512], F32, f"ssb{i}") for i in range(2)]
            sgb = [bB.alloc([128, 512], F32, f"sgb{i}") for i in range(2)]
            tB = [bB.alloc([128, 512], F32, f"tB{i}") for i in range(2)]
            bBs = Bump(arena, *R_SCR)
            stB = [bBs.alloc([128, 16], F32, f"stB{i}") for i in range(3)]
            wv_v, wv_r = need(WL["vb"])
            ctxB = {}

            def pB_s0(t):
                c0, R = TILES[t]
                pvb, pvr = psum(hold=True)
                ctxB[t] = (pvb, pvr)
                mm(pvb[0:R, :], [(hT[:, k, c0:c0 + R], wv_v[:, k, :]) for k in range(8)], [wv_r, r_hT_t[t]], [pvr])
                st_, str_ = stB[t % 3]
                S.op("dve", lambda v, o=st_[0:R, 0:6], i=pvb[0:R, :]: v.bn_stats(out=o, in_=i), [pvr], [str_])
                S.op("dve", lambda v, o=st_[0:R, 6:8], i=st_[0:R, 0:6]: v.bn_aggr(out=o, in_=i), [str_], [str_])
                rstd_from_ms(st_[0:R, 7:8], st_[0:R, 8:9], str_, str_)

            def pB_s1(t):
                c0, R = TILES[t]
                pvb, pvr = ctxB.pop(t)
                st_, str_ = stB[t % 3]
                v_, vr_ = vt[t % 2]
                act(st_[0:R, 9:10], st_[0:R, 6:7], AF.Identity, [str_], [str_], scale=st_[0:R, 8:9])
                act(st_[0:R, 9:10], st_[0:R, 9:10], AF.Identity, [str_], [str_], scale=-1.0)
                act(v_[0:R, :], pvb[0:R, :], AF.Identity, [pvr, str_], [vr_], scale=st_[0:R, 8:9], bias=st_[0:R, 9:10])
                punhold(pvr)
                if t == 16:
                    dve_tt(v_[0:R, :], v_[0:R, :], sgug_bc[0:R, :], ALU.mult, [vr_, r_sgug], [vr_])
                    S.dma("sp", o_sgu[l], v_[0:R, :], [vr_], [])
                    act(vnb[0:R, t, :], v_[0:R, :], AF.Copy, [vr_], [r_vnb_t[t]])
                else:
                    S.op("pool", lambda g_, o=vnb[0:R, t, :], a=v_[0:R, :], b=sgug_bc[0:R, :]:
                         g_.tensor_tensor(out=o, in0=a, in1=b, op=ALU.mult), [vr_, r_sgug], [r_vnb_t[t]])

            pipeline([pB_s0, pB_s1], NTILE)
            close(WL["vb"])
            wu_v, wu_r = need(WL["u"])
            wgb_v, wgb_r = need(WL["gb"])
            step = 0
            for bi, (c0, W, tl) in enumerate(BLOCKS):
                rh = [r_hT_t[t] for t in tl]
                for g in range(4):
                    ps_, psr = psum()
                    groups = []
                    for i, t in enumerate(tl):
                        R = TILES[t][1]
                        groups.append((ps_[:, i * 128:i * 128 + R],
                                       [(vnb[0:R, t, g * 128:(g + 1) * 128], WT[0:R, g, 0:R]),
                                        (ones_b[0:1, :], sgub_b[0:1, g * 128:g * 128 + R])]))
                    mm_groups(groups, [r_vnb_t[t] for t in tl] + [r_WT, r_ones, r_sgub], [psr])
                    pu, pur = psum()
                    mm(pu[:, 0:W], [(wu_v[:, k, g * 128:(g + 1) * 128], hT[:, k, c0:c0 + W]) for k in range(8)],
                       [wu_r] + rh, [pur])
                    pg, pgr = psum()
                    mm(pg[:, 0:W], [(wgb_v[:, k, g * 128:(g + 1) * 128], hT[:, k, c0:c0 + W]) for k in range(8)],
                       [wgb_r] + rh, [pgr])
                    s_, sr_ = ssb[step % 2]
                    g_, gr_ = sgb[step % 2]
                    t_, tr_ = tB[step % 2]
                    act(s_[:, 0:W], ps_[:, 0:W], AF.Copy, [psr], [sr_])
                    act(g_[:, 0:W], pg[:, 0:W], AF.Silu, [pgr], [gr_])
                    dve_tt(t_[:, 0:W], pu[:, 0:W], s_[:, 0:W], ALU.mult, [pur, sr_], [tr_])
                    dve_tt(ycat[:, 4 + g, c0:c0 + W], t_[:, 0:W], g_[:, 0:W], ALU.mult, [tr_, gr_],
                           [r_yc_t[4 + g][t] for t in tl])
                    step += 1
            close(WL["u"])
            close(WL["gb"])

            wzl_v, wzl_r = need(WL["zl"])
            wq_v, wq_r = need(WL["q"])
            wk_v, wk_r = need(WL["k"])
            wvc_v, wvc_r = need(WL["vc"])
            bCs = Bump(arena, *R_MT)
            lsb = [bCs.alloc([128, 512], F32, f"lsb{i}") for i in range(1)]
            lhb = [bCs.alloc([128, 512], BF16, f"lhb{i}") for i in range(2)]
            llb = [bCs.alloc([128, 512], BF16, f"llb{i}") for i in range(2)]
            eq = [bCs.alloc([128, 4, 128], F32, f"eq{i}") for i in range(2)]
            ek = [bCs.alloc([128, 4, 128], F32, f"ek{i}") for i in range(2)]
            qtl = [bCs.alloc([128, 4, 128], BF16, f"qtl{i}") for i in range(3)]
            ktl = [bCs.alloc([128, 4, 128], BF16, f"ktl{i}") for i in range(2)]
            ktok = [bCs.alloc([128, 512], BF16, f"ktok{i}") for i in range(2)]
            attm = [bCs.alloc([128, 4, 128], BF16, f"attm{i}") for i in range(2)]
            vb = [bCs.alloc([128, 1024], BF16, f"vb{i}") for i in range(2)]
            ycb = [bCs.alloc([128, 1024], BF16, f"ycb{i}") for i in range(2)]
            bCt = Bump(arena, *R_SCR)
            zhb = [bCt.alloc([17, 128], BF16, f"zhb{i}") for i in range(2)]
            zlb = [bCt.alloc([17, 128], BF16, f"zlb{i}") for i in range(2)]
            ebt = [bCt.alloc([128, 4], F32, f"ebt{i}") for i in range(5)]
            stC = [bCt.alloc([128, 16], F32, f"stC{i}") for i in range(2)]
            junkc, r_junkc = bCt.alloc([128, 256], BF16, "junkc")
            Sf, r_Sf = bCt.alloc([128, 4, 256], F32, "Sf")
            Sbb = [bCt.alloc([128, 4, 256], BF16, f"Sb{i}") for i in range(2)]
            QS = 128.0 ** -0.5
            for zz, zr in zhb:
                S.op("dve", lambda v, o=zz: v.memset(o, 1.0), [], [zr])
            for zz, zr in zlb:
                S.op("dve", lambda v, o=zz: v.memset(o, 0.0), [], [zr])
            S.op("dve", lambda v: v.memset(Sf, 0.0), [], [r_Sf])
            S.op("dve", lambda v, o=Sbb[1][0]: v.memset(o, 0.0), [], [Sbb[1][1]])
            cC = {}

            def rhC(t):
                return [r_hT_t[t]]

            def pC_s0(t):
                c0, R = TILES[t]
                pz, pzr = psum()
                mm(pz[0:16, 0:R], [(wzl_v[:, k, :], hT[:, k, c0:c0 + R]) for k in range(8)], [wzl_r] + rhC(t), [pzr])
                zh_, zhr_ = zhb[t % 2]
                zl_, zlr_ = zlb[t % 2]
                dve_copy(zh_[0:16, 0:R], pz[0:16, 0:R], [pzr], [zhr_])
                dve_tt(zl_[0:16, 0:R], pz[0:16, 0:R], zh_[0:16, 0:R], ALU.subtract, [pzr, zhr_], [zlr_])

            def pC_s1(t):
                c0, R = TILES[t]
                zh_, zhr_ = zhb[t % 2]
                zl_, zlr_ = zlb[t % 2]
                pp, ppr = psum()
                mm(pp[0:R, :], [(zh_[0:17, 0:R], wa_hi[0:17, :]), (zh_[0:17, 0:R], wa_lo[0:17, :]),
                                (zl_[0:17, 0:R], wa_hi[0:17, :])], [zhr_, zlr_, r_wahi, r_walo], [ppr])
                l_, lr_ = lsb[0]
                lh_, lhr_ = lhb[t % 2]
                ll_, llr_ = llb[t % 2]
                act(l_[0:R, :], pp[0:R, :], AF.Exp, [ppr], [lr_], scale=-1.0)
                act(l_[0:R, :], l_[0:R, :], AF.Ln, [lr_, r_onec], [lr_], bias=onec[0:R, :], scale=1.0)
                dve_copy(lh_[0:R, :], l_[0:R, :], [lr_], [lhr_])
                dve_tt(ll_[0:R, :], l_[0:R, :], lh_[0:R, :], ALU.subtract, [lr_, lhr_], [llr_])

            def pC_s2(t):
                c0, R = TILES[t]
                lh_, lhr_ = lhb[t % 2]
                ll_, llr_ = llb[t % 2]
                pbc, pbcr = psum()
                pbc3 = pbc[:, :].rearrange("p (h i) -> p h i", i=128)
                mm_groups([(pbc3[:, h, 0:R], [(lh_[0:R, h * 128:(h + 1) * 128], ucum_b[0:R, 0:R]),
                                              (ll_[0:R, h * 128:(h + 1) * 128], ucum_b[0:R, 0:R])]) for h in range(4)],
                          [lhr_, llr_, r_ucumb], [pbcr])
                eq_, eqr = eq[t % 2]
                ek_, ekr = ek[t % 2]
                eb_, ebr = ebt[t % 5]
                act(eq_[:, :, 0:R], pbc3[:, :, 0:R], AF.Exp, [pbcr], [eqr])
                act(ek_[:, :, 0:R], pbc3[:, :, 0:R], AF.Exp, [pbcr], [ekr], scale=-1.0)
                act(eb_[:, :], pbc3[:, :, R - 1], AF.Exp, [pbcr], [ebr])

            def pC_s3(t):
                c0, R = TILES[t]
                eq_, eqr = eq[t % 2]
                ek_, ekr = ek[t % 2]
                pq, pqr = psum()
                pq3 = pq[:, :].rearrange("p (h i) -> p h i", i=128)
                mm_groups([(pq3[:, h, 0:R], [(wq_v[:, k, h * 128:(h + 1) * 128], hT[:, k, c0:c0 + R]) for k in range(8)])
                           for h in range(4)], [wq_r] + rhC(t), [pqr])
                pk, pkr = psum()
                pk3 = pk[:, :].rearrange("p (h i) -> p h i", i=128)
                mm_groups([(pk3[:, h, 0:R], [(wk_v[:, k, h * 128:(h + 1) * 128], hT[:, k, c0:c0 + R]) for k in range(8)])
                           for h in range(4)], [wk_r] + rhC(t), [pkr])
                q_, qr_ = qtl[t % 3]
                k_, kr_ = ktl[t % 2]
                dve_stt(q_[:, :, 0:R], pq3[:, :, 0:R], QS, eq_[:, :, 0:R], ALU.mult, ALU.mult, [pqr, eqr], [qr_])
                dve_tt(k_[:, :, 0:R], pk3[:, :, 0:R], ek_[:, :, 0:R], ALU.mult, [pkr, ekr], [kr_])

            def pC_s4(t):
                c0, R = TILES[t]
                q_, qr_ = qtl[t % 3]
                k_, kr_ = ktl[t % 2]
                pkt, pktr = psum()
                pkt3 = pbf(pkt)[:, 0:512].rearrange("p (h d) -> p h d", d=128)
                transposes([(pkt3[0:R, h, :], k_[:, h, 0:R], ident_b) for h in range(4)], None, [kr_, r_identb], [pktr])
                kt_, ktr_ = ktok[t % 2]
                act(kt_[0:R, :], pbf(pkt)[0:R, 0:512], AF.Copy, [pktr], [ktr_])
                pat, patr = psum()
                pat3 = pat[:, :].rearrange("p (h i) -> p h i", i=128)
                mm_groups([(pat3[0:R, h, 0:R], [(k_[:, h, 0:R], q_[:, h, 0:R])]) for h in range(4)], [kr_, qr_], [patr])
                at_, atr_ = attm[t % 2]
                dve_tt(at_[0:R, :, 0:R], pat3[0:R, :, 0:R], masku[0:R, 0:R].unsqueeze(1).broadcast_to([R, 4, R]),
                       ALU.mult, [patr, r_masku], [atr_])
                v_, vr_ = vb[t % 2]
                for hf in range(2):
                    pv_, pvr_ = psum()
                    mm(pv_[0:R, :], [(hT[:, k, c0:c0 + R], wvc_v[:, k, hf * 512:(hf + 1) * 512]) for k in range(8)],
                       [wvc_r] + rhC(t), [pvr_])
                    act(v_[0:R, hf * 512:(hf + 1) * 512], pv_[0:R, :], AF.Copy, [pvr_], [vr_])

            def pC_s5(t):
                c0, R = TILES[t]
                sq = 0 if t < 16 else 1
                eb_, ebr = ebt[t % 5]
                q_, qr_ = qtl[t % 3]
                kt_, ktr_ = ktok[t % 2]
                at_, atr_ = attm[t % 2]
                v_, vr_ = vb[t % 2]
                Sprev, r_Sprev = Sbb[(t + 1) % 2]
                Snew, r_Snew = Sbb[t % 2]
                if t == 16:
                    S.dma("sp", Sf, sgla[l].rearrange("h d v -> d h v"), [], [r_Sf])
                    act(Sprev, Sf, AF.Copy, [r_Sf], [r_Sprev])
                st_, str_ = stC[t % 2]
                yc_, ycr_ = ycb[t % 2]
                pos = []
                for hf in range(2):
                    po, por = psum(acc=True)
                    mm_groups([(po[0:R, (h % 2) * 256:(h % 2 + 1) * 256],
                                [(at_[0:R, h, 0:R], v_[0:R, h * 256:(h + 1) * 256]),
                                 (q_[:, h, 0:R], Sprev[:, h, :])]) for h in (2 * hf, 2 * hf + 1)],
                              [atr_, vr_, qr_, r_Sprev], [por])
                    pos.append((po, por))
                for hf in range(2):
                    pkv, pkvr = psum()
                    mm_groups([(pkv[:, (h % 2) * 256:(h % 2 + 1) * 256],
                                [(kt_[0:R, h * 128:(h + 1) * 128], v_[0:R, h * 256:(h + 1) * 256])])
                               for h in (2 * hf, 2 * hf + 1)], [ktr_, vr_], [pkvr])
                    for h in (2 * hf, 2 * hf + 1):
                        dve_ts(Sf[:, h, :], Sf[:, h, :], eb_[:, h:h + 1], None, ALU.mult, ALU.bypass, [r_Sf, ebr], [r_Sf])
                        dve_stt(Sf[:, h, :], pkv[:, (h % 2) * 256:(h % 2 + 1) * 256], eb_[:, h:h + 1], Sf[:, h, :],
                                ALU.mult, ALU.add, [pkvr, ebr, r_Sf], [r_Sf])
                if t in (15, 16):
                    S.dma("sp", o_gla[l, sq].rearrange("h d v -> d h v"), Sf, [r_Sf], [])
                if t < 15:
                    S.op("pool", lambda g_, o=Snew, i=Sf: g_.tensor_copy(out=o, in_=i), [r_Sf], [r_Snew])
                for h in range(4):
                    po, por = pos[h // 2]
                    act(junkc[0:R, :], po[0:R, (h % 2) * 256:(h % 2 + 1) * 256], AF.Square, [por], [r_junkc, str_],
                        accum_out=st_[0:R, h:h + 1])
                rstd_from_ms(st_[0:R, 0:4], st_[0:R, 8:12], str_, str_, scale=1.0 / 256)
                for h in range(4):
                    po, por = pos[h // 2]
                    dve_stt(yc_[0:R, h * 256:(h + 1) * 256], po[0:R, (h % 2) * 256:(h % 2 + 1) * 256],
                            st_[0:R, 8 + h:9 + h], glag_bc[0:R, :], ALU.mult, ALU.mult, [por, str_, r_glag], [ycr_])

            def pC_s6(t):
                c0, R = TILES[t]
                yc_, ycr_ = ycb[t % 2]
                pyt, pytr = psum()
                pyt3 = pbf(pyt).rearrange("p (c t) -> p c t", t=128)
                transposes([(pyt3[:, c, 0:R], yc_[0:R, c * 128:(c + 1) * 128], ident_b[0:R, 0:R]) for c in range(8)],
                           None, [ycr_, r_identb], [pytr])
                act(ycat[:, 8:16, c0:c0 + R], pyt3[:, :, 0:R], AF.Copy, [pytr], [r_yc_t[8 + c][t] for c in range(8)])

            def c2_item(hf, bi, cc):
                def emit():
                    c0, W, tl = BLOCKS[bi]
                    c = hf * 4 + cc
                    wg_v, wg_r = need(WL["gc"][hf])
                    pg, pgr = psum()
                    mm(pg[:, 0:W], [(wg_v[:, k, cc * 128:(cc + 1) * 128], hT[:, k, c0:c0 + W]) for k in range(8)],
                       [wg_r] + [r_hT_t[t] for t in tl], [pgr])
                    act(pg[:, 0:W], pg[:, 0:W], AF.Silu, [pgr], [pgr])
                    rr = [r_yc_t[8 + c][t] for t in tl]
                    dve_tt(ycat[:, 8 + c, c0:c0 + W], ycat[:, 8 + c, c0:c0 + W], pg[:, 0:W], ALU.mult, rr + [pgr], rr)
                return emit
            c2_items = [[c2_item(hf, bi, cc) for bi in range(5) for cc in range(4)] for hf in range(2)]

            def c1_filler():
                for _ in range(2):
                    if len(c2_items[0]) > 8:
                        c2_items[0].pop(0)()

            nrot[0] = 6
            pipeline([pC_s0, pC_s1, pC_s2, pC_s3, pC_s4, pC_s5, pC_s6], NTILE, filler=c1_filler,
                     order=[4, 3, 2, 1, 0, 6, 5])
            nrot[0] = 8
            for nm in ("zl", "q", "k", "vc"):
                close(WL[nm])
            for hf in range(2):
                while c2_items[hf]:
                    c2_items[hf].pop(0)()
                close(WL["gc"][hf])
                if hf == 0 and l + 1 < DEPTH:
                    small_params(l + 1, (R_MT[0] + 24576, R_MT[1]))
                    ada(l + 1, (R_MT[0], R_MT[0] + 24576))
            if DEBUG and l == 0:
                S.barrier()
                S.dma("pool", dbg["hT"], hT.rearrange("p k t -> p (k t)"), [], [])
                S.dma("pool", dbg["ycat"], ycat.rearrange("p k t -> p (k t)"), [], [])
                S.barrier()

            Mem.claim(R_MT[0], R_MT[1], r_mT_t)
            bM = Bump(arena, *R_SCR)
            gsg = [bM.alloc([128, 512], F32, f"gsg{i}") for i in range(3)]
            tM = [bM.alloc([128, 512], F32, f"tM{i}") for i in range(2)]
            KOFF = (0, 4, 8)
            KN = (4, 4, 8)
            for dp in range(4):
                wo_, wor_ = need(WL["wo"][dp])
                wg_, wgr_ = need(WL["wgm"][dp])
                for dd in range(2):
                    dm = dp * 2 + dd
                    ds = slice(dd * 128, (dd + 1) * 128)
                    for bi, (c0, W, tl) in enumerate(BLOCKS):
                        rh = [r_hT_t[t] for t in tl]
                        for i in range(3):
                            pg, pgr = psum()
                            mm(pg[:, 0:W], [(wg_[:, k, i, ds], hT[:, k, c0:c0 + W]) for k in range(8)], [wgr_] + rh, [pgr])
                            act(gsg[i][0][:, 0:W], pg[:, 0:W], AF.Sigmoid, [pgr], [gsg[i][1]])
                        for i in range(3):
                            pp_, ppr_ = psum()
                            ry = [r_yc_t[KOFF[i] + kk][t] for kk in range(KN[i]) for t in tl]
                            mm(pp_[:, 0:W], [(wo_[:, KOFF[i] + kk, ds], ycat[:, KOFF[i] + kk, c0:c0 + W])
                                             for kk in range(KN[i])], [wor_] + ry, [ppr_])
                            if i == 0:
                                dve_tt(tM[0][0][:, 0:W], pp_[:, 0:W], gsg[0][0][:, 0:W], ALU.mult,
                                       [ppr_, gsg[0][1]], [tM[0][1]])
                            elif i == 1:
                                dve_tt(tM[1][0][:, 0:W], pp_[:, 0:W], gsg[1][0][:, 0:W], ALU.mult,
                                       [ppr_, gsg[1][1]], [tM[1][1]])
                                dve_tt(tM[0][0][:, 0:W], tM[0][0][:, 0:W], tM[1][0][:, 0:W], ALU.add,
                                       [tM[0][1], tM[1][1]], [tM[0][1]])
                            else:
                                dve_tt(tM[1][0][:, 0:W], pp_[:, 0:W], gsg[2][0][:, 0:W], ALU.mult,
                                       [ppr_, gsg[2][1]], [tM[1][1]])
                                dve_tt(mT[:, dm, c0:c0 + W], tM[0][0][:, 0:W], tM[1][0][:, 0:W], ALU.add,
                                       [tM[0][1], tM[1][1]], [r_mT_t[t] for t in tl])
                close(WL["wo"][dp])
                close(WL["wgm"][dp])
            if DEBUG and l == 0:
                S.dma("pool", dbg["mT"], mT.rearrange("p k t -> p (k t)"), [], [])
                S.barrier()

            fuse = l + 1 < DEPTH
            if fuse:
                Mem.claim(R_HT[0], R_HT[1], r_hT_t)
            bO = Bump(arena, *R_YC)
            bOs = Bump(arena, *R_SCR)
            wout_v, wout_r = need(WL["wout"])
            G_bc, r_Gbc = bO.alloc([128, 1024], F32, "G_bc")
            idg, r_idg = bO.alloc([128, 8, 128], F32, "idg")
            NXO = 9 if fuse else 6
            xo = [bO.alloc([128, 1024], F32, f"xo{i}") for i in range(NXO)]
            tO = [bO.alloc([128, 1024], F32, f"tO{i}") for i in range(2)]
            stO = [bOs.alloc([128, 8], F32, f"stO{i}") for i in range(5)]
            junko, r_junko = bOs.alloc([128, 512], BF16, "junko")
            cO = {}

            def pO_ld(t):
                c0, R = TILES[t]
                xa, xr = xo[t % NXO]
                S.dma("sp", xa[0:R, :], x_src[c0:c0 + R, :], [r_x1s[t]], [xr])

            def pO_nop(t):
                pass

            def pO_mm(t):
                c0, R = TILES[t]
                sa, sr = stO[t % 5]
                pos = []
                for hf in range(2):
                    po, por = psum(hold=True)
                    mm(po[0:R, :], [(mT[:, k, c0:c0 + R], wout_v[:, k, hf * 512:(hf + 1) * 512]) for k in range(8)],
                       [r_mT_t[t], wout_r], [por])
                    act(junko[0:R, :], po[0:R, :], AF.Square, [por], [r_junko, sr], accum_out=sa[0:R, hf:hf + 1])
                    pos.append((po, por))
                cO[t] = pos

            def pO_sc(t):
                c0, R = TILES[t]
                sa, sr = stO[t % 5]
                act(sa[0:R, 2:3], sa[0:R, 1:2], AF.Identity, [sr, r_epsc], [sr], scale=1.0 / D, bias=epsc[0:R, :])
                act(sa[0:R, 4:5], sa[0:R, 0:1], AF.Ln, [sr], [sr], scale=1.0 / D, bias=sa[0:R, 2:3])
                act(sa[0:R, 4:5], sa[0:R, 4:5], AF.Exp, [sr], [sr], scale=-0.5)

            def pO_gt(t):
                c0, R = TILES[t]
                if t in (0, 16):
                    sq = 0 if t < 16 else 1
                    for k in range(8):
                        dve_ts(idg[:, k, :], ident_f, modcL[:, 16 + k, sq:sq + 1], None, ALU.mult, ALU.bypass,
                               [r_identf, r_modcL], [r_idg])
                    for hf in range(2):
                        pb, pr = psum()
                        mm_groups([(pb[:, kk * 128:(kk + 1) * 128], [(ones_f, idg[:, hf * 4 + kk, :])]) for kk in range(4)],
                                  [r_onesf, r_idg], [pr])
                        act(G_bc[:, hf * 512:(hf + 1) * 512], pb[:, :], AF.Copy, [pr], [r_Gbc])
                ta, tr = tO[t % 2]
                sa, sr = stO[t % 5]
                pos = cO.pop(t)
                for hf in range(2):
                    po, por = pos[hf]
                    hs = slice(hf * 512, (hf + 1) * 512)
                    dve_stt(ta[0:R, hs], po[0:R, :], sa[0:R, 4:5], G_bc[0:R, hs], ALU.mult, ALU.mult,
                            [por, sr, r_Gbc], [tr])
                    punhold(por)

            def pO_add(t):
                c0, R = TILES[t]
                xa, xr = xo[t % NXO]
                ta, tr = tO[t % 2]
                S.op("pool", lambda g, o=xa[0:R, :], a=xa[0:R, :], b=ta[0:R, :]:
                     g.tensor_tensor(out=o, in0=a, in1=b, op=ALU.add), [xr, tr], [xr])
                S.dma("sp", x_dst[c0:c0 + R, :], xa[0:R, :], [xr], [r_x1s[t]])

            stages = [pO_ld, pO_nop, pO_mm, pO_sc, pO_gt, pO_add]
            if fuse:
                stages += p0_stages(l + 1, bO, lambda t: xo[t % NXO], bOs)
            pipeline(stages, NTILE)
            close(WL["wout"])

        S.final_wait("sp")

        keys = S.sem_keys()
        sems = {}
        for kx in keys:
            sems[kx] = es.enter_context(nc.semaphore("s_" + "_".join(str(p) for p in kx)))
        with nc.Block() as block:
            @block.sync
            def _(e):
                S.emit("sp", e, sems)

            @block.gpsimd
            def _(e):
                S.emit("pool", e, sems)

            @block.scalar
            def _(e):
                S.emit("act", e, sems)

            @block.vector
            def _(e):
                S.emit("dve", e, sems)

            @block.tensor
            def _(e):
                S.emit("pe", e, sems)
    return nc


def _consts():
    j = np.arange(128)[:, None]
    i = np.arange(128)[None, :]
    masku = (j <= i).astype(np.float32)
    ucum = (masku * (-1.0 / 16.0)).astype(np.float32)
    invc = np.ones((128, 64), np.float32)
    for g, w in enumerate(POOL_W):
        for t in range(16):
            invc[:, g * 16 + t] = 1.0 / min(w, t + 1)
    return {"k_ident": np.eye(128, dtype=np.float32), "k_masku": masku, "k_ucum": ucum, "k_invc": invc}


_NC_CACHE = {}


def kernel(x_prompt, x_sample, state_pool, state_gla, c_prompt, c_sample,
           ada_w, ada_b, pre_norm_g, post_norm_g, w_in, pool_w, pool_scale,
           sgu_norm_g, sgu_w, sgu_b, gla_wa2, gla_ba, gla_norm_g,
           w_oa, w_ob, w_oc, w_out):
    f = lambda a: np.ascontiguousarray(np.asarray(a, dtype=np.float32))
    x_prompt, x_sample, state_pool, state_gla, c_prompt, c_sample = map(
        f, (x_prompt, x_sample, state_pool, state_gla, c_prompt, c_sample))
    shared = {
        "ada_w": f(ada_w), "ada_b": f(ada_b), "pre_g": f(pre_norm_g), "post_g": f(post_norm_g), "w_in": f(w_in),
        "pool_w": f(pool_w), "pool_scale": f(pool_scale), "sgu_g": f(sgu_norm_g), "sgu_w": f(sgu_w),
        "sgu_b": f(sgu_b).reshape(DEPTH, 512), "gla_wa2": f(gla_wa2), "gla_ba": f(gla_ba), "gla_g": f(gla_norm_g),
        "w_oa": f(w_oa), "w_ob": f(w_ob), "w_oc": f(w_oc), "w_out": f(w_out),
    }
    shared.update(_consts())
    n = 8
    in_maps = []
    for b in range(n):
        m = dict(shared)
        m["xin"] = np.ascontiguousarray(np.concatenate([x_prompt[b], x_sample[b]], axis=0))
        m["c2"] = np.ascontiguousarray(np.stack([c_prompt[b], c_sample[b]], axis=0))
        m["spool"] = np.ascontiguousarray(state_pool[:, b])
        m["sgla"] = np.ascontiguousarray(state_gla[:, b])
        in_maps.append(m)
    if "nc" not in _NC_CACHE:
        _NC_CACHE["nc"] = build_program()
    nc = _NC_CACHE["nc"]
    res = run_bass_kernel_spmd(nc, in_maps, core_ids=list(range(n)))
    rs = res.results
    if DEBUG:
        kernel.dbg = rs
    y_prompt = np.stack([rs[b]["y"][:TP] for b in range(n)], axis=0)
    y_sample = np.stack([rs[b]["y"][TP:] for b in range(n)], axis=0)
    pool_p = np.stack([rs[b]["o_pool"][:, 0] for b in range(n)], axis=1)
    pool_s = np.stack([rs[b]["o_pool"][:, 1] for b in range(n)], axis=1)
    gla_p = np.stack([rs[b]["o_gla"][:, 0] for b in range(n)], axis=1)
    gla_s = np.stack([rs[b]["o_gla"][:, 1] for b in range(n)], axis=1)
    sgu_s = np.stack([rs[b]["o_sgu"] for b in range(n)], axis=1)
    return (y_prompt.astype(np.float32), y_sample.astype(np.float32), pool_p.astype(np.float32),
            gla_p.astype(np.float32), pool_s.astype(np.float32), gla_s.astype(np.float32),
            sgu_s.astype(np.float32))
```

```python
import numpy as np
from contextlib import ExitStack
import concourse.bass as bass
import concourse.mybir as mybir
from concourse.bass_utils import run_bass_kernel_spmd

F32 = mybir.dt.float32
BF16 = mybir.dt.bfloat16
AF = mybir.ActivationFunctionType
ALU = mybir.AluOpType

D = 1024
TP = 2048
TS = 64
NT = TP + TS
DEPTH = 2
D_IN = 8720
EPS = 1e-6
C_A, C_GA, C_U, C_VB, C_GB, C_Q, C_K, C_VC, C_GC, C_LR, C_GM = (
    0, 512, 1024, 1536, 2048, 2560, 3072, 3584, 4608, 5632, 5648)
POOL_W = (2, 4, 8, 16)

DEBUG = False


class Res:
    __slots__ = ("w", "r", "name")

    def __init__(self, name=""):
        self.w = None
        self.r = []
        self.name = name


class Sched:
    ENGS = ("pe", "act", "dve", "pool", "sp")
    NDMA = 8

    def __init__(self):
        self.ops = {e: [] for e in self.ENGS}
        self.count = {e: 0 for e in self.ENGS}
        self.known = {e: {} for e in self.ENGS}
        self.dma_n = {e: 0 for e in self.ENGS}
        self.dma_tokens = []

    def _waits(self, eng, reads, writes, extra=()):
        toks = list(extra)
        for r in reads:
            if r.w is not None:
                toks.append(r.w)
        for w in writes:
            if w.w is not None:
                toks.append(w.w)
            toks.extend(w.r)
        kn = self.known[eng]
        waits = {}
        for (sem, val) in toks:
            if sem == ("e", eng) and eng == "pe":
                continue
            if kn.get(sem, 0) >= val:
                continue
            if waits.get(sem, 0) < val:
                waits[sem] = val
        for sem, val in waits.items():
            kn[sem] = val
        return list(waits.items())

    def _commit(self, tok, reads, writes):
        for r in reads:
            r.r.append(tok)
        for w in writes:
            w.w = tok
            w.r = []

    def op(self, eng, fn, reads=(), writes=()):
        waits = self._waits(eng, reads, writes)
        self.count[eng] += 1
        tok = (("e", eng), self.count[eng])
        self.ops[eng].append((waits, fn, tok, 1))
        self._commit(tok, reads, writes)
        return tok

    def dma(self, eng, out, in_, reads=(), writes=(), **kw):
        n = self.dma_n[eng]
        self.dma_n[eng] += 1
        sem = ("d", eng, n % self.NDMA)
        val = 16 * (n // self.NDMA + 1)
        extra = [(sem, val - 16)] if val > 16 else []
        waits = self._waits(eng, reads, writes, extra)
        tok = (sem, val)

        def fn(e, out=out, in_=in_, kw=kw):
            return e.dma_start(out=out, in_=in_, **kw)
        self.ops[eng].append((waits, fn, tok, 16))
        self._commit(tok, reads, writes)
        self.dma_tokens.append(tok)
        return tok

    def barrier(self):
        toks = [(("e", e), self.count[e]) for e in self.ENGS if self.count[e] > 0]
        last = {}
        for (sem, val) in self.dma_tokens:
            if last.get(sem, 0) < val:
                last[sem] = val
        toks += list(last.items())
        for e in self.ENGS:
            kn = self.known[e]
            waits = []
            for (sem, val) in toks:
                if sem == ("e", e) and e == "pe":
                    continue
                if kn.get(sem, 0) < val:
                    kn[sem] = val
                    waits.append((sem, val))
            if waits:
                self.ops[e].append((waits, None, None, 0))

    def final_wait(self, eng="sp"):
        last = {}
        for (sem, val) in self.dma_tokens:
            if last.get(sem, 0) < val:
                last[sem] = val
        self.ops[eng].append((list(last.items()), None, None, 0))

    def sem_keys(self):
        keys = [("e", e) for e in self.ENGS]
        for e in self.ENGS:
            if self.dma_n[e] > 0:
                keys += [("d", e, i) for i in range(min(self.NDMA, self.dma_n[e]))]
        return keys

    def emit(self, eng, engine, sems):
        for (waits, fn, tok, inc) in self.ops[eng]:
            for (sem, val) in waits:
                engine.wait_ge(sems[sem], val)
            if fn is not None:
                ins = fn(engine)
                ins.then_inc(sems[tok[0]], inc)


class Mem:
    live = []

    @classmethod
    def claim(cls, start, end, res_list):
        seed = {}
        keep = []
        for (s0, e0, rl) in cls.live:
            if s0 < end and start < e0:
                for r in rl:
                    toks = list(r.r)
                    if r.w is not None:
                        toks.append(r.w)
                    for (sem, val) in toks:
                        if seed.get(sem, 0) < val:
                            seed[sem] = val
                if not (start <= s0 and e0 <= end):
                    keep.append((s0, e0, rl))
            else:
                keep.append((s0, e0, rl))
        keep.append((start, end, res_list))
        cls.live = keep
        for r in res_list:
            r.w = None
            r.r = list(seed.items())


class Bump:
    def __init__(self, arena, start, end):
        self.arena = arena
        self.p = start
        self.end = end

    def alloc(self, shape, dt, name=""):
        esz = 4 if dt == F32 else 2
        n = 1
        for s in shape[1:]:
            n *= s
        nbytes = (n * esz + 31) // 32 * 32
        assert self.p + nbytes <= self.end, f"arena overflow for {name}: {self.p}+{nbytes}>{self.end}"
        o4 = self.p // 4
        v = self.arena[:, o4:o4 + nbytes // 4]
        if dt != F32:
            v = v.bitcast(dt)
        v = v[:, 0:n]
        if len(shape) == 3:
            v = v.rearrange("p (a b) -> p a b", b=shape[2])
        elif len(shape) == 4:
            v = v.rearrange("p (a b c) -> p a b c", b=shape[2], c=shape[3])
        if shape[0] < 128:
            v = v[0:shape[0]]
        res = Res(name)
        Mem.claim(self.p, self.p + nbytes, [res])
        self.p += nbytes
        return v, res


def build_program():
    nc = bass.Bass("TRN2", target_bir_lowering=False)
    S = Sched()
    Mem.live = []

    def din(name, shape):
        return nc.dram_tensor(name, list(shape), F32, kind="ExternalInput").ap()

    def dout(name, shape):
        return nc.dram_tensor(name, list(shape), F32, kind="ExternalOutput").ap()

    xin = din("xin", [NT, D])
    c2 = din("c2", [2, D])
    spool = din("spool", [DEPTH, 15, 512])
    sgla = din("sgla", [DEPTH, 4, 128, 256])
    ada_w = din("ada_w", [DEPTH, D, 3 * D])
    ada_b = din("ada_b", [DEPTH, 3 * D])
    pre_g = din("pre_g", [DEPTH, D])
    post_g = din("post_g", [DEPTH, D])
    w_in = din("w_in", [DEPTH, D, D_IN])
    pool_w = din("pool_w", [DEPTH, 4, 128, 128])
    pool_scale = din("pool_scale", [DEPTH, 512])
    sgu_g = din("sgu_g", [DEPTH, 512])
    sgu_w = din("sgu_w", [DEPTH, 4, 128, 128])
    sgu_b = din("sgu_b", [DEPTH, 512])
    gla_wa2 = din("gla_wa2", [DEPTH, 16, 512])
    gla_ba = din("gla_ba", [DEPTH, 512])
    gla_g = din("gla_g", [DEPTH, 256])
    w_oa = din("w_oa", [DEPTH, 512, D])
    w_ob = din("w_ob", [DEPTH, 512, D])
    w_oc = din("w_oc", [DEPTH, 1024, D])
    w_out = din("w_out", [DEPTH, D, D])
    k_ident = din("k_ident", [128, 128])
    k_masku = din("k_masku", [128, 128])
    k_ucum = din("k_ucum", [128, 128])
    k_invc = din("k_invc", [128, 64])

    y = dout("y", [NT, D])
    o_pool = dout("o_pool", [DEPTH, 2, 15, 512])
    o_gla = dout("o_gla", [DEPTH, 2, 4, 128, 256])
    o_sgu = dout("o_sgu", [DEPTH, TS, 512])
    x1s = nc.dram_tensor("x1s", [NT, D], F32, kind="Internal").ap()
    dbg = {}
    if DEBUG:
        dbg["hT"] = dout("d_hT", [128, 8 * NT])
        dbg["ycat"] = dout("d_ycat", [128, 16 * NT])
        dbg["mT"] = dout("d_mT", [128, 8 * NT])

    ARENA_B = 210944
    with ExitStack() as es:
        arena = es.enter_context(nc.sbuf_tensor("arena", [128, ARENA_B // 4], F32))
        banks = [es.enter_context(nc.psum_tensor(f"bank{i}", [128, 512], F32)) for i in range(8)]
        bank_res = [Res(f"bank{i}") for i in range(8)]
        bank_i = [0]

        nrot = [8]
        acc_i = [0]

        held = set()

        def psum(acc=False, hold=False):
            if acc:
                i = 6 + acc_i[0] % 2
                acc_i[0] += 1
            else:
                for _ in range(nrot[0] + 1):
                    i = bank_i[0] % nrot[0]
                    bank_i[0] += 1
                    if i not in held:
                        break
                else:
                    raise RuntimeError("no free PSUM bank")
            if hold:
                held.add(i)
            return banks[i], bank_res[i]

        def punhold(res):
            held.discard(bank_res.index(res))

        def pbf(bank):
            return bank[:, :].bitcast(BF16)

        HT_B, YC_B, MT_B = 8 * NT * 2, 16 * NT * 2, 8 * NT * 2
        R_HT = (0, HT_B)
        R_YC = (HT_B, HT_B + YC_B)
        R_MT = (HT_B + YC_B, HT_B + YC_B + MT_B)
        R_CONST = (R_MT[1], R_MT[1] + 14336)
        R_WA = (R_CONST[1], R_CONST[1] + 51200)
        R_SCR = (R_WA[1], ARENA_B)
        assert R_SCR[1] - R_SCR[0] == 10240

        def resident(rg, k):
            o4 = rg[0] // 4
            return arena[:, o4:o4 + (rg[1] - rg[0]) // 4].bitcast(BF16).rearrange("p (k t) -> p k t", k=k)
        hT = resident(R_HT, 8)
        ycat = resident(R_YC, 16)
        mT = resident(R_MT, 8)
        NTILE = 17
        r_hT_t = [Res(f"hT{t}") for t in range(NTILE)]
        r_yc_t = [[Res(f"yc{c}_{t}") for t in range(NTILE)] for c in range(16)]
        r_mT_t = [Res(f"mT{t}") for t in range(NTILE)]

        bc = Bump(arena, *R_CONST)
        ident_f, r_identf = bc.alloc([128, 128], F32, "ident_f")
        ident_b, r_identb = bc.alloc([128, 128], BF16, "ident_b")
        masku, r_masku = bc.alloc([128, 128], F32, "masku")
        ucum, r_ucum = bc.alloc([128, 128], F32, "ucum")
        invc, r_invc = bc.alloc([128, 64], F32, "invc")
        ones_f, r_onesf = bc.alloc([128, 128], F32, "ones_f")
        ones_b, r_ones = bc.alloc([1, 128], BF16, "ones_b")
        scT, r_scT = bc.alloc([128, 8, 2], BF16, "scT")
        pool_wb, r_poolw = bc.alloc([128, 4, 128], BF16, "pool_wb")
        WT, r_WT = bc.alloc([128, 4, 128], BF16, "WT")
        pscale, r_pscale = bc.alloc([128, 4], F32, "pscale")
        sgug_bc, r_sgug = bc.alloc([128, 512], F32, "sgug_bc")
        glag_bc, r_glag = bc.alloc([128, 256], F32, "glag_bc")
        sgub_b, r_sgub = bc.alloc([1, 512], BF16, "sgub_b")
        wa_hi, r_wahi = bc.alloc([17, 512], BF16, "wa_hi")
        wa_lo, r_walo = bc.alloc([17, 512], BF16, "wa_lo")
        ucum_b, r_ucumb = bc.alloc([128, 128], BF16, "ucum_b")
        modcs = [bc.alloc([128, 24, 2], F32, f"modc{i}") for i in range(2)]
        epsc, r_epsc = bc.alloc([128, 1], F32, "epsc")
        onec, r_onec = bc.alloc([128, 1], F32, "onec")

        class Chunk:
            __slots__ = ("ap", "res", "closed", "iv")

            def __init__(self):
                self.ap = None
                self.res = None
                self.closed = False
                self.iv = None

        wa_p = [R_WA[0]]
        wa_chunks = []
        wa_pending = []

        def enqueue(nbytes, emit_fn):
            ch = Chunk()
            wa_pending.append((ch, nbytes, emit_fn))
            return ch

        def pump():
            while wa_pending:
                ch, nbytes, emit_fn = wa_pending[0]
                opens = [c.iv for c in wa_chunks if not c.closed]

                def fits(p0):
                    if p0 + nbytes > R_WA[1]:
                        return False
                    return all(not (iv[0] < p0 + nbytes and p0 < iv[1]) for iv in opens)
                cands = [wa_p[0], R_WA[0]] + sorted(iv[1] for iv in opens)
                p = next((c for c in cands if fits(c)), None)
                if p is None:
                    break
                wa_pending.pop(0)
                res = Res("wa")
                Mem.claim(p, p + nbytes, [res])
                ch.iv = (p, p + nbytes)
                ch.res = res
                flat = arena[:, p // 4:(p + nbytes) // 4].bitcast(BF16)
                ch.ap = emit_fn(flat, res)
                wa_chunks[:] = [c for c in wa_chunks if not (c.closed and p <= c.iv[0] and c.iv[1] <= p + nbytes)]
                wa_chunks.append(ch)
                wa_p[0] = p + nbytes

        def need(ch):
            if ch.ap is None:
                pump()
            assert ch.ap is not None, "weight chunk not resident: weight arena too small for this order"
            return ch.ap, ch.res

        def close(ch):
            ch.closed = True
            pump()

        def wload(dram_cols, ncols, k=8):
            def emit(flat, res, dram_cols=dram_cols, ncols=ncols, k=k):
                v = flat[:, 0:k * ncols].rearrange("p (k n) -> p k n", n=ncols)
                for c0_ in range(0, ncols, 512):
                    c1_ = min(ncols, c0_ + 512)
                    S.dma("pool", v[:, :, c0_:c1_], dram_cols[:, c0_:c1_].rearrange("(k p) n -> p k n", p=128), [], [res])
                return v
            return enqueue(k * ncols * 2, emit)

        def wload_multi(nbytes, shape_fn, parts):
            def emit(flat, res, shape_fn=shape_fn, parts=parts):
                v = shape_fn(flat)
                for dst_fn, src in parts:
                    S.dma("pool", dst_fn(v), src.rearrange("(k p) n -> p k n", p=128), [], [res])
                return v
            return enqueue(nbytes, emit)

        WQ = [dict() for _ in range(DEPTH)]

        def enq_ada(l_):
            WQ[l_]["ada"] = [wload(ada_w[l_][:, cg * 512:(cg + 1) * 512], 512) for cg in range(6)]

        enq_ada(0)
        for l_ in range(DEPTH):
            wl_ = w_in[l_]
            q = WQ[l_]
            q["a"] = wload(wl_[:, C_A:C_A + 512], 512)
            q["ga"] = wload(wl_[:, C_GA:C_GA + 512], 512)
            q["vb"] = wload(wl_[:, C_VB:C_VB + 512], 512)
            q["u"] = wload(wl_[:, C_U:C_U + 512], 512)
            q["gb"] = wload(wl_[:, C_GB:C_GB + 512], 512)
            q["zl"] = wload(wl_[:, C_LR:C_LR + 16], 16)
            q["q"] = wload(wl_[:, C_Q:C_Q + 512], 512)
            q["k"] = wload(wl_[:, C_K:C_K + 512], 512)
            q["vc"] = wload(wl_[:, C_VC:C_VC + 1024], 1024)
            q["gc"] = [wload(wl_[:, C_GC:C_GC + 512], 512)]
            if l_ + 1 < DEPTH:
                enq_ada(l_ + 1)
            q["gc"].append(wload(wl_[:, C_GC + 512:C_GC + 1024], 512))
            q["wo"] = []
            q["wgm"] = []
            for dp in range(4):
                cs = slice(dp * 256, (dp + 1) * 256)
                q["wo"].append(wload_multi(
                    16 * 256 * 2, lambda f: f.rearrange("p (k n) -> p k n", n=256),
                    [(lambda v: v[:, 0:4, :], w_oa[l_][:, cs]), (lambda v: v[:, 4:8, :], w_ob[l_][:, cs]),
                     (lambda v: v[:, 8:16, :], w_oc[l_][:, cs])]))
                q["wgm"].append(wload_multi(
                    8 * 3 * 256 * 2, lambda f: f.rearrange("p (k i n) -> p k i n", i=3, n=256),
                    [((lambda v, i=i: v[:, :, i, :]),
                      wl_[:, C_GM + i * 1024 + dp * 256:C_GM + i * 1024 + (dp + 1) * 256]) for i in range(3)]))
            q["wout"] = wload(w_out[l_], 1024)

        def mm(out, pairs, reads, writes):
            def fn(pe, out=out, pairs=pairs):
                n = len(pairs)
                for i, (l, r) in enumerate(pairs):
                    ins = pe.matmul(out, lhsT=l, rhs=r, start=(i == 0), stop=(i == n - 1))
                return ins
            S.op("pe", fn, reads, writes)

        def mm_groups(groups, reads, writes):
            def fn(pe, groups=groups):
                for out, pairs in groups:
                    n = len(pairs)
                    for i, (l, r) in enumerate(pairs):
                        ins = pe.matmul(out, lhsT=l, rhs=r, start=(i == 0), stop=(i == n - 1))
                return ins
            S.op("pe", fn, reads, writes)

        def transposes(items, ident, reads, writes):
            def fn(pe, items=items):
                for out, in_, idn in items:
                    ins = pe.transpose(out=out, in_=in_, identity=idn)
                return ins
            S.op("pe", fn, reads, writes)

        def act(out, in_, func, reads, writes, **kw):
            S.op("act", lambda a, out=out, in_=in_, func=func, kw=kw: a.activation(out=out, in_=in_, func=func, **kw),
                 reads, writes)

        def dve_tt(out, in0, in1, op, reads, writes):
            S.op("dve", lambda v, o=out, a=in0, b=in1, op=op: v.tensor_tensor(out=o, in0=a, in1=b, op=op), reads, writes)

        def dve_ts(out, in0, s1, s2, op0, op1, reads, writes):
            S.op("dve", lambda v, o=out, a=in0, s1=s1, s2=s2, op0=op0, op1=op1:
                 v.tensor_scalar(out=o, in0=a, scalar1=s1, scalar2=s2, op0=op0, op1=op1), reads, writes)

        def dve_stt(out, in0, scalar, in1, op0, op1, reads, writes):
            S.op("dve", lambda v, o=out, a=in0, s=scalar, b=in1, op0=op0, op1=op1:
                 v.scalar_tensor_tensor(out=o, in0=a, scalar=s, in1=b, op0=op0, op1=op1), reads, writes)

        def dve_copy(out, in_, reads, writes):
            S.op("dve", lambda v, o=out, i=in_: v.tensor_copy(out=o, in_=i), reads, writes)

        def rstd_from_ms(ms, out, r_ms, r_out, scale=1.0):
            act(out, ms, AF.Ln, [r_ms, r_epsc], [r_out], bias=epsc[0:ms.shape[0], :], scale=scale)
            act(out, out, AF.Exp, [r_out], [r_out], scale=-0.5)

        class Pipe:
            def __init__(self, stages, n, filler=None, order=None):
                self.stages = stages
                self.n = n
                self.ns = len(stages)
                self.order = order if order is not None else list(reversed(range(self.ns)))
                self.filler = filler
                self.k = 0

            def step(self):
                if self.k >= self.n + self.ns - 1:
                    return False
                for si in self.order:
                    i = self.k - si
                    if 0 <= i < self.n:
                        self.stages[si](i)
                if self.filler is not None and self.k >= self.n - 1:
                    self.filler()
                self.k += 1
                return True

            def run(self):
                while self.step():
                    pass

        def pipeline(stages, n, filler=None, order=None):
            Pipe(stages, n, filler, order).run()

        S.dma("sp", ident_f, k_ident, [], [r_identf])
        S.dma("pool", ident_b, k_ident, [], [r_identb])
        S.dma("sp", masku, k_masku, [], [r_masku])
        S.dma("sp", ucum, k_ucum, [], [r_ucum])
        S.dma("pool", ucum_b, k_ucum, [], [r_ucumb])
        S.dma("sp", invc, k_invc, [], [r_invc])
        S.op("dve", lambda v: v.memset(ones_f, 1.0), [], [r_onesf])
        S.op("dve", lambda v: v.memset(ones_b, 1.0), [], [r_ones])
        S.op("dve", lambda v: v.memset(epsc, EPS), [], [r_epsc])
        S.op("dve", lambda v: v.memset(onec, 1.0), [], [r_onec])

        bs = Bump(arena, *R_SCR)
        crow, r_crow = bs.alloc([2, 1024], F32, "crow")
        S.dma("sp", crow, c2, [], [r_crow])
        act(crow, crow, AF.Silu, [r_crow], [r_crow])
        pb, pr = psum()
        transposes([(pb[:, 2 * k:2 * k + 2], crow[0:2, k * 128:(k + 1) * 128], ident_f[0:2, 0:2]) for k in range(8)],
                   None, [r_crow, r_identf], [pr])
        dve_copy(scT, pb[:, 0:16].rearrange("p (k s) -> p k s", s=2), [pr], [r_scT])

        pump()
        TILES = [(t * 128, 128) for t in range(16)] + [(TP, TS)]
        r_x1s = [Res(f"x1s{t}") for t in range(17)]
        BLOCKS = [(0, 512, [0, 1, 2, 3]), (512, 512, [4, 5, 6, 7]), (1024, 512, [8, 9, 10, 11]),
                  (1536, 512, [12, 13, 14, 15]), (TP, TS, [16])]

        def small_params(l, region):
            bsm = Bump(arena, *region)
            stg, r_stg = bsm.alloc([128, 4, 128], F32, "stg")
            S.dma("pool", pool_wb, pool_w[l].rearrange("g c d -> c g d"), [], [r_poolw])
            S.dma("sp", pscale, pool_scale[l].rearrange("(g p) -> p g", p=128), [], [r_pscale],
                  allow_slow_non_contiguous=True)
            S.dma("sp", sgug_bc, sgu_g[l].partition_broadcast(128), [], [r_sgug])
            S.dma("sp", glag_bc, gla_g[l].partition_broadcast(128), [], [r_glag])
            S.dma("pool", sgub_b, sgu_b[l].rearrange("(o n) -> o n", o=1), [], [r_sgub])
            wa2f, r_wa2f = bsm.alloc([17, 512], F32, "wa2f")
            S.dma("sp", wa2f[0:16, :], gla_wa2[l], [], [r_wa2f])
            S.dma("sp", wa2f[16:17, :], gla_ba[l].rearrange("(o n) -> o n", o=1), [r_wa2f], [r_wa2f])
            dve_copy(wa_hi, wa2f, [r_wa2f], [r_wahi])
            dve_tt(wa_lo, wa2f, wa_hi, ALU.subtract, [r_wa2f, r_wahi], [r_walo])
            S.dma("sp", stg, sgu_w[l].rearrange("g i j -> i g j"), [], [r_stg])
            pb, pr = psum()
            transposes([(pb[:, g * 128:(g + 1) * 128], stg[:, g, :], ident_f) for g in range(4)], None,
                       [r_stg, r_identf], [pr])
            dve_tt(WT, pb[:, :].rearrange("p (g i) -> p g i", i=128),
                   masku.unsqueeze(1).broadcast_to([128, 4, 128]), ALU.mult, [pr, r_masku], [r_WT])

        def ada(l, region, run=True):
            WL = WQ[l]
            modc, r_modc = modcs[l % 2]
            ba = Bump(arena, *region)
            mod, r_mod = ba.alloc([2, 3072], F32, "mod")
            preg2, r_preg2 = ba.alloc([2, 1024], F32, "preg2")
            postg2, r_postg2 = ba.alloc([2, 1024], F32, "postg2")
            S.dma("sp", mod, ada_b[l].partition_broadcast(2), [], [r_mod])
            S.dma("sp", preg2, pre_g[l].partition_broadcast(2), [], [r_preg2])
            S.dma("sp", postg2, post_g[l].partition_broadcast(2), [], [r_postg2])

            def to_cols(j0, j1):
                pb, pr = psum()
                transposes([(pb[:, 2 * j:2 * j + 2], mod[0:2, j * 128:(j + 1) * 128], ident_f[0:2, 0:2])
                            for j in range(j0, j1)], None, [r_mod, r_identf], [pr])
                dve_copy(modc[:, j0:j1, :], pb[:, 2 * j0:2 * j1].rearrange("p (j s) -> p j s", s=2), [pr], [r_modc])

            def group(cg):
                wv, wr = need(WL["ada"][cg])
                pb, pr = psum()
                mm(pb[0:2, :], [(scT[:, k, :], wv[:, k, :]) for k in range(8)], [r_scT, wr], [pr])
                dve_tt(mod[:, cg * 512:(cg + 1) * 512], pb[0:2, :], mod[:, cg * 512:(cg + 1) * 512], ALU.add,
                       [pr, r_mod], [r_mod])
                close(WL["ada"][cg])
                if cg == 3:
                    dve_stt(mod[:, 1024:2048], mod[:, 1024:2048], 1.0, preg2, ALU.add, ALU.mult,
                            [r_mod, r_preg2], [r_mod])
                    to_cols(0, 16)
                if cg == 5:
                    dve_tt(mod[:, 2048:3072], mod[:, 2048:3072], postg2, ALU.mult, [r_mod, r_postg2], [r_mod])
                    to_cols(16, 24)
            steps = [(lambda cg=cg: group(cg)) for cg in range(6)]
            if run:
                for st in steps:
                    st()
            return steps

        def p0_stages(l, bump, get_x, stat_bump, pool_xn=False, raw=False):
            modc, r_modc = modcs[l % 2]
            xnb = [bump.alloc([128, 1024], BF16, f"xnb{i}") for i in range(2)]
            tmod, r_tmod = bump.alloc([128, 8, 128], F32, "tmod")
            junk, r_junk = bump.alloc([128, 1024], BF16, "junk")
            st0 = [stat_bump.alloc([128, 4], F32, f"st0{i}") for i in range(4)]

            def sq(t):
                c0, R = TILES[t]
                xa, xr = get_x(t)
                sa, sr = st0[t % 4]
                if raw:
                    S.op("dve", lambda v, o=junk[0:R, :], a=xa[0:R, :], acc=sa[0:R, 0:1]:
                         v.scalar_tensor_tensor(out=o, in0=a, scalar=1.0, in1=a, op0=ALU.mult, op1=ALU.mult,
                                                accum_out=acc), [xr], [r_junk, sr])
                else:
                    act(junk[0:R, :], xa[0:R, :], AF.Square, [xr], [r_junk, sr], accum_out=sa[0:R, 0:1])

            def sc(t):
                c0, R = TILES[t]
                sa, sr = st0[t % 4]
                pass

            def nrm(t):
                c0, R = TILES[t]
                xa, xr = get_x(t)
                sa, sr = st0[t % 4]
                xn, xnr = xnb[t % 2]
                rstd_from_ms(sa[0:R, 0:1], sa[0:R, 2:3], sr, sr, scale=1.0 / D)
                if pool_xn:
                    S.op("pool", lambda g_, o=xn[0:R, :], a=xa[0:R, :], sc_=sa[0:R, 2:3]:
                         g_.tensor_scalar(out=o, in0=a, scalar1=sc_, scalar2=None, op0=ALU.mult), [xr, sr], [xnr])
                else:
                    act(xn[0:R, :], xa[0:R, :], AF.Copy, [xr, sr], [xnr], scale=sa[0:R, 2:3])

            def tp(t):
                c0, R = TILES[t]
                xn, xnr = xnb[t % 2]
                pb, pr = psum(hold=True)
                pv = pbf(pb).rearrange("p (k t) -> p k t", t=128)
                transposes([(pv[:, k, 0:R], xn[0:R, k * 128:(k + 1) * 128], ident_b[0:R, 0:R]) for k in range(8)],
                           None, [xnr, r_identb], [pr])
                ctx0[t] = (pv, pr)

            def md(t):
                c0, R = TILES[t]
                sq_ = 0 if t < 16 else 1
                pv, pr = ctx0.pop(t)
                if raw:
                    dve_copy(hT[:, :, c0:c0 + R], pv[:, :, 0:R], [pr], [r_hT_t[t]])
                    punhold(pr)
                    return
                dve_tt(tmod[:, :, 0:R], pv[:, :, 0:R], modc[:, 8:16, sq_:sq_ + 1].broadcast_to([128, 8, R]), ALU.mult,
                       [pr, r_modc], [r_tmod])
                dve_tt(hT[:, :, c0:c0 + R], tmod[:, :, 0:R], modc[:, 0:8, sq_:sq_ + 1].broadcast_to([128, 8, R]),
                       ALU.add, [r_tmod, r_modc], [r_hT_t[t]])
                punhold(pr)

            ctx0 = {}
            return [sq, nrm, tp, md]

        for l in range(DEPTH):
            x_src = xin if l == 0 else x1s
            x_dst = x1s if l == 0 else y
            wl = w_in[l]
            WL = WQ[l]
            modcL, r_modcL = modcs[l % 2]

            if l == 0:
                small_params(0, R_SCR)
                Mem.claim(R_HT[0], R_HT[1], r_hT_t)
                ada_steps = ada(0, (R_YC[0], R_YC[0] + 24576), run=False)
                b0 = Bump(arena, R_YC[0] + 24576, R_MT[1])
                xt = [b0.alloc([128, 1024], F32, f"xt{i}") for i in range(8)]

                def p0_ld(t):
                    c0, R = TILES[t]
                    xa, xr = xt[t % 8]
                    S.dma("sp", xa[0:R, :], x_src[c0:c0 + R, :], [r_x1s[t]], [xr])

                def p0_nop(t):
                    pass

                P0 = Pipe([p0_ld, p0_nop] + p0_stages(0, b0, lambda t: xt[t % 8], b0, raw=True), NTILE)
                while P0.step():
                    if P0.k in (4, 8, 12, 16, 19, 22):
                        ada_steps.pop(0)()
                while ada_steps:
                    ada_steps.pop(0)()
                modc0, r_modc0 = modcs[0]
                for bi, (c0, W, tl) in enumerate(BLOCKS):
                    sq0 = 0 if bi < 4 else 1
                    rr = [r_hT_t[t] for t in tl]
                    for k in range(8):
                        if k % 2 == 0:
                            dve_ts(hT[:, k, c0:c0 + W], hT[:, k, c0:c0 + W], modc0[:, 8 + k, sq0:sq0 + 1],
                                   modc0[:, k, sq0:sq0 + 1], ALU.mult, ALU.add, rr + [r_modc0], rr)
                        else:
                            act(hT[:, k, c0:c0 + W], hT[:, k, c0:c0 + W], AF.Identity, rr + [r_modc0], rr,
                                scale=modc0[:, 8 + k, sq0:sq0 + 1], bias=modc0[:, k, sq0:sq0 + 1])

            Mem.claim(R_YC[0], R_YC[1], [r for rl in r_yc_t for r in rl])
            bA = Bump(arena, *R_MT)
            abuf = [bA.alloc([128, 15 + 512], F32, f"abuf{g}") for g in range(4)]
            tA = [bA.alloc([128, 15 + 512], F32, f"tA{i}") for i in range(3)]
            dT = [bA.alloc([128, 512], BF16, f"dT{i}") for i in range(4)]
            sga = [bA.alloc([128, 512], F32, f"sga{i}") for i in range(4)]
            fx, r_fx = bA.alloc([128, 16], F32, "fx")
            hrow, r_hrow = bA.alloc([16, 512], F32, "hrow")
            orow, r_orow = bA.alloc([16, 512], F32, "orow")
            wa_v, wa_r = need(WL["a"])
            wg_v, wg_r = need(WL["ga"])
            ctxA = {}

            def pA_s0(it):
                bi, g = divmod(it, 4)
                c0, W, tl = BLOCKS[bi]
                rh = [r_hT_t[t] for t in tl]
                if bi == 0 and g == 0:
                    for gg in range(4):
                        S.op("dve", lambda v, o=abuf[gg][0][:, 0:15]: v.memset(o, 0.0), [], [abuf[gg][1]])
                if bi == 4 and g == 0:
                    S.dma("sp", hrow[0:15, :], spool[l], [], [r_hrow])
                    pb, pr = psum()
                    transposes([(pb[:, gg * 16:gg * 16 + 15], hrow[0:15, gg * 128:(gg + 1) * 128], ident_f[0:15, 0:15])
                                for gg in range(4)], None, [r_hrow, r_identf], [pr])
                    for gg in range(4):
                        dve_copy(abuf[gg][0][:, 0:15], pb[:, gg * 16:gg * 16 + 15], [pr], [abuf[gg][1]])
                ab, ar = abuf[g]
                w = POOL_W[g]
                pa, par = psum()
                mm(pa[:, 0:W], [(wa_v[:, k, g * 128:(g + 1) * 128], hT[:, k, c0:c0 + W]) for k in range(8)],
                   [wa_r] + rh, [par])
                act(ab[:, 15:15 + W], pa[:, 0:W], AF.Copy, [par], [ar])
                pg, pgr = psum()
                mm(pg[:, 0:W], [(wg_v[:, k, g * 128:(g + 1) * 128], hT[:, k, c0:c0 + W]) for k in range(8)],
                   [wg_r] + rh, [pgr])
                sg_, sgr = sga[it % 4]
                act(sg_[:, 0:W], pg[:, 0:W], AF.Silu, [pgr], [sgr])
                src, srcr = ab, ar
                sh = 1
                lo = 1
                ti = 0
                while sh < w:
                    dst, dstr = tA[(it + ti) % 3]
                    dve_tt(dst[:, lo:15 + W], src[:, lo:15 + W], src[:, lo - sh:15 + W - sh], ALU.add,
                           [srcr], [dstr])
                    src, srcr = dst, dstr
                    sh *= 2
                    lo += sh
                    ti += 1
                d_, dr = dT[it % 4]
                dve_stt(d_[:, 0:W], src[:, 15:15 + W], 1.0 / w, ab[:, 15:15 + W], ALU.mult, ALU.subtract,
                        [srcr, ar], [dr])
                if bi == 0:
                    dve_tt(fx[:, 0:15], src[:, 15:30], invc[:, g * 16:g * 16 + 15], ALU.mult, [srcr, r_invc], [r_fx])
                    dve_tt(d_[:, 0:15], fx[:, 0:15], ab[:, 15:30], ALU.subtract, [r_fx, ar], [dr])
                if bi < 3:
                    dve_copy(ab[:, 0:15], ab[:, W:W + 15], [ar], [ar])

            def pA_s1(it):
                bi, g = divmod(it, 4)
                c0, W, tl = BLOCKS[bi]
                sq = 0 if bi < 4 else 1
                sg_, sgr = sga[it % 4]
                d_, dr = dT[it % 4]
                py, pyr = psum()
                mm(py[:, 0:W], [(pool_wb[:, g, :], d_[:, 0:W])], [r_poolw, dr], [pyr])
                dve_stt(ycat[:, g, c0:c0 + W], py[:, 0:W], pscale[:, g:g + 1], sg_[:, 0:W], ALU.mult, ALU.mult,
                        [pyr, r_pscale, sgr], [r_yc_t[g][t] for t in tl])
                if bi in (3, 4) and g == 3:
                    pho, phr = psum()
                    transposes([(pho[0:15, gg * 128:(gg + 1) * 128], abuf[gg][0][:, W:W + 15], ident_f)
                                for gg in range(4)], None, [abuf[gg][1] for gg in range(4)] + [r_identf], [phr])
                    act(orow[0:15, :], pho[0:15, :], AF.Copy, [phr], [r_orow])
                    S.dma("sp", o_pool[l, sq], orow[0:15, :], [r_orow], [])

            pipeline([pA_s0, (lambda it: None), pA_s1], 20)
            close(WL["a"])
            close(WL["ga"])

            bB = Bump(arena, *R_MT)
            vnb_p0 = bB.p
            vnb, r_vnb = bB.alloc([128, 17, 512], BF16, "vnb")
            r_vnb_t = [Res(f"vnb{t}") for t in range(NTILE)]
            Mem.claim(vnb_p0, bB.p, r_vnb_t)
            vt = [bB.alloc([128, 512], F32, f"vt{i}") for i in range(2)]
            ssb = [bB.alloc([128, 512], F32, f"ssb{i}") for i in range(2)]
            sgb = [bB.alloc([128, 512], F32, f"sgb{i}") for i in range(2)]
            tB = [bB.alloc([128, 512], F32, f"tB{i}") for i in range(2)]
            bBs = Bump(arena, *R_SCR)
            stB = [bBs.alloc([128, 16], F32, f"stB{i}") for i in range(3)]
            wv_v, wv_r = need(WL["vb"])
            ctxB = {}

            def pB_s0(t):
                c0, R = TILES[t]
                pvb, pvr = psum(hold=True)
                ctxB[t] = (pvb, pvr)
                mm(pvb[0:R, :], [(hT[:, k, c0:c0 + R], wv_v[:, k, :]) for k in range(8)], [wv_r, r_hT_t[t]], [pvr])
                st_, str_ = stB[t % 3]
                S.op("dve", lambda v, o=st_[0:R, 0:6], i=pvb[0:R, :]: v.bn_stats(out=o, in_=i), [pvr], [str_])
                S.op("dve", lambda v, o=st_[0:R, 6:8], i=st_[0:R, 0:6]: v.bn_aggr(out=o, in_=i), [str_], [str_])
                rstd_from_ms(st_[0:R, 7:8], st_[0:R, 8:9], str_, str_)

            def pB_s1(t):
                c0, R = TILES[t]
                pvb, pvr = ctxB.pop(t)
                st_, str_ = stB[t % 3]
                v_, vr_ = vt[t % 2]
                act(st_[0:R, 9:10], st_[0:R, 6:7], AF.Identity, [str_], [str_], scale=st_[0:R, 8:9])
                act(st_[0:R, 9:10], st_[0:R, 9:10], AF.Identity, [str_], [str_], scale=-1.0)
                act(v_[0:R, :], pvb[0:R, :], AF.Identity, [pvr, str_], [vr_], scale=st_[0:R, 8:9], bias=st_[0:R, 9:10])
                punhold(pvr)
                if t == 16:
                    dve_tt(v_[0:R, :], v_[0:R, :], sgug_bc[0:R, :], ALU.mult, [vr_, r_sgug], [vr_])
                    S.dma("sp", o_sgu[l], v_[0:R, :], [vr_], [])
                    act(vnb[0:R, t, :], v_[0:R, :], AF.Copy, [vr_], [r_vnb_t[t]])
                else:
                    S.op("pool", lambda g_, o=vnb[0:R, t, :], a=v_[0:R, :], b=sgug_bc[0:R, :]:
                         g_.tensor_tensor(out=o, in0=a, in1=b, op=ALU.mult), [vr_, r_sgug], [r_vnb_t[t]])

            pipeline([pB_s0, pB_s1], NTILE)
            close(WL["vb"])
            wu_v, wu_r = need(WL["u"])
            wgb_v, wgb_r = need(WL["gb"])
            step = 0
            for bi, (c0, W, tl) in enumerate(BLOCKS):
                rh = [r_hT_t[t] for t in tl]
                for g in range(4):
                    ps_, psr = psum()
                    groups = []
                    for i, t in enumerate(tl):
                        R = TILES[t][1]
                        groups.append((ps_[:, i * 128:i * 128 + R],
                                       [(vnb[0:R, t, g * 128:(g + 1) * 128], WT[0:R, g, 0:R]),
                                        (ones_b[0:1, :], sgub_b[0:1, g * 128:g * 128 + R])]))
                    mm_groups(groups, [r_vnb_t[t] for t in tl] + [r_WT, r_ones, r_sgub], [psr])
                    pu, pur = psum()
                    mm(pu[:, 0:W], [(wu_v[:, k, g * 128:(g + 1) * 128], hT[:, k, c0:c0 + W]) for k in range(8)],
                       [wu_r] + rh, [pur])
                    pg, pgr = psum()
                    mm(pg[:, 0:W], [(wgb_v[:, k, g * 128:(g + 1) * 128], hT[:, k, c0:c0 + W]) for k in range(8)],
                       [wgb_r] + rh, [pgr])
                    s_, sr_ = ssb[step % 2]
                    g_, gr_ = sgb[step % 2]
                    t_, tr_ = tB[step % 2]
                    act(s_[:, 0:W], ps_[:, 0:W], AF.Copy, [psr], [sr_])
                    act(g_[:, 0:W], pg[:, 0:W], AF.Silu, [pgr], [gr_])
                    dve_tt(t_[:, 0:W], pu[:, 0:W], s_[:, 0:W], ALU.mult, [pur, sr_], [tr_])
                    dve_tt(ycat[:, 4 + g, c0:c0 + W], t_[:, 0:W], g_[:, 0:W], ALU.mult, [tr_, gr_],
                           [r_yc_t[4 + g][t] for t in tl])
                    step += 1
            close(WL["u"])
            close(WL["gb"])

            wzl_v, wzl_r = need(WL["zl"])
            wq_v, wq_r = need(WL["q"])
            wk_v, wk_r = need(WL["k"])
            wvc_v, wvc_r = need(WL["vc"])
            bCs = Bump(arena, *R_MT)
            lsb = [bCs.alloc([128, 512], F32, f"lsb{i}") for i in range(1)]
            lhb = [bCs.alloc([128, 512], BF16, f"lhb{i}") for i in range(2)]
            llb = [bCs.alloc([128, 512], BF16, f"llb{i}") for i in range(2)]
            eq = [bCs.alloc([128, 4, 128], F32, f"eq{i}") for i in range(2)]
            ek = [bCs.alloc([128, 4, 128], F32, f"ek{i}") for i in range(2)]
            qtl = [bCs.alloc([128, 4, 128], BF16, f"qtl{i}") for i in range(3)]
            ktl = [bCs.alloc([128, 4, 128], BF16, f"ktl{i}") for i in range(2)]
            ktok = [bCs.alloc([128, 512], BF16, f"ktok{i}") for i in range(2)]
            attm = [bCs.alloc([128, 4, 128], BF16, f"attm{i}") for i in range(2)]
            vb = [bCs.alloc([128, 1024], BF16, f"vb{i}") for i in range(2)]
            ycb = [bCs.alloc([128, 1024], BF16, f"ycb{i}") for i in range(2)]
            bCt = Bump(arena, *R_SCR)
            zhb = [bCt.alloc([17, 128], BF16, f"zhb{i}") for i in range(2)]
            zlb = [bCt.alloc([17, 128], BF16, f"zlb{i}") for i in range(2)]
            ebt = [bCt.alloc([128, 4], F32, f"ebt{i}") for i in range(5)]
            stC = [bCt.alloc([128, 16], F32, f"stC{i}") for i in range(2)]
            junkc, r_junkc = bCt.alloc([128, 256], BF16, "junkc")
            Sf, r_Sf = bCt.alloc([128, 4, 256], F32, "Sf")
            Sbb = [bCt.alloc([128, 4, 256], BF16, f"Sb{i}") for i in range(2)]
            QS = 128.0 ** -0.5
            for zz, zr in zhb:
                S.op("dve", lambda v, o=zz: v.memset(o, 1.0), [], [zr])
            for zz, zr in zlb:
                S.op("dve", lambda v, o=zz: v.memset(o, 0.0), [], [zr])
            S.op("dve", lambda v: v.memset(Sf, 0.0), [], [r_Sf])
            S.op("dve", lambda v, o=Sbb[1][0]: v.memset(o, 0.0), [], [Sbb[1][1]])
            cC = {}

            def rhC(t):
                return [r_hT_t[t]]

            def pC_s0(t):
                c0, R = TILES[t]
                pz, pzr = psum()
                mm(pz[0:16, 0:R], [(wzl_v[:, k, :], hT[:, k, c0:c0 + R]) for k in range(8)], [wzl_r] + rhC(t), [pzr])
                zh_, zhr_ = zhb[t % 2]
                zl_, zlr_ = zlb[t % 2]
                act(zh_[0:16, 0:R], pz[0:16, 0:R], AF.Copy, [pzr], [zhr_])
                dve_tt(zl_[0:16, 0:R], pz[0:16, 0:R], zh_[0:16, 0:R], ALU.subtract, [pzr, zhr_], [zlr_])

            def pC_s1(t):
                c0, R = TILES[t]
                zh_, zhr_ = zhb[t % 2]
                zl_, zlr_ = zlb[t % 2]
                pp, ppr = psum()
                mm(pp[0:R, :], [(zh_[0:17, 0:R], wa_hi[0:17, :]), (zh_[0:17, 0:R], wa_lo[0:17, :]),
                                (zl_[0:17, 0:R], wa_hi[0:17, :])], [zhr_, zlr_, r_wahi, r_walo], [ppr])
                l_, lr_ = lsb[0]
                lh_, lhr_ = lhb[t % 2]
                ll_, llr_ = llb[t % 2]
                act(l_[0:R, :], pp[0:R, :], AF.Exp, [ppr], [lr_], scale=-1.0)
                act(l_[0:R, :], l_[0:R, :], AF.Ln, [lr_, r_onec], [lr_], bias=onec[0:R, :], scale=1.0)
                dve_copy(lh_[0:R, :], l_[0:R, :], [lr_], [lhr_])
                dve_tt(ll_[0:R, :], l_[0:R, :], lh_[0:R, :], ALU.subtract, [lr_, lhr_], [llr_])

            def pC_s2(t):
                c0, R = TILES[t]
                lh_, lhr_ = lhb[t % 2]
                ll_, llr_ = llb[t % 2]
                pbc, pbcr = psum()
                pbc3 = pbc[:, :].rearrange("p (h i) -> p h i", i=128)
                mm_groups([(pbc3[:, h, 0:R], [(lh_[0:R, h * 128:(h + 1) * 128], ucum_b[0:R, 0:R]),
                                              (ll_[0:R, h * 128:(h + 1) * 128], ucum_b[0:R, 0:R])]) for h in range(4)],
                          [lhr_, llr_, r_ucumb], [pbcr])
                eq_, eqr = eq[t % 2]
                ek_, ekr = ek[t % 2]
                eb_, ebr = ebt[t % 5]
                act(eq_[:, :, 0:R], pbc3[:, :, 0:R], AF.Exp, [pbcr], [eqr])
                act(ek_[:, :, 0:R], pbc3[:, :, 0:R], AF.Exp, [pbcr], [ekr], scale=-1.0)
                act(eb_[:, :], pbc3[:, :, R - 1], AF.Exp, [pbcr], [ebr])

            def pC_s3(t):
                c0, R = TILES[t]
                eq_, eqr = eq[t % 2]
                ek_, ekr = ek[t % 2]
                pq, pqr = psum()
                pq3 = pq[:, :].rearrange("p (h i) -> p h i", i=128)
                mm_groups([(pq3[:, h, 0:R], [(wq_v[:, k, h * 128:(h + 1) * 128], hT[:, k, c0:c0 + R]) for k in range(8)])
                           for h in range(4)], [wq_r] + rhC(t), [pqr])
                pk, pkr = psum()
                pk3 = pk[:, :].rearrange("p (h i) -> p h i", i=128)
                mm_groups([(pk3[:, h, 0:R], [(wk_v[:, k, h * 128:(h + 1) * 128], hT[:, k, c0:c0 + R]) for k in range(8)])
                           for h in range(4)], [wk_r] + rhC(t), [pkr])
                q_, qr_ = qtl[t % 3]
                k_, kr_ = ktl[t % 2]
                dve_stt(q_[:, :, 0:R], pq3[:, :, 0:R], QS, eq_[:, :, 0:R], ALU.mult, ALU.mult, [pqr, eqr], [qr_])
                dve_tt(k_[:, :, 0:R], pk3[:, :, 0:R], ek_[:, :, 0:R], ALU.mult, [pkr, ekr], [kr_])

            def pC_s4(t):
                c0, R = TILES[t]
                q_, qr_ = qtl[t % 3]
                k_, kr_ = ktl[t % 2]
                pkt, pktr = psum()
                pkt3 = pbf(pkt)[:, 0:512].rearrange("p (h d) -> p h d", d=128)
                transposes([(pkt3[0:R, h, :], k_[:, h, 0:R], ident_b) for h in range(4)], None, [kr_, r_identb], [pktr])
                kt_, ktr_ = ktok[t % 2]
                act(kt_[0:R, :], pbf(pkt)[0:R, 0:512], AF.Copy, [pktr], [ktr_])
                pat, patr = psum()
                pat3 = pat[:, :].rearrange("p (h i) -> p h i", i=128)
                mm_groups([(pat3[0:R, h, 0:R], [(k_[:, h, 0:R], q_[:, h, 0:R])]) for h in range(4)], [kr_, qr_], [patr])
                at_, atr_ = attm[t % 2]
                dve_tt(at_[0:R, :, 0:R], pat3[0:R, :, 0:R], masku[0:R, 0:R].unsqueeze(1).broadcast_to([R, 4, R]),
                       ALU.mult, [patr, r_masku], [atr_])
                v_, vr_ = vb[t % 2]
                for hf in range(2):
                    pv_, pvr_ = psum()
                    mm(pv_[0:R, :], [(hT[:, k, c0:c0 + R], wvc_v[:, k, hf * 512:(hf + 1) * 512]) for k in range(8)],
                       [wvc_r] + rhC(t), [pvr_])
                    act(v_[0:R, hf * 512:(hf + 1) * 512], pv_[0:R, :], AF.Copy, [pvr_], [vr_])

            def pC_s5(t):
                c0, R = TILES[t]
                sq = 0 if t < 16 else 1
                eb_, ebr = ebt[t % 5]
                q_, qr_ = qtl[t % 3]
                kt_, ktr_ = ktok[t % 2]
                at_, atr_ = attm[t % 2]
                v_, vr_ = vb[t % 2]
                Sprev, r_Sprev = Sbb[(t + 1) % 2]
                Snew, r_Snew = Sbb[t % 2]
                if t == 16:
                    S.dma("sp", Sf, sgla[l].rearrange("h d v -> d h v"), [], [r_Sf])
                    act(Sprev, Sf, AF.Copy, [r_Sf], [r_Sprev])
                st_, str_ = stC[t % 2]
                yc_, ycr_ = ycb[t % 2]
                pos = []
                for hf in range(2):
                    po, por = psum(acc=True)
                    mm_groups([(po[0:R, (h % 2) * 256:(h % 2 + 1) * 256],
                                [(at_[0:R, h, 0:R], v_[0:R, h * 256:(h + 1) * 256]),
                                 (q_[:, h, 0:R], Sprev[:, h, :])]) for h in (2 * hf, 2 * hf + 1)],
                              [atr_, vr_, qr_, r_Sprev], [por])
                    pos.append((po, por))
                for hf in range(2):
                    pkv, pkvr = psum()
                    mm_groups([(pkv[:, (h % 2) * 256:(h % 2 + 1) * 256],
                                [(kt_[0:R, h * 128:(h + 1) * 128], v_[0:R, h * 256:(h + 1) * 256])])
                               for h in (2 * hf, 2 * hf + 1)], [ktr_, vr_], [pkvr])
                    for h in (2 * hf, 2 * hf + 1):
                        dve_ts(Sf[:, h, :], Sf[:, h, :], eb_[:, h:h + 1], None, ALU.mult, ALU.bypass, [r_Sf, ebr], [r_Sf])
                        dve_stt(Sf[:, h, :], pkv[:, (h % 2) * 256:(h % 2 + 1) * 256], eb_[:, h:h + 1], Sf[:, h, :],
                                ALU.mult, ALU.add, [pkvr, ebr, r_Sf], [r_Sf])
                if t in (15, 16):
                    S.dma("sp", o_gla[l, sq].rearrange("h d v -> d h v"), Sf, [r_Sf], [])
                if t < 15:
                    S.op("pool", lambda g_, o=Snew, i=Sf: g_.tensor_copy(out=o, in_=i), [r_Sf], [r_Snew])
                for h in range(4):
                    po, por = pos[h // 2]
                    act(junkc[0:R, :], po[0:R, (h % 2) * 256:(h % 2 + 1) * 256], AF.Square, [por], [r_junkc, str_],
                        accum_out=st_[0:R, h:h + 1])
                rstd_from_ms(st_[0:R, 0:4], st_[0:R, 8:12], str_, str_, scale=1.0 / 256)
                for h in range(4):
                    po, por = pos[h // 2]
                    dve_stt(yc_[0:R, h * 256:(h + 1) * 256], po[0:R, (h % 2) * 256:(h % 2 + 1) * 256],
                            st_[0:R, 8 + h:9 + h], glag_bc[0:R, :], ALU.mult, ALU.mult, [por, str_, r_glag], [ycr_])

            def pC_s6(t):
                c0, R = TILES[t]
                yc_, ycr_ = ycb[t % 2]
                pyt, pytr = psum()
                pyt3 = pbf(pyt).rearrange("p (c t) -> p c t", t=128)
                transposes([(pyt3[:, c, 0:R], yc_[0:R, c * 128:(c + 1) * 128], ident_b[0:R, 0:R]) for c in range(8)],
                           None, [ycr_, r_identb], [pytr])
                act(ycat[:, 8:16, c0:c0 + R], pyt3[:, :, 0:R], AF.Copy, [pytr], [r_yc_t[8 + c][t] for c in range(8)])

            def c2_item(hf, bi, cc):
                def emit():
                    c0, W, tl = BLOCKS[bi]
                    c = hf * 4 + cc
                    wg_v, wg_r = need(WL["gc"][hf])
                    pg, pgr = psum()
                    mm(pg[:, 0:W], [(wg_v[:, k, cc * 128:(cc + 1) * 128], hT[:, k, c0:c0 + W]) for k in range(8)],
                       [wg_r] + [r_hT_t[t] for t in tl], [pgr])
                    act(pg[:, 0:W], pg[:, 0:W], AF.Silu, [pgr], [pgr])
                    rr = [r_yc_t[8 + c][t] for t in tl]
                    dve_tt(ycat[:, 8 + c, c0:c0 + W], ycat[:, 8 + c, c0:c0 + W], pg[:, 0:W], ALU.mult, rr + [pgr], rr)
                return emit
            c2_items = [[c2_item(hf, bi, cc) for bi in range(5) for cc in range(4)] for hf in range(2)]

            def c1_filler():
                for _ in range(2):
                    if len(c2_items[0]) > 8:
                        c2_items[0].pop(0)()

            nrot[0] = 6
            pipeline([pC_s0, pC_s1, pC_s2, pC_s3, pC_s4, pC_s5, pC_s6], NTILE, filler=c1_filler,
                     order=[4, 3, 2, 1, 0, 6, 5])
            nrot[0] = 8
            for nm in ("zl", "q", "k", "vc"):
                close(WL[nm])
            for hf in range(2):
                while c2_items[hf]:
                    c2_items[hf].pop(0)()
                close(WL["gc"][hf])
                if hf == 0 and l + 1 < DEPTH:
                    small_params(l + 1, (R_MT[0] + 24576, R_MT[1]))
                    ada(l + 1, (R_MT[0], R_MT[0] + 24576))
            if DEBUG and l == 0:
                S.barrier()
                S.dma("pool", dbg["hT"], hT.rearrange("p k t -> p (k t)"), [], [])
                S.dma("pool", dbg["ycat"], ycat.rearrange("p k t -> p (k t)"), [], [])
                S.barrier()

            Mem.claim(R_MT[0], R_MT[1], r_mT_t)
            bM = Bump(arena, *R_SCR)
            gsg = [bM.alloc([128, 512], F32, f"gsg{i}") for i in range(3)]
            tM = [bM.alloc([128, 512], F32, f"tM{i}") for i in range(2)]
            KOFF = (0, 4, 8)
            KN = (4, 4, 8)
            for dp in range(4):
                wo_, wor_ = need(WL["wo"][dp])
                wg_, wgr_ = need(WL["wgm"][dp])
                for dd in range(2):
                    dm = dp * 2 + dd
                    ds = slice(dd * 128, (dd + 1) * 128)
                    for bi, (c0, W, tl) in enumerate(BLOCKS):
                        rh = [r_hT_t[t] for t in tl]
                        for i in range(3):
                            pg, pgr = psum()
                            mm(pg[:, 0:W], [(wg_[:, k, i, ds], hT[:, k, c0:c0 + W]) for k in range(8)], [wgr_] + rh, [pgr])
                            act(gsg[i][0][:, 0:W], pg[:, 0:W], AF.Sigmoid, [pgr], [gsg[i][1]])
                        for i in range(3):
                            pp_, ppr_ = psum()
                            ry = [r_yc_t[KOFF[i] + kk][t] for kk in range(KN[i]) for t in tl]
                            mm(pp_[:, 0:W], [(wo_[:, KOFF[i] + kk, ds], ycat[:, KOFF[i] + kk, c0:c0 + W])
                                             for kk in range(KN[i])], [wor_] + ry, [ppr_])
                            if i == 0:
                                dve_tt(tM[0][0][:, 0:W], pp_[:, 0:W], gsg[0][0][:, 0:W], ALU.mult,
                                       [ppr_, gsg[0][1]], [tM[0][1]])
                            elif i == 1:
                                dve_tt(tM[1][0][:, 0:W], pp_[:, 0:W], gsg[1][0][:, 0:W], ALU.mult,
                                       [ppr_, gsg[1][1]], [tM[1][1]])
                                dve_tt(tM[0][0][:, 0:W], tM[0][0][:, 0:W], tM[1][0][:, 0:W], ALU.add,
                                       [tM[0][1], tM[1][1]], [tM[0][1]])
                            else:
                                dve_tt(tM[1][0][:, 0:W], pp_[:, 0:W], gsg[2][0][:, 0:W], ALU.mult,
                                       [ppr_, gsg[2][1]], [tM[1][1]])
                                dve_tt(mT[:, dm, c0:c0 + W], tM[0][0][:, 0:W], tM[1][0][:, 0:W], ALU.add,
                                       [tM[0][1], tM[1][1]], [r_mT_t[t] for t in tl])
                close(WL["wo"][dp])
                close(WL["wgm"][dp])
            if DEBUG and l == 0:
                S.dma("pool", dbg["mT"], mT.rearrange("p k t -> p (k t)"), [], [])
                S.barrier()

            fuse = l + 1 < DEPTH
            if fuse:
                Mem.claim(R_HT[0], R_HT[1], r_hT_t)
            bO = Bump(arena, *R_YC)
            bOs = Bump(arena, *R_SCR)
            wout_v, wout_r = need(WL["wout"])
            G_bc, r_Gbc = bO.alloc([128, 1024], F32, "G_bc")
            idg, r_idg = bO.alloc([128, 8, 128], F32, "idg")
            NXO = 9 if fuse else 6
            xo = [bO.alloc([128, 1024], F32, f"xo{i}") for i in range(NXO)]
            tO = [bO.alloc([128, 1024], F32, f"tO{i}") for i in range(2)]
            stO = [bOs.alloc([128, 8], F32, f"stO{i}") for i in range(5)]
            junko, r_junko = bOs.alloc([128, 512], BF16, "junko")
            cO = {}

            def pO_ld(t):
                c0, R = TILES[t]
                xa, xr = xo[t % NXO]
                S.dma("sp", xa[0:R, :], x_src[c0:c0 + R, :], [r_x1s[t]], [xr])

            def pO_nop(t):
                pass

            def pO_mm(t):
                c0, R = TILES[t]
                sa, sr = stO[t % 5]
                pos = []
                for hf in range(2):
                    po, por = psum(hold=True)
                    mm(po[0:R, :], [(mT[:, k, c0:c0 + R], wout_v[:, k, hf * 512:(hf + 1) * 512]) for k in range(8)],
                       [r_mT_t[t], wout_r], [por])
                    act(junko[0:R, :], po[0:R, :], AF.Square, [por], [r_junko, sr], accum_out=sa[0:R, hf:hf + 1])
                    pos.append((po, por))
                cO[t] = pos

            def pO_sc(t):
                c0, R = TILES[t]
                sa, sr = stO[t % 5]
                act(sa[0:R, 2:3], sa[0:R, 1:2], AF.Identity, [sr, r_epsc], [sr], scale=1.0 / D, bias=epsc[0:R, :])
                act(sa[0:R, 4:5], sa[0:R, 0:1], AF.Ln, [sr], [sr], scale=1.0 / D, bias=sa[0:R, 2:3])
                act(sa[0:R, 4:5], sa[0:R, 4:5], AF.Exp, [sr], [sr], scale=-0.5)

            def pO_gt(t):
                c0, R = TILES[t]
                if t in (0, 16):
                    sq = 0 if t < 16 else 1
                    for k in range(8):
                        dve_ts(idg[:, k, :], ident_f, modcL[:, 16 + k, sq:sq + 1], None, ALU.mult, ALU.bypass,
                               [r_identf, r_modcL], [r_idg])
                    for hf in range(2):
                        pb, pr = psum()
                        mm_groups([(pb[:, kk * 128:(kk + 1) * 128], [(ones_f, idg[:, hf * 4 + kk, :])]) for kk in range(4)],
                                  [r_onesf, r_idg], [pr])
                        act(G_bc[:, hf * 512:(hf + 1) * 512], pb[:, :], AF.Copy, [pr], [r_Gbc])
                ta, tr = tO[t % 2]
                sa, sr = stO[t % 5]
                pos = cO.pop(t)
                for hf in range(2):
                    po, por = pos[hf]
                    hs = slice(hf * 512, (hf + 1) * 512)
                    dve_stt(ta[0:R, hs], po[0:R, :], sa[0:R, 4:5], G_bc[0:R, hs], ALU.mult, ALU.mult,
                            [por, sr, r_Gbc], [tr])
                    punhold(por)

            def pO_add(t):
                c0, R = TILES[t]
                xa, xr = xo[t % NXO]
                ta, tr = tO[t % 2]
                dve_tt(xa[0:R, :], xa[0:R, :], ta[0:R, :], ALU.add, [xr, tr], [xr])
                S.dma("sp", x_dst[c0:c0 + R, :], xa[0:R, :], [xr], [r_x1s[t]])

            stages = [pO_ld, pO_nop, pO_mm, pO_sc, pO_gt, pO_add]
            if fuse:
                stages += p0_stages(l + 1, bO, lambda t: xo[t % NXO], bOs)
            pipeline(stages, NTILE)
            close(WL["wout"])

        S.final_wait("sp")

        keys = S.sem_keys()
        sems = {}
        for kx in keys:
            sems[kx] = es.enter_context(nc.semaphore("s_" + "_".join(str(p) for p in kx)))
        with nc.Block() as block:
            @block.sync
            def _(e):
                S.emit("sp", e, sems)

            @block.gpsimd
            def _(e):
                S.emit("pool", e, sems)

            @block.scalar
            def _(e):
                S.emit("act", e, sems)

            @block.vector
            def _(e):
                S.emit("dve", e, sems)

            @block.tensor
            def _(e):
                S.emit("pe", e, sems)
    return nc


def _consts():
    j = np.arange(128)[:, None]
    i = np.arange(128)[None, :]
    masku = (j <= i).astype(np.float32)
    ucum = (masku * (-1.0 / 16.0)).astype(np.float32)
    invc = np.ones((128, 64), np.float32)
    for g, w in enumerate(POOL_W):
        for t in range(16):
            invc[:, g * 16 + t] = 1.0 / min(w, t + 1)
    return {"k_ident": np.eye(128, dtype=np.float32), "k_masku": masku, "k_ucum": ucum, "k_invc": invc}


_NC_CACHE = {}


def kernel(x_prompt, x_sample, state_pool, state_gla, c_prompt, c_sample,
           ada_w, ada_b, pre_norm_g, post_norm_g, w_in, pool_w, pool_scale,
           sgu_norm_g, sgu_w, sgu_b, gla_wa2, gla_ba, gla_norm_g,
           w_oa, w_ob, w_oc, w_out):
    f = lambda a: np.ascontiguousarray(np.asarray(a, dtype=np.float32))
    x_prompt, x_sample, state_pool, state_gla, c_prompt, c_sample = map(
        f, (x_prompt, x_sample, state_pool, state_gla, c_prompt, c_sample))
    shared = {
        "ada_w": f(ada_w), "ada_b": f(ada_b), "pre_g": f(pre_norm_g), "post_g": f(post_norm_g), "w_in": f(w_in),
        "pool_w": f(pool_w), "pool_scale": f(pool_scale), "sgu_g": f(sgu_norm_g), "sgu_w": f(sgu_w),
        "sgu_b": f(sgu_b).reshape(DEPTH, 512), "gla_wa2": f(gla_wa2), "gla_ba": f(gla_ba), "gla_g": f(gla_norm_g),
        "w_oa": f(w_oa), "w_ob": f(w_ob), "w_oc": f(w_oc), "w_out": f(w_out),
    }
    shared.update(_consts())
    n = 8
    in_maps = []
    for b in range(n):
        m = dict(shared)
        m["xin"] = np.ascontiguousarray(np.concatenate([x_prompt[b], x_sample[b]], axis=0))
        m["c2"] = np.ascontiguousarray(np.stack([c_prompt[b], c_sample[b]], axis=0))
        m["spool"] = np.ascontiguousarray(state_pool[:, b])
        m["sgla"] = np.ascontiguousarray(state_gla[:, b])
        in_maps.append(m)
    if "nc" not in _NC_CACHE:
        _NC_CACHE["nc"] = build_program()
    nc = _NC_CACHE["nc"]
    res = run_bass_kernel_spmd(nc, in_maps, core_ids=list(range(n)))
    rs = res.results
    if DEBUG:
        kernel.dbg = rs
    y_prompt = np.stack([rs[b]["y"][:TP] for b in range(n)], axis=0)
    y_sample = np.stack([rs[b]["y"][TP:] for b in range(n)], axis=0)
    pool_p = np.stack([rs[b]["o_pool"][:, 0] for b in range(n)], axis=1)
    pool_s = np.stack([rs[b]["o_pool"][:, 1] for b in range(n)], axis=1)
    gla_p = np.stack([rs[b]["o_gla"][:, 0] for b in range(n)], axis=1)
    gla_s = np.stack([rs[b]["o_gla"][:, 1] for b in range(n)], axis=1)
    sgu_s = np.stack([rs[b]["o_sgu"] for b in range(n)], axis=1)
    return (y_prompt.astype(np.float32), y_sample.astype(np.float32), pool_p.astype(np.float32),
            gla_p.astype(np.float32), pool_s.astype(np.float32), gla_s.astype(np.float32),
            sgu_s.astype(np.float32))
```

```python
import numpy as np
from contextlib import ExitStack
import concourse.bass as bass
import concourse.mybir as mybir
from concourse.bass_utils import run_bass_kernel_spmd

F32 = mybir.dt.float32
BF16 = mybir.dt.bfloat16
AF = mybir.ActivationFunctionType
ALU = mybir.AluOpType

D = 1024
TP = 2048
TS = 64
NT = TP + TS
DEPTH = 2
D_IN = 8720
EPS = 1e-6
C_A, C_GA, C_U, C_VB, C_GB, C_Q, C_K, C_VC, C_GC, C_LR, C_GM = (
    0, 512, 1024, 1536, 2048, 2560, 3072, 3584, 4608, 5632, 5648)
POOL_W = (2, 4, 8, 16)

DEBUG = False


class Res:
    __slots__ = ("w", "r", "name")

    def __init__(self, name=""):
        self.w = None
        self.r = []
        self.name = name


class Sched:
    ENGS = ("pe", "act", "dve", "pool", "sp")
    NDMA = 8

    def __init__(self):
        self.ops = {e: [] for e in self.ENGS}
        self.count = {e: 0 for e in self.ENGS}
        self.known = {e: {} for e in self.ENGS}
        self.dma_n = {e: 0 for e in self.ENGS}
        self.dma_tokens = []

    def _waits(self, eng, reads, writes, extra=()):
        toks = list(extra)
        for r in reads:
            if r.w is not None:
                toks.append(r.w)
        for w in writes:
            if w.w is not None:
                toks.append(w.w)
            toks.extend(w.r)
        kn = self.known[eng]
        waits = {}
        for (sem, val) in toks:
            if sem == ("e", eng) and eng == "pe":
                continue
            if kn.get(sem, 0) >= val:
                continue
            if waits.get(sem, 0) < val:
                waits[sem] = val
        for sem, val in waits.items():
            kn[sem] = val
        return list(waits.items())

    def _commit(self, tok, reads, writes):
        for r in reads:
            r.r.append(tok)
        for w in writes:
            w.w = tok
            w.r = []

    def op(self, eng, fn, reads=(), writes=()):
        waits = self._waits(eng, reads, writes)
        self.count[eng] += 1
        tok = (("e", eng), self.count[eng])
        self.ops[eng].append((waits, fn, tok, 1))
        self._commit(tok, reads, writes)
        return tok

    def dma(self, eng, out, in_, reads=(), writes=(), **kw):
        n = self.dma_n[eng]
        self.dma_n[eng] += 1
        sem = ("d", eng, n % self.NDMA)
        val = 16 * (n // self.NDMA + 1)
        extra = [(sem, val - 16)] if val > 16 else []
        waits = self._waits(eng, reads, writes, extra)
        tok = (sem, val)

        def fn(e, out=out, in_=in_, kw=kw):
            return e.dma_start(out=out, in_=in_, **kw)
        self.ops[eng].append((waits, fn, tok, 16))
        self._commit(tok, reads, writes)
        self.dma_tokens.append(tok)
        return tok

    def barrier(self):
        toks = [(("e", e), self.count[e]) for e in self.ENGS if self.count[e] > 0]
        last = {}
        for (sem, val) in self.dma_tokens:
            if last.get(sem, 0) < val:
                last[sem] = val
        toks += list(last.items())
        for e in self.ENGS:
            kn = self.known[e]
            waits = []
            for (sem, val) in toks:
                if sem == ("e", e) and e == "pe":
                    continue
                if kn.get(sem, 0) < val:
                    kn[sem] = val
                    waits.append((sem, val))
            if waits:
                self.ops[e].append((waits, None, None, 0))

    def final_wait(self, eng="sp"):
        last = {}
        for (sem, val) in self.dma_tokens:
            if last.get(sem, 0) < val:
                last[sem] = val
        self.ops[eng].append((list(last.items()), None, None, 0))

    def sem_keys(self):
        keys = [("e", e) for e in self.ENGS]
        for e in self.ENGS:
            if self.dma_n[e] > 0:
                keys += [("d", e, i) for i in range(min(self.NDMA, self.dma_n[e]))]
        return keys

    def emit(self, eng, engine, sems):
        for (waits, fn, tok, inc) in self.ops[eng]:
            for (sem, val) in waits:
                engine.wait_ge(sems[sem], val)
            if fn is not None:
                ins = fn(engine)
                ins.then_inc(sems[tok[0]], inc)


class Mem:
    live = []

    @classmethod
    def claim(cls, start, end, res_list):
        seed = {}
        keep = []
        for (s0, e0, rl) in cls.live:
            if s0 < end and start < e0:
                for r in rl:
                    toks = list(r.r)
                    if r.w is not None:
                        toks.append(r.w)
                    for (sem, val) in toks:
                        if seed.get(sem, 0) < val:
                            seed[sem] = val
                if not (start <= s0 and e0 <= end):
                    keep.append((s0, e0, rl))
            else:
                keep.append((s0, e0, rl))
        keep.append((start, end, res_list))
        cls.live = keep
        for r in res_list:
            r.w = None
            r.r = list(seed.items())


class Bump:
    def __init__(self, arena, start, end):
        self.arena = arena
        self.p = start
        self.end = end

    def alloc(self, shape, dt, name=""):
        esz = 4 if dt == F32 else 2
        n = 1
        for s in shape[1:]:
            n *= s
        nbytes = (n * esz + 31) // 32 * 32
        assert self.p + nbytes <= self.end, f"arena overflow for {name}: {self.p}+{nbytes}>{self.end}"
        o4 = self.p // 4
        v = self.arena[:, o4:o4 + nbytes // 4]
        if dt != F32:
            v = v.bitcast(dt)
        v = v[:, 0:n]
        if len(shape) == 3:
            v = v.rearrange("p (a b) -> p a b", b=shape[2])
        elif len(shape) == 4:
            v = v.rearrange("p (a b c) -> p a b c", b=shape[2], c=shape[3])
        if shape[0] < 128:
            v = v[0:shape[0]]
        res = Res(name)
        Mem.claim(self.p, self.p + nbytes, [res])
        self.p += nbytes
        return v, res


def build_program():
    nc = bass.Bass("TRN2", target_bir_lowering=False)
    S = Sched()
    Mem.live = []

    def din(name, shape):
        return nc.dram_tensor(name, list(shape), F32, kind="ExternalInput").ap()

    def dout(name, shape):
        return nc.dram_tensor(name, list(shape), F32, kind="ExternalOutput").ap()

    xin = din("xin", [NT, D])
    c2 = din("c2", [2, D])
    spool = din("spool", [DEPTH, 15, 512])
    sgla = din("sgla", [DEPTH, 4, 128, 256])
    ada_w = din("ada_w", [DEPTH, D, 3 * D])
    ada_b = din("ada_b", [DEPTH, 3 * D])
    pre_g = din("pre_g", [DEPTH, D])
    post_g = din("post_g", [DEPTH, D])
    w_in = din("w_in", [DEPTH, D, D_IN])
    pool_w = din("pool_w", [DEPTH, 4, 128, 128])
    pool_scale = din("pool_scale", [DEPTH, 512])
    sgu_g = din("sgu_g", [DEPTH, 512])
    sgu_w = din("sgu_w", [DEPTH, 4, 128, 128])
    sgu_b = din("sgu_b", [DEPTH, 512])
    gla_wa2 = din("gla_wa2", [DEPTH, 16, 512])
    gla_ba = din("gla_ba", [DEPTH, 512])
    gla_g = din("gla_g", [DEPTH, 256])
    w_oa = din("w_oa", [DEPTH, 512, D])
    w_ob = din("w_ob", [DEPTH, 512, D])
    w_oc = din("w_oc", [DEPTH, 1024, D])
    w_out = din("w_out", [DEPTH, D, D])
    k_ident = din("k_ident", [128, 128])
    k_masku = din("k_masku", [128, 128])
    k_ucum = din("k_ucum", [128, 128])
    k_invc = din("k_invc", [128, 64])

    y = dout("y", [NT, D])
    o_pool = dout("o_pool", [DEPTH, 2, 15, 512])
    o_gla = dout("o_gla", [DEPTH, 2, 4, 128, 256])
    o_sgu = dout("o_sgu", [DEPTH, TS, 512])
    x1s = nc.dram_tensor("x1s", [NT, D], F32, kind="Internal").ap()
    dbg = {}
    if DEBUG:
        dbg["hT"] = dout("d_hT", [128, 8 * NT])
        dbg["ycat"] = dout("d_ycat", [128, 16 * NT])
        dbg["mT"] = dout("d_mT", [128, 8 * NT])

    ARENA_B = 210944
    with ExitStack() as es:
        arena = es.enter_context(nc.sbuf_tensor("arena", [128, ARENA_B // 4], F32))
        banks = [es.enter_context(nc.psum_tensor(f"bank{i}", [128, 512], F32)) for i in range(8)]
        bank_res = [Res(f"bank{i}") for i in range(8)]
        bank_i = [0]

        nrot = [8]
        acc_i = [0]

        held = set()

        def psum(acc=False, hold=False):
            if acc:
                i = 6 + acc_i[0] % 2
                acc_i[0] += 1
            else:
                for _ in range(nrot[0] + 1):
                    i = bank_i[0] % nrot[0]
                    bank_i[0] += 1
                    if i not in held:
                        break
                else:
                    raise RuntimeError("no free PSUM bank")
            if hold:
                held.add(i)
            return banks[i], bank_res[i]

        def punhold(res):
            held.discard(bank_res.index(res))

        def pbf(bank):
            return bank[:, :].bitcast(BF16)

        HT_B, YC_B, MT_B = 8 * NT * 2, 16 * NT * 2, 8 * NT * 2
        R_HT = (0, HT_B)
        R_YC = (HT_B, HT_B + YC_B)
        R_MT = (HT_B + YC_B, HT_B + YC_B + MT_B)
        R_CONST = (R_MT[1], R_MT[1] + 14336)
        R_WA = (R_CONST[1], R_CONST[1] + 51200)
        R_SCR = (R_WA[1], ARENA_B)
        assert R_SCR[1] - R_SCR[0] == 10240

        def resident(rg, k):
            o4 = rg[0] // 4
            return arena[:, o4:o4 + (rg[1] - rg[0]) // 4].bitcast(BF16).rearrange("p (k t) -> p k t", k=k)
        hT = resident(R_HT, 8)
        ycat = resident(R_YC, 16)
        mT = resident(R_MT, 8)
        NTILE = 17
        r_hT_t = [Res(f"hT{t}") for t in range(NTILE)]
        r_yc_t = [[Res(f"yc{c}_{t}") for t in range(NTILE)] for c in range(16)]
        r_mT_t = [Res(f"mT{t}") for t in range(NTILE)]

        bc = Bump(arena, *R_CONST)
        ident_f, r_identf = bc.alloc([128, 128], F32, "ident_f")
        ident_b, r_identb = bc.alloc([128, 128], BF16, "ident_b")
        masku, r_masku = bc.alloc([128, 128], F32, "masku")
        ucum, r_ucum = bc.alloc([128, 128], F32, "ucum")
        invc, r_invc = bc.alloc([128, 64], F32, "invc")
        ones_f, r_onesf = bc.alloc([128, 128], F32, "ones_f")
        ones_b, r_ones = bc.alloc([1, 128], BF16, "ones_b")
        scT, r_scT = bc.alloc([128, 8, 2], BF16, "scT")
        pool_wb, r_poolw = bc.alloc([128, 4, 128], BF16, "pool_wb")
        WT, r_WT = bc.alloc([128, 4, 128], BF16, "WT")
        pscale, r_pscale = bc.alloc([128, 4], F32, "pscale")
        sgug_bc, r_sgug = bc.alloc([128, 512], F32, "sgug_bc")
        glag_bc, r_glag = bc.alloc([128, 256], F32, "glag_bc")
        sgub_b, r_sgub = bc.alloc([1, 512], BF16, "sgub_b")
        wa_hi, r_wahi = bc.alloc([17, 512], BF16, "wa_hi")
        wa_lo, r_walo = bc.alloc([17, 512], BF16, "wa_lo")
        ucum_b, r_ucumb = bc.alloc([128, 128], BF16, "ucum_b")
        modcs = [bc.alloc([128, 24, 2], F32, f"modc{i}") for i in range(2)]
        epsc, r_epsc = bc.alloc([128, 1], F32, "epsc")
        onec, r_onec = bc.alloc([128, 1], F32, "onec")

        class Chunk:
            __slots__ = ("ap", "res", "closed", "iv")

            def __init__(self):
                self.ap = None
                self.res = None
                self.closed = False
                self.iv = None

        wa_p = [R_WA[0]]
        wa_chunks = []
        wa_pending = []

        def enqueue(nbytes, emit_fn):
            ch = Chunk()
            wa_pending.append((ch, nbytes, emit_fn))
            return ch

        def pump():
            while wa_pending:
                ch, nbytes, emit_fn = wa_pending[0]
                opens = [c.iv for c in wa_chunks if not c.closed]

                def fits(p0):
                    if p0 + nbytes > R_WA[1]:
                        return False
                    return all(not (iv[0] < p0 + nbytes and p0 < iv[1]) for iv in opens)
                cands = [wa_p[0], R_WA[0]] + sorted(iv[1] for iv in opens)
                p = next((c for c in cands if fits(c)), None)
                if p is None:
                    break
                wa_pending.pop(0)
                res = Res("wa")
                Mem.claim(p, p + nbytes, [res])
                ch.iv = (p, p + nbytes)
                ch.res = res
                flat = arena[:, p // 4:(p + nbytes) // 4].bitcast(BF16)
                ch.ap = emit_fn(flat, res)
                wa_chunks[:] = [c for c in wa_chunks if not (c.closed and p <= c.iv[0] and c.iv[1] <= p + nbytes)]
                wa_chunks.append(ch)
                wa_p[0] = p + nbytes

        def need(ch):
            if ch.ap is None:
                pump()
            assert ch.ap is not None, "weight chunk not resident: weight arena too small for this order"
            return ch.ap, ch.res

        def close(ch):
            ch.closed = True
            pump()

        def wload(dram_cols, ncols, k=8):
            def emit(flat, res, dram_cols=dram_cols, ncols=ncols, k=k):
                v = flat[:, 0:k * ncols].rearrange("p (k n) -> p k n", n=ncols)
                for c0_ in range(0, ncols, 512):
                    c1_ = min(ncols, c0_ + 512)
                    S.dma("pool", v[:, :, c0_:c1_], dram_cols[:, c0_:c1_].rearrange("(k p) n -> p k n", p=128), [], [res])
                return v
            return enqueue(k * ncols * 2, emit)

        def wload_multi(nbytes, shape_fn, parts):
            def emit(flat, res, shape_fn=shape_fn, parts=parts):
                v = shape_fn(flat)
                for dst_fn, src in parts:
                    S.dma("pool", dst_fn(v), src.rearrange("(k p) n -> p k n", p=128), [], [res])
                return v
            return enqueue(nbytes, emit)

        WQ = [dict() for _ in range(DEPTH)]

        def enq_ada(l_):
            WQ[l_]["ada"] = [wload(ada_w[l_][:, cg * 512:(cg + 1) * 512], 512) for cg in range(6)]

        enq_ada(0)
        for l_ in range(DEPTH):
            wl_ = w_in[l_]
            q = WQ[l_]
            q["a"] = wload(wl_[:, C_A:C_A + 512], 512)
            q["ga"] = wload(wl_[:, C_GA:C_GA + 512], 512)
            q["vb"] = wload(wl_[:, C_VB:C_VB + 512], 512)
            q["u"] = wload(wl_[:, C_U:C_U + 512], 512)
            q["gb"] = wload(wl_[:, C_GB:C_GB + 512], 512)
            q["zl"] = wload(wl_[:, C_LR:C_LR + 16], 16)
            q["q"] = wload(wl_[:, C_Q:C_Q + 512], 512)
            q["k"] = wload(wl_[:, C_K:C_K + 512], 512)
            q["vc"] = wload(wl_[:, C_VC:C_VC + 1024], 1024)
            q["gc"] = [wload(wl_[:, C_GC:C_GC + 512], 512)]
            if l_ + 1 < DEPTH:
                enq_ada(l_ + 1)
            q["gc"].append(wload(wl_[:, C_GC + 512:C_GC + 1024], 512))
            q["wo"] = []
            q["wgm"] = []
            for dp in range(4):
                cs = slice(dp * 256, (dp + 1) * 256)
                q["wo"].append(wload_multi(
                    16 * 256 * 2, lambda f: f.rearrange("p (k n) -> p k n", n=256),
                    [(lambda v: v[:, 0:4, :], w_oa[l_][:, cs]), (lambda v: v[:, 4:8, :], w_ob[l_][:, cs]),
                     (lambda v: v[:, 8:16, :], w_oc[l_][:, cs])]))
                q["wgm"].append(wload_multi(
                    8 * 3 * 256 * 2, lambda f: f.rearrange("p (k i n) -> p k i n", i=3, n=256),
                    [((lambda v, i=i: v[:, :, i, :]),
                      wl_[:, C_GM + i * 1024 + dp * 256:C_GM + i * 1024 + (dp + 1) * 256]) for i in range(3)]))
            q["wout"] = wload(w_out[l_], 1024)

        def mm(out, pairs, reads, writes):
            def fn(pe, out=out, pairs=pairs):
                n = len(pairs)
                for i, (l, r) in enumerate(pairs):
                    ins = pe.matmul(out, lhsT=l, rhs=r, start=(i == 0), stop=(i == n - 1))
                return ins
            S.op("pe", fn, reads, writes)

        def mm_groups(groups, reads, writes):
            def fn(pe, groups=groups):
                for out, pairs in groups:
                    n = len(pairs)
                    for i, (l, r) in enumerate(pairs):
                        ins = pe.matmul(out, lhsT=l, rhs=r, start=(i == 0), stop=(i == n - 1))
                return ins
            S.op("pe", fn, reads, writes)

        def transposes(items, ident, reads, writes):
            def fn(pe, items=items):
                for out, in_, idn in items:
                    ins = pe.transpose(out=out, in_=in_, identity=idn)
                return ins
            S.op("pe", fn, reads, writes)

        def act(out, in_, func, reads, writes, **kw):
            S.op("act", lambda a, out=out, in_=in_, func=func, kw=kw: a.activation(out=out, in_=in_, func=func, **kw),
                 reads, writes)

        def dve_tt(out, in0, in1, op, reads, writes):
            S.op("dve", lambda v, o=out, a=in0, b=in1, op=op: v.tensor_tensor(out=o, in0=a, in1=b, op=op), reads, writes)

        def dve_ts(out, in0, s1, s2, op0, op1, reads, writes):
            S.op("dve", lambda v, o=out, a=in0, s1=s1, s2=s2, op0=op0, op1=op1:
                 v.tensor_scalar(out=o, in0=a, scalar1=s1, scalar2=s2, op0=op0, op1=op1), reads, writes)

        def dve_stt(out, in0, scalar, in1, op0, op1, reads, writes):
            S.op("dve", lambda v, o=out, a=in0, s=scalar, b=in1, op0=op0, op1=op1:
                 v.scalar_tensor_tensor(out=o, in0=a, scalar=s, in1=b, op0=op0, op1=op1), reads, writes)

        def dve_copy(out, in_, reads, writes):
            S.op("dve", lambda v, o=out, i=in_: v.tensor_copy(out=o, in_=i), reads, writes)

        def rstd_from_ms(ms, out, r_ms, r_out, scale=1.0):
            act(out, ms, AF.Ln, [r_ms, r_epsc], [r_out], bias=epsc[0:ms.shape[0], :], scale=scale)
            act(out, out, AF.Exp, [r_out], [r_out], scale=-0.5)

        class Pipe:
            def __init__(self, stages, n, filler=None, order=None):
                self.stages = stages
                self.n = n
                self.ns = len(stages)
                self.order = order if order is not None else list(reversed(range(self.ns)))
                self.filler = filler
                self.k = 0

            def step(self):
                if self.k >= self.n + self.ns - 1:
                    return False
                for si in self.order:
                    i = self.k - si
                    if 0 <= i < self.n:
                        self.stages[si](i)
                if self.filler is not None and self.k >= self.n - 1:
                    self.filler()
                self.k += 1
                return True

            def run(self):
                while self.step():
                    pass

        def pipeline(stages, n, filler=None, order=None):
            Pipe(stages, n, filler, order).run()

        S.dma("sp", ident_f, k_ident, [], [r_identf])
        S.dma("pool", ident_b, k_ident, [], [r_identb])
        S.dma("sp", masku, k_masku, [], [r_masku])
        S.dma("sp", ucum, k_ucum, [], [r_ucum])
        S.dma("pool", ucum_b, k_ucum, [], [r_ucumb])
        S.dma("sp", invc, k_invc, [], [r_invc])
        S.op("dve", lambda v: v.memset(ones_f, 1.0), [], [r_onesf])
        S.op("dve", lambda v: v.memset(ones_b, 1.0), [], [r_ones])
        S.op("dve", lambda v: v.memset(epsc, EPS), [], [r_epsc])
        S.op("dve", lambda v: v.memset(onec, 1.0), [], [r_onec])

        bs = Bump(arena, *R_SCR)
        crow, r_crow = bs.alloc([2, 1024], F32, "crow")
        S.dma("sp", crow, c2, [], [r_crow])
        act(crow, crow, AF.Silu, [r_crow], [r_crow])
        pb, pr = psum()
        transposes([(pb[:, 2 * k:2 * k + 2], crow[0:2, k * 128:(k + 1) * 128], ident_f[0:2, 0:2]) for k in range(8)],
                   None, [r_crow, r_identf], [pr])
        dve_copy(scT, pb[:, 0:16].rearrange("p (k s) -> p k s", s=2), [pr], [r_scT])

        pump()
        TILES = [(t * 128, 128) for t in range(16)] + [(TP, TS)]
        r_x1s = [Res(f"x1s{t}") for t in range(17)]
        BLOCKS = [(0, 512, [0, 1, 2, 3]), (512, 512, [4, 5, 6, 7]), (1024, 512, [8, 9, 10, 11]),
                  (1536, 512, [12, 13, 14, 15]), (TP, TS, [16])]

        def small_params(l, region):
            bsm = Bump(arena, *region)
            stg, r_stg = bsm.alloc([128, 4, 128], F32, "stg")
            S.dma("pool", pool_wb, pool_w[l].rearrange("g c d -> c g d"), [], [r_poolw])
            S.dma("sp", pscale, pool_scale[l].rearrange("(g p) -> p g", p=128), [], [r_pscale],
                  allow_slow_non_contiguous=True)
            S.dma("sp", sgug_bc, sgu_g[l].partition_broadcast(128), [], [r_sgug])
            S.dma("sp", glag_bc, gla_g[l].partition_broadcast(128), [], [r_glag])
            S.dma("pool", sgub_b, sgu_b[l].rearrange("(o n) -> o n", o=1), [], [r_sgub])
            wa2f, r_wa2f = bsm.alloc([17, 512], F32, "wa2f")
            S.dma("sp", wa2f[0:16, :], gla_wa2[l], [], [r_wa2f])
            S.dma("sp", wa2f[16:17, :], gla_ba[l].rearrange("(o n) -> o n", o=1), [r_wa2f], [r_wa2f])
            dve_copy(wa_hi, wa2f, [r_wa2f], [r_wahi])
            dve_tt(wa_lo, wa2f, wa_hi, ALU.subtract, [r_wa2f, r_wahi], [r_walo])
            S.dma("sp", stg, sgu_w[l].rearrange("g i j -> i g j"), [], [r_stg])
            pb, pr = psum()
            transposes([(pb[:, g * 128:(g + 1) * 128], stg[:, g, :], ident_f) for g in range(4)], None,
                       [r_stg, r_identf], [pr])
            dve_tt(WT, pb[:, :].rearrange("p (g i) -> p g i", i=128),
                   masku.unsqueeze(1).broadcast_to([128, 4, 128]), ALU.mult, [pr, r_masku], [r_WT])

        def ada(l, region, run=True):
            WL = WQ[l]
            modc, r_modc = modcs[l % 2]
            ba = Bump(arena, *region)
            mod, r_mod = ba.alloc([2, 3072], F32, "mod")
            preg2, r_preg2 = ba.alloc([2, 1024], F32, "preg2")
            postg2, r_postg2 = ba.alloc([2, 1024], F32, "postg2")
            S.dma("sp", mod, ada_b[l].partition_broadcast(2), [], [r_mod])
            S.dma("sp", preg2, pre_g[l].partition_broadcast(2), [], [r_preg2])
            S.dma("sp", postg2, post_g[l].partition_broadcast(2), [], [r_postg2])

            def to_cols(j0, j1):
                pb, pr = psum()
                transposes([(pb[:, 2 * j:2 * j + 2], mod[0:2, j * 128:(j + 1) * 128], ident_f[0:2, 0:2])
                            for j in range(j0, j1)], None, [r_mod, r_identf], [pr])
                dve_copy(modc[:, j0:j1, :], pb[:, 2 * j0:2 * j1].rearrange("p (j s) -> p j s", s=2), [pr], [r_modc])

            def group(cg):
                wv, wr = need(WL["ada"][cg])
                pb, pr = psum()
                mm(pb[0:2, :], [(scT[:, k, :], wv[:, k, :]) for k in range(8)], [r_scT, wr], [pr])
                dve_tt(mod[:, cg * 512:(cg + 1) * 512], pb[0:2, :], mod[:, cg * 512:(cg + 1) * 512], ALU.add,
                       [pr, r_mod], [r_mod])
                close(WL["ada"][cg])
                if cg == 3:
                    dve_stt(mod[:, 1024:2048], mod[:, 1024:2048], 1.0, preg2, ALU.add, ALU.mult,
                            [r_mod, r_preg2], [r_mod])
                    to_cols(0, 16)
                if cg == 5:
                    dve_tt(mod[:, 2048:3072], mod[:, 2048:3072], postg2, ALU.mult, [r_mod, r_postg2], [r_mod])
                    to_cols(16, 24)
            steps = [(lambda cg=cg: group(cg)) for cg in range(6)]
            if run:
                for st in steps:
                    st()
            return steps

        def p0_stages(l, bump, get_x, stat_bump, pool_xn=False, raw=False):
            modc, r_modc = modcs[l % 2]
            xnb = [bump.alloc([128, 1024], BF16, f"xnb{i}") for i in range(2)]
            tmod, r_tmod = bump.alloc([128, 8, 128], F32, "tmod")
            junk, r_junk = bump.alloc([128, 1024], BF16, "junk")
            st0 = [stat_bump.alloc([128, 4], F32, f"st0{i}") for i in range(4)]

            def sq(t):
                c0, R = TILES[t]
                xa, xr = get_x(t)
                sa, sr = st0[t % 4]
                if raw:
                    S.op("dve", lambda v, o=junk[0:R, :], a=xa[0:R, :], acc=sa[0:R, 0:1]:
                         v.scalar_tensor_tensor(out=o, in0=a, scalar=1.0, in1=a, op0=ALU.mult, op1=ALU.mult,
                                                accum_out=acc), [xr], [r_junk, sr])
                else:
                    act(junk[0:R, :], xa[0:R, :], AF.Square, [xr], [r_junk, sr], accum_out=sa[0:R, 0:1])

            def sc(t):
                c0, R = TILES[t]
                sa, sr = st0[t % 4]
                pass

            def nrm(t):
                c0, R = TILES[t]
                xa, xr = get_x(t)
                sa, sr = st0[t % 4]
                xn, xnr = xnb[t % 2]
                rstd_from_ms(sa[0:R, 0:1], sa[0:R, 2:3], sr, sr, scale=1.0 / D)
                if pool_xn:
                    S.op("pool", lambda g_, o=xn[0:R, :], a=xa[0:R, :], sc_=sa[0:R, 2:3]:
                         g_.tensor_scalar(out=o, in0=a, scalar1=sc_, scalar2=None, op0=ALU.mult), [xr, sr], [xnr])
                else:
                    act(xn[0:R, :], xa[0:R, :], AF.Copy, [xr, sr], [xnr], scale=sa[0:R, 2:3])

            def tp(t):
                c0, R = TILES[t]
                xn, xnr = xnb[t % 2]
                pb, pr = psum(hold=True)
                pv = pbf(pb).rearrange("p (k t) -> p k t", t=128)
                transposes([(pv[:, k, 0:R], xn[0:R, k * 128:(k + 1) * 128], ident_b[0:R, 0:R]) for k in range(8)],
                           None, [xnr, r_identb], [pr])
                ctx0[t] = (pv, pr)

            def md(t):
                c0, R = TILES[t]
                sq_ = 0 if t < 16 else 1
                pv, pr = ctx0.pop(t)
                if raw:
                    dve_copy(hT[:, :, c0:c0 + R], pv[:, :, 0:R], [pr], [r_hT_t[t]])
                    punhold(pr)
                    return
                dve_tt(tmod[:, :, 0:R], pv[:, :, 0:R], modc[:, 8:16, sq_:sq_ + 1].broadcast_to([128, 8, R]), ALU.mult,
                       [pr, r_modc], [r_tmod])
                dve_tt(hT[:, :, c0:c0 + R], tmod[:, :, 0:R], modc[:, 0:8, sq_:sq_ + 1].broadcast_to([128, 8, R]),
                       ALU.add, [r_tmod, r_modc], [r_hT_t[t]])
                punhold(pr)

            ctx0 = {}
            return [sq, nrm, tp, md]

        for l in range(DEPTH):
            x_src = xin if l == 0 else x1s
            x_dst = x1s if l == 0 else y
            wl = w_in[l]
            WL = WQ[l]
            modcL, r_modcL = modcs[l % 2]

            if l == 0:
                small_params(0, R_SCR)
                Mem.claim(R_HT[0], R_HT[1], r_hT_t)
                ada_steps = ada(0, (R_YC[0], R_YC[0] + 24576), run=False)
                b0 = Bump(arena, R_YC[0] + 24576, R_MT[1])
                xt = [b0.alloc([128, 1024], F32, f"xt{i}") for i in range(8)]

                def p0_ld(t):
                    c0, R = TILES[t]
                    xa, xr = xt[t % 8]
                    S.dma("sp", xa[0:R, :], x_src[c0:c0 + R, :], [r_x1s[t]], [xr])

                def p0_nop(t):
                    pass

                P0 = Pipe([p0_ld, p0_nop] + p0_stages(0, b0, lambda t: xt[t % 8], b0, raw=True), NTILE)
                while P0.step():
                    if P0.k in (4, 8, 12, 16, 19, 22):
                        ada_steps.pop(0)()
                while ada_steps:
                    ada_steps.pop(0)()
                modc0, r_modc0 = modcs[0]
                for bi, (c0, W, tl) in enumerate(BLOCKS):
                    sq0 = 0 if bi < 4 else 1
                    rr = [r_hT_t[t] for t in tl]
                    for k in range(8):
                        if k % 2 == 0:
                            dve_ts(hT[:, k, c0:c0 + W], hT[:, k, c0:c0 + W], modc0[:, 8 + k, sq0:sq0 + 1],
                                   modc0[:, k, sq0:sq0 + 1], ALU.mult, ALU.add, rr + [r_modc0], rr)
                        else:
                            act(hT[:, k, c0:c0 + W], hT[:, k, c0:c0 + W], AF.Identity, rr + [r_modc0], rr,
                                scale=modc0[:, 8 + k, sq0:sq0 + 1], bias=modc0[:, k, sq0:sq0 + 1])

            Mem.claim(R_YC[0], R_YC[1], [r for rl in r_yc_t for r in rl])
            bA = Bump(arena, *R_MT)
            abuf = [bA.alloc([128, 15 + 512], F32, f"abuf{g}") for g in range(4)]
            tA = [bA.alloc([128, 15 + 512], F32, f"tA{i}") for i in range(3)]
            dT = [bA.alloc([128, 512], BF16, f"dT{i}") for i in range(4)]
            sga = [bA.alloc([128, 512], F32, f"sga{i}") for i in range(4)]
            fx, r_fx = bA.alloc([128, 16], F32, "fx")
            hrow, r_hrow = bA.alloc([16, 512], F32, "hrow")
            orow, r_orow = bA.alloc([16, 512], F32, "orow")
            wa_v, wa_r = need(WL["a"])
            wg_v, wg_r = need(WL["ga"])
            ctxA = {}

            def pA_s0(it):
                bi, g = divmod(it, 4)
                c0, W, tl = BLOCKS[bi]
                rh = [r_hT_t[t] for t in tl]
                if bi == 0 and g == 0:
                    for gg in range(4):
                        S.op("dve", lambda v, o=abuf[gg][0][:, 0:15]: v.memset(o, 0.0), [], [abuf[gg][1]])
                if bi == 4 and g == 0:
                    S.dma("sp", hrow[0:15, :], spool[l], [], [r_hrow])
                    pb, pr = psum()
                    transposes([(pb[:, gg * 16:gg * 16 + 15], hrow[0:15, gg * 128:(gg + 1) * 128], ident_f[0:15, 0:15])
                                for gg in range(4)], None, [r_hrow, r_identf], [pr])
                    for gg in range(4):
                        dve_copy(abuf[gg][0][:, 0:15], pb[:, gg * 16:gg * 16 + 15], [pr], [abuf[gg][1]])
                ab, ar = abuf[g]
                w = POOL_W[g]
                pa, par = psum()
                mm(pa[:, 0:W], [(wa_v[:, k, g * 128:(g + 1) * 128], hT[:, k, c0:c0 + W]) for k in range(8)],
                   [wa_r] + rh, [par])
                act(ab[:, 15:15 + W], pa[:, 0:W], AF.Copy, [par], [ar])
                pg, pgr = psum()
                mm(pg[:, 0:W], [(wg_v[:, k, g * 128:(g + 1) * 128], hT[:, k, c0:c0 + W]) for k in range(8)],
                   [wg_r] + rh, [pgr])
                sg_, sgr = sga[it % 4]
                act(sg_[:, 0:W], pg[:, 0:W], AF.Silu, [pgr], [sgr])
                src, srcr = ab, ar
                sh = 1
                lo = 1
                ti = 0
                while sh < w:
                    dst, dstr = tA[(it + ti) % 3]
                    dve_tt(dst[:, lo:15 + W], src[:, lo:15 + W], src[:, lo - sh:15 + W - sh], ALU.add,
                           [srcr], [dstr])
                    src, srcr = dst, dstr
                    sh *= 2
                    lo += sh
                    ti += 1
                d_, dr = dT[it % 4]
                dve_stt(d_[:, 0:W], src[:, 15:15 + W], 1.0 / w, ab[:, 15:15 + W], ALU.mult, ALU.subtract,
                        [srcr, ar], [dr])
                if bi == 0:
                    dve_tt(fx[:, 0:15], src[:, 15:30], invc[:, g * 16:g * 16 + 15], ALU.mult, [srcr, r_invc], [r_fx])
                    dve_tt(d_[:, 0:15], fx[:, 0:15], ab[:, 15:30], ALU.subtract, [r_fx, ar], [dr])
                if bi < 3:
                    dve_copy(ab[:, 0:15], ab[:, W:W + 15], [ar], [ar])

            def pA_s1(it):
                bi, g = divmod(it, 4)
                c0, W, tl = BLOCKS[bi]
                sq = 0 if bi < 4 else 1
                sg_, sgr = sga[it % 4]
                d_, dr = dT[it % 4]
                py, pyr = psum()
                mm(py[:, 0:W], [(pool_wb[:, g, :], d_[:, 0:W])], [r_poolw, dr], [pyr])
                dve_stt(ycat[:, g, c0:c0 + W], py[:, 0:W], pscale[:, g:g + 1], sg_[:, 0:W], ALU.mult, ALU.mult,
                        [pyr, r_pscale, sgr], [r_yc_t[g][t] for t in tl])
                if bi in (3, 4) and g == 3:
                    pho, phr = psum()
                    transposes([(pho[0:15, gg * 128:(gg + 1) * 128], abuf[gg][0][:, W:W + 15], ident_f)
                                for gg in range(4)], None, [abuf[gg][1] for gg in range(4)] + [r_identf], [phr])
                    act(orow[0:15, :], pho[0:15, :], AF.Copy, [phr], [r_orow])
                    S.dma("sp", o_pool[l, sq], orow[0:15, :], [r_orow], [])

            pipeline([pA_s0, (lambda it: None), pA_s1], 20)
            close(WL["a"])
            close(WL["ga"])

            bB = Bump(arena, *R_MT)
            vnb_p0 = bB.p
            vnb, r_vnb = bB.alloc([128, 17, 512], BF16, "vnb")
            r_vnb_t = [Res(f"vnb{t}") for t in range(NTILE)]
            Mem.claim(vnb_p0, bB.p, r_vnb_t)
            vt = [bB.alloc([128, 512], F32, f"vt{i}") for i in range(2)]
            ssb = [bB.alloc([128, 512], F32, f"ssb{i}") for i in range(2)]
            sgb = [bB.alloc([128, 512], F32, f"sgb{i}") for i in range(2)]
            tB = [bB.alloc([128, 512], F32, f"tB{i}") for i in range(2)]
            bBs = Bump(arena, *R_SCR)
            stB = [bBs.alloc([128, 16], F32, f"stB{i}") for i in range(3)]
            wv_v, wv_r = need(WL["vb"])
            ctxB = {}

            def pB_s0(t):
                c0, R = TILES[t]
                pvb, pvr = psum(hold=True)
                ctxB[t] = (pvb, pvr)
                mm(pvb[0:R, :], [(hT[:, k, c0:c0 + R], wv_v[:, k, :]) for k in range(8)], [wv_r, r_hT_t[t]], [pvr])
                st_, str_ = stB[t % 3]
                S.op("dve", lambda v, o=st_[0:R, 0:6], i=pvb[0:R, :]: v.bn_stats(out=o, in_=i), [pvr], [str_])
                S.op("dve", lambda v, o=st_[0:R, 6:8], i=st_[0:R, 0:6]: v.bn_aggr(out=o, in_=i), [str_], [str_])
                rstd_from_ms(st_[0:R, 7:8], st_[0:R, 8:9], str_, str_)

            def pB_s1(t):
                c0, R = TILES[t]
                pvb, pvr = ctxB.pop(t)
                st_, str_ = stB[t % 3]
                v_, vr_ = vt[t % 2]
                act(st_[0:R, 9:10], st_[0:R, 6:7], AF.Identity, [str_], [str_], scale=st_[0:R, 8:9])
                act(st_[0:R, 9:10], st_[0:R, 9:10], AF.Identity, [str_], [str_], scale=-1.0)
                act(v_[0:R, :], pvb[0:R, :], AF.Identity, [pvr, str_], [vr_], scale=st_[0:R, 8:9], bias=st_[0:R, 9:10])
                punhold(pvr)
                if t == 16:
                    dve_tt(v_[0:R, :], v_[0:R, :], sgug_bc[0:R, :], ALU.mult, [vr_, r_sgug], [vr_])
                    S.dma("sp", o_sgu[l], v_[0:R, :], [vr_], [])
                    act(vnb[0:R, t, :], v_[0:R, :], AF.Copy, [vr_], [r_vnb_t[t]])
                else:
                    S.op("pool", lambda g_, o=vnb[0:R, t, :], a=v_[0:R, :], b=sgug_bc[0:R, :]:
                         g_.tensor_tensor(out=o, in0=a, in1=b, op=ALU.mult), [vr_, r_sgug], [r_vnb_t[t]])

            pipeline([pB_s0, pB_s1], NTILE)
            close(WL["vb"])
            wu_v, wu_r = need(WL["u"])
            wgb_v, wgb_r = need(WL["gb"])
            step = 0
            for bi, (c0, W, tl) in enumerate(BLOCKS):
                rh = [r_hT_t[t] for t in tl]
                for g in range(4):
                    ps_, psr = psum()
                    groups = []
                    for i, t in enumerate(tl):
                        R = TILES[t][1]
                        groups.append((ps_[:, i * 128:i * 128 + R],
                                       [(vnb[0:R, t, g * 128:(g + 1) * 128], WT[0:R, g, 0:R]),
                                        (ones_b[0:1, :], sgub_b[0:1, g * 128:g * 128 + R])]))
                    mm_groups(groups, [r_vnb_t[t] for t in tl] + [r_WT, r_ones, r_sgub], [psr])
                    pu, pur = psum()
                    mm(pu[:, 0:W], [(wu_v[:, k, g * 128:(g + 1) * 128], hT[:, k, c0:c0 + W]) for k in range(8)],
                       [wu_r] + rh, [pur])
                    pg, pgr = psum()
                    mm(pg[:, 0:W], [(wgb_v[:, k, g * 128:(g + 1) * 128], hT[:, k, c0:c0 + W]) for k in range(8)],
                       [wgb_r] + rh, [pgr])
                    s_, sr_ = ssb[step % 2]
                    g_, gr_ = sgb[step % 2]
                    t_, tr_ = tB[step % 2]
                    act(s_[:, 0:W], ps_[:, 0:W], AF.Copy, [psr], [sr_])
                    act(g_[:, 0:W], pg[:, 0:W], AF.Silu, [pgr], [gr_])
                    dve_tt(t_[:, 0:W], pu[:, 0:W], s_[:, 0:W], ALU.mult, [pur, sr_], [tr_])
                    dve_tt(ycat[:, 4 + g, c0:c0 + W], t_[:, 0:W], g_[:, 0:W], ALU.mult, [tr_, gr_],
                           [r_yc_t[4 + g][t] for t in tl])
                    step += 1
            close(WL["u"])
            close(WL["gb"])

            wzl_v, wzl_r = need(WL["zl"])
            wq_v, wq_r = need(WL["q"])
            wk_v, wk_r = need(WL["k"])
            wvc_v, wvc_r = need(WL["vc"])
            bCs = Bump(arena, *R_MT)
            lsb = [bCs.alloc([128, 512], F32, f"lsb{i}") for i in range(1)]
            lhb = [bCs.alloc([128, 512], BF16, f"lhb{i}") for i in range(2)]
            llb = [bCs.alloc([128, 512], BF16, f"llb{i}") for i in range(2)]
            eq = [bCs.alloc([128, 4, 128], F32, f"eq{i}") for i in range(2)]
            ek = [bCs.alloc([128, 4, 128], F32, f"ek{i}") for i in range(2)]
            qtl = [bCs.alloc([128, 4, 128], BF16, f"qtl{i}") for i in range(3)]
            ktl = [bCs.alloc([128, 4, 128], BF16, f"ktl{i}") for i in range(2)]
            ktok = [bCs.alloc([128, 512], BF16, f"ktok{i}") for i in range(2)]
            attm = [bCs.alloc([128, 4, 128], BF16, f"attm{i}") for i in range(2)]
            vb = [bCs.alloc([128, 1024], BF16, f"vb{i}") for i in range(2)]
            ycb = [bCs.alloc([128, 1024], BF16, f"ycb{i}") for i in range(2)]
            bCt = Bump(arena, *R_SCR)
            zhb = [bCt.alloc([17, 128], BF16, f"zhb{i}") for i in range(2)]
            zlb = [bCt.alloc([17, 128], BF16, f"zlb{i}") for i in range(2)]
            ebt = [bCt.alloc([128, 4], F32, f"ebt{i}") for i in range(5)]
            stC = [bCt.alloc([128, 16], F32, f"stC{i}") for i in range(2)]
            junkc, r_junkc = bCt.alloc([128, 256], BF16, "junkc")
            Sf, r_Sf = bCt.alloc([128, 4, 256], F32, "Sf")
            Sbb = [bCt.alloc([128, 4, 256], BF16, f"Sb{i}") for i in range(2)]
            QS = 128.0 ** -0.5
            for zz, zr in zhb:
                S.op("dve", lambda v, o=zz: v.memset(o, 1.0), [], [zr])
            for zz, zr in zlb:
                S.op("dve", lambda v, o=zz: v.memset(o, 0.0), [], [zr])
            S.op("dve", lambda v: v.memset(Sf, 0.0), [], [r_Sf])
            S.op("dve", lambda v, o=Sbb[1][0]: v.memset(o, 0.0), [], [Sbb[1][1]])
            cC = {}

            def rhC(t):
                return [r_hT_t[t]]

            def pC_s0(t):
                c0, R = TILES[t]
                pz, pzr = psum()
                mm(pz[0:16, 0:R], [(wzl_v[:, k, :], hT[:, k, c0:c0 + R]) for k in range(8)], [wzl_r] + rhC(t), [pzr])
                zh_, zhr_ = zhb[t % 2]
                zl_, zlr_ = zlb[t % 2]
                act(zh_[0:16, 0:R], pz[0:16, 0:R], AF.Copy, [pzr], [zhr_])
                dve_tt(zl_[0:16, 0:R], pz[0:16, 0:R], zh_[0:16, 0:R], ALU.subtract, [pzr, zhr_], [zlr_])

            def pC_s1(t):
                c0, R = TILES[t]
                zh_, zhr_ = zhb[t % 2]
                zl_, zlr_ = zlb[t % 2]
                pp, ppr = psum()
                mm(pp[0:R, :], [(zh_[0:17, 0:R], wa_hi[0:17, :]), (zh_[0:17, 0:R], wa_lo[0:17, :]),
                                (zl_[0:17, 0:R], wa_hi[0:17, :])], [zhr_, zlr_, r_wahi, r_walo], [ppr])
                l_, lr_ = lsb[0]
                lh_, lhr_ = lhb[t % 2]
                ll_, llr_ = llb[t % 2]
                act(l_[0:R, :], pp[0:R, :], AF.Exp, [ppr], [lr_], scale=-1.0)
                act(l_[0:R, :], l_[0:R, :], AF.Ln, [lr_, r_onec], [lr_], bias=onec[0:R, :], scale=1.0)
                dve_copy(lh_[0:R, :], l_[0:R, :], [lr_], [lhr_])
                dve_tt(ll_[0:R, :], l_[0:R, :], lh_[0:R, :], ALU.subtract, [lr_, lhr_], [llr_])

            def pC_s2(t):
                c0, R = TILES[t]
                lh_, lhr_ = lhb[t % 2]
                ll_, llr_ = llb[t % 2]
                pbc, pbcr = psum()
                pbc3 = pbc[:, :].rearrange("p (h i) -> p h i", i=128)
                mm_groups([(pbc3[:, h, 0:R], [(lh_[0:R, h * 128:(h + 1) * 128], ucum_b[0:R, 0:R]),
                                              (ll_[0:R, h * 128:(h + 1) * 128], ucum_b[0:R, 0:R])]) for h in range(4)],
                          [lhr_, llr_, r_ucumb], [pbcr])
                eq_, eqr = eq[t % 2]
                ek_, ekr = ek[t % 2]
                eb_, ebr = ebt[t % 5]
                act(eq_[:, :, 0:R], pbc3[:, :, 0:R], AF.Exp, [pbcr], [eqr])
                act(ek_[:, :, 0:R], pbc3[:, :, 0:R], AF.Exp, [pbcr], [ekr], scale=-1.0)
                act(eb_[:, :], pbc3[:, :, R - 1], AF.Exp, [pbcr], [ebr])

            def pC_s3(t):
                c0, R = TILES[t]
                eq_, eqr = eq[t % 2]
                ek_, ekr = ek[t % 2]
                pq, pqr = psum()
                pq3 = pq[:, :].rearrange("p (h i) -> p h i", i=128)
                mm_groups([(pq3[:, h, 0:R], [(wq_v[:, k, h * 128:(h + 1) * 128], hT[:, k, c0:c0 + R]) for k in range(8)])
                           for h in range(4)], [wq_r] + rhC(t), [pqr])
                pk, pkr = psum()
                pk3 = pk[:, :].rearrange("p (h i) -> p h i", i=128)
                mm_groups([(pk3[:, h, 0:R], [(wk_v[:, k, h * 128:(h + 1) * 128], hT[:, k, c0:c0 + R]) for k in range(8)])
                           for h in range(4)], [wk_r] + rhC(t), [pkr])
                q_, qr_ = qtl[t % 3]
                k_, kr_ = ktl[t % 2]
                dve_stt(q_[:, :, 0:R], pq3[:, :, 0:R], QS, eq_[:, :, 0:R], ALU.mult, ALU.mult, [pqr, eqr], [qr_])
                dve_tt(k_[:, :, 0:R], pk3[:, :, 0:R], ek_[:, :, 0:R], ALU.mult, [pkr, ekr], [kr_])

            def pC_s4(t):
                c0, R = TILES[t]
                q_, qr_ = qtl[t % 3]
                k_, kr_ = ktl[t % 2]
                pkt, pktr = psum()
                pkt3 = pbf(pkt)[:, 0:512].rearrange("p (h d) -> p h d", d=128)
                transposes([(pkt3[0:R, h, :], k_[:, h, 0:R], ident_b) for h in range(4)], None, [kr_, r_identb], [pktr])
                kt_, ktr_ = ktok[t % 2]
                act(kt_[0:R, :], pbf(pkt)[0:R, 0:512], AF.Copy, [pktr], [ktr_])
                pat, patr = psum()
                pat3 = pat[:, :].rearrange("p (h i) -> p h i", i=128)
                mm_groups([(pat3[0:R, h, 0:R], [(k_[:, h, 0:R], q_[:, h, 0:R])]) for h in range(4)], [kr_, qr_], [patr])
                at_, atr_ = attm[t % 2]
                dve_tt(at_[0:R, :, 0:R], pat3[0:R, :, 0:R], masku[0:R, 0:R].unsqueeze(1).broadcast_to([R, 4, R]),
                       ALU.mult, [patr, r_masku], [atr_])
                v_, vr_ = vb[t % 2]
                for hf in range(2):
                    pv_, pvr_ = psum()
                    mm(pv_[0:R, :], [(hT[:, k, c0:c0 + R], wvc_v[:, k, hf * 512:(hf + 1) * 512]) for k in range(8)],
                       [wvc_r] + rhC(t), [pvr_])
                    act(v_[0:R, hf * 512:(hf + 1) * 512], pv_[0:R, :], AF.Copy, [pvr_], [vr_])

            def pC_s5(t):
                c0, R = TILES[t]
                sq = 0 if t < 16 else 1
                eb_, ebr = ebt[t % 5]
                q_, qr_ = qtl[t % 3]
                kt_, ktr_ = ktok[t % 2]
                at_, atr_ = attm[t % 2]
                v_, vr_ = vb[t % 2]
                Sprev, r_Sprev = Sbb[(t + 1) % 2]
                Snew, r_Snew = Sbb[t % 2]
                if t == 16:
                    S.dma("sp", Sf, sgla[l].rearrange("h d v -> d h v"), [], [r_Sf])
                    act(Sprev, Sf, AF.Copy, [r_Sf], [r_Sprev])
                st_, str_ = stC[t % 2]
                yc_, ycr_ = ycb[t % 2]
                pos = []
                for hf in range(2):
                    po, por = psum(acc=True)
                    mm_groups([(po[0:R, (h % 2) * 256:(h % 2 + 1) * 256],
                                [(at_[0:R, h, 0:R], v_[0:R, h * 256:(h + 1) * 256]),
                                 (q_[:, h, 0:R], Sprev[:, h, :])]) for h in (2 * hf, 2 * hf + 1)],
                              [atr_, vr_, qr_, r_Sprev], [por])
                    pos.append((po, por))
                for hf in range(2):
                    pkv, pkvr = psum()
                    mm_groups([(pkv[:, (h % 2) * 256:(h % 2 + 1) * 256],
                                [(kt_[0:R, h * 128:(h + 1) * 128], v_[0:R, h * 256:(h + 1) * 256])])
                               for h in (2 * hf, 2 * hf + 1)], [ktr_, vr_], [pkvr])
                    for h in (2 * hf, 2 * hf + 1):
                        dve_ts(Sf[:, h, :], Sf[:, h, :], eb_[:, h:h + 1], None, ALU.mult, ALU.bypass, [r_Sf, ebr], [r_Sf])
                        dve_stt(Sf[:, h, :], pkv[:, (h % 2) * 256:(h % 2 + 1) * 256], eb_[:, h:h + 1], Sf[:, h, :],
                                ALU.mult, ALU.add, [pkvr, ebr, r_Sf], [r_Sf])
                if t in (15, 16):
                    S.dma("sp", o_gla[l, sq].rearrange("h d v -> d h v"), Sf, [r_Sf], [])
                if t < 15:
                    S.op("pool", lambda g_, o=Snew, i=Sf: g_.tensor_copy(out=o, in_=i), [r_Sf], [r_Snew])
                for h in range(4):
                    po, por = pos[h // 2]
                    act(junkc[0:R, :], po[0:R, (h % 2) * 256:(h % 2 + 1) * 256], AF.Square, [por], [r_junkc, str_],
                        accum_out=st_[0:R, h:h + 1])
                rstd_from_ms(st_[0:R, 0:4], st_[0:R, 8:12], str_, str_, scale=1.0 / 256)
                for h in range(4):
                    po, por = pos[h // 2]
                    dve_stt(yc_[0:R, h * 256:(h + 1) * 256], po[0:R, (h % 2) * 256:(h % 2 + 1) * 256],
                            st_[0:R, 8 + h:9 + h], glag_bc[0:R, :], ALU.mult, ALU.mult, [por, str_, r_glag], [ycr_])

            def pC_s6(t):
                c0, R = TILES[t]
                yc_, ycr_ = ycb[t % 2]
                pyt, pytr = psum()
                pyt3 = pbf(pyt).rearrange("p (c t) -> p c t", t=128)
                transposes([(pyt3[:, c, 0:R], yc_[0:R, c * 128:(c + 1) * 128], ident_b[0:R, 0:R]) for c in range(8)],
                           None, [ycr_, r_identb], [pytr])
                act(ycat[:, 8:16, c0:c0 + R], pyt3[:, :, 0:R], AF.Copy, [pytr], [r_yc_t[8 + c][t] for c in range(8)])

            def c2_item(hf, bi, cc):
                def emit():
                    c0, W, tl = BLOCKS[bi]
                    c = hf * 4 + cc
                    wg_v, wg_r = need(WL["gc"][hf])
                    pg, pgr = psum()
                    mm(pg[:, 0:W], [(wg_v[:, k, cc * 128:(cc + 1) * 128], hT[:, k, c0:c0 + W]) for k in range(8)],
                       [wg_r] + [r_hT_t[t] for t in tl], [pgr])
                    act(pg[:, 0:W], pg[:, 0:W], AF.Silu, [pgr], [pgr])
                    rr = [r_yc_t[8 + c][t] for t in tl]
                    dve_tt(ycat[:, 8 + c, c0:c0 + W], ycat[:, 8 + c, c0:c0 + W], pg[:, 0:W], ALU.mult, rr + [pgr], rr)
                return emit
            c2_items = [[c2_item(hf, bi, cc) for bi in range(5) for cc in range(4)] for hf in range(2)]

            def c1_filler():
                for _ in range(2):
                    if len(c2_items[0]) > 8:
                        c2_items[0].pop(0)()

            nrot[0] = 6
            pipeline([pC_s0, pC_s1, pC_s2, pC_s3, pC_s4, pC_s5, pC_s6], NTILE, filler=c1_filler,
                     order=[6, 5, 4, 3, 2, 1, 0])
            nrot[0] = 8
            for nm in ("zl", "q", "k", "vc"):
                close(WL[nm])
            for hf in range(2):
                while c2_items[hf]:
                    c2_items[hf].pop(0)()
                close(WL["gc"][hf])
                if hf == 0 and l + 1 < DEPTH:
                    small_params(l + 1, (R_MT[0] + 24576, R_MT[1]))
                    ada(l + 1, (R_MT[0], R_MT[0] + 24576))
            if DEBUG and l == 0:
                S.barrier()
                S.dma("pool", dbg["hT"], hT.rearrange("p k t -> p (k t)"), [], [])
                S.dma("pool", dbg["ycat"], ycat.rearrange("p k t -> p (k t)"), [], [])
                S.barrier()

            Mem.claim(R_MT[0], R_MT[1], r_mT_t)
            bM = Bump(arena, *R_SCR)
            gsg = [bM.alloc([128, 512], F32, f"gsg{i}") for i in range(3)]
            tM = [bM.alloc([128, 512], F32, f"tM{i}") for i in range(2)]
            KOFF = (0, 4, 8)
            KN = (4, 4, 8)
            for dp in range(4):
                wo_, wor_ = need(WL["wo"][dp])
                wg_, wgr_ = need(WL["wgm"][dp])
                for dd in range(2):
                    dm = dp * 2 + dd
                    ds = slice(dd * 128, (dd + 1) * 128)
                    for bi, (c0, W, tl) in enumerate(BLOCKS):
                        rh = [r_hT_t[t] for t in tl]
                        for i in range(3):
                            pg, pgr = psum()
                            mm(pg[:, 0:W], [(wg_[:, k, i, ds], hT[:, k, c0:c0 + W]) for k in range(8)], [wgr_] + rh, [pgr])
                            act(gsg[i][0][:, 0:W], pg[:, 0:W], AF.Sigmoid, [pgr], [gsg[i][1]])
                        for i in range(3):
                            pp_, ppr_ = psum()
                            ry = [r_yc_t[KOFF[i] + kk][t] for kk in range(KN[i]) for t in tl]
                            mm(pp_[:, 0:W], [(wo_[:, KOFF[i] + kk, ds], ycat[:, KOFF[i] + kk, c0:c0 + W])
                                             for kk in range(KN[i])], [wor_] + ry, [ppr_])
                            if i == 0:
                                dve_tt(tM[0][0][:, 0:W], pp_[:, 0:W], gsg[0][0][:, 0:W], ALU.mult,
                                       [ppr_, gsg[0][1]], [tM[0][1]])
                            elif i == 1:
                                dve_tt(tM[1][0][:, 0:W], pp_[:, 0:W], gsg[1][0][:, 0:W], ALU.mult,
                                       [ppr_, gsg[1][1]], [tM[1][1]])
                                dve_tt(tM[0][0][:, 0:W], tM[0][0][:, 0:W], tM[1][0][:, 0:W], ALU.add,
                                       [tM[0][1], tM[1][1]], [tM[0][1]])
                            else:
                                dve_tt(tM[1][0][:, 0:W], pp_[:, 0:W], gsg[2][0][:, 0:W], ALU.mult,
                                       [ppr_, gsg[2][1]], [tM[1][1]])
                                dve_tt(mT[:, dm, c0:c0 + W], tM[0][0][:, 0:W], tM[1][0][:, 0:W], ALU.add,
                                       [tM[0][1], tM[1][1]], [r_mT_t[t] for t in tl])
                close(WL["wo"][dp])
                close(WL["wgm"][dp])
            if DEBUG and l == 0:
                S.dma("pool", dbg["mT"], mT.rearrange("p k t -> p (k t)"), [], [])
                S.barrier()

            fuse = l + 1 < DEPTH
            if fuse:
                Mem.claim(R_HT[0], R_HT[1], r_hT_t)
            bO = Bump(arena, *R_YC)
            bOs = Bump(arena, *R_SCR)
            wout_v, wout_r = need(WL["wout"])
            G_bc, r_Gbc = bO.alloc([128, 1024], F32, "G_bc")
            idg, r_idg = bO.alloc([128, 8, 128], F32, "idg")
            NXO = 9 if fuse else 6
            xo = [bO.alloc([128, 1024], F32, f"xo{i}") for i in range(NXO)]
            tO = [bO.alloc([128, 1024], F32, f"tO{i}") for i in range(2)]
            stO = [bOs.alloc([128, 8], F32, f"stO{i}") for i in range(5)]
            junko, r_junko = bOs.alloc([128, 512], BF16, "junko")
            cO = {}

            def pO_ld(t):
                c0, R = TILES[t]
                xa, xr = xo[t % NXO]
                S.dma("sp", xa[0:R, :], x_src[c0:c0 + R, :], [r_x1s[t]], [xr])

            def pO_nop(t):
                pass

            def pO_mm(t):
                c0, R = TILES[t]
                sa, sr = stO[t % 5]
                pos = []
                for hf in range(2):
                    po, por = psum(hold=True)
                    mm(po[0:R, :], [(mT[:, k, c0:c0 + R], wout_v[:, k, hf * 512:(hf + 1) * 512]) for k in range(8)],
                       [r_mT_t[t], wout_r], [por])
                    act(junko[0:R, :], po[0:R, :], AF.Square, [por], [r_junko, sr], accum_out=sa[0:R, hf:hf + 1])
                    pos.append((po, por))
                cO[t] = pos

            def pO_sc(t):
                c0, R = TILES[t]
                sa, sr = stO[t % 5]
                act(sa[0:R, 2:3], sa[0:R, 1:2], AF.Identity, [sr, r_epsc], [sr], scale=1.0 / D, bias=epsc[0:R, :])
                act(sa[0:R, 4:5], sa[0:R, 0:1], AF.Ln, [sr], [sr], scale=1.0 / D, bias=sa[0:R, 2:3])
                act(sa[0:R, 4:5], sa[0:R, 4:5], AF.Exp, [sr], [sr], scale=-0.5)

            def pO_gt(t):
                c0, R = TILES[t]
                if t in (0, 16):
                    sq = 0 if t < 16 else 1
                    for k in range(8):
                        dve_ts(idg[:, k, :], ident_f, modcL[:, 16 + k, sq:sq + 1], None, ALU.mult, ALU.bypass,
                               [r_identf, r_modcL], [r_idg])
                    for hf in range(2):
                        pb, pr = psum()
                        mm_groups([(pb[:, kk * 128:(kk + 1) * 128], [(ones_f, idg[:, hf * 4 + kk, :])]) for kk in range(4)],
                                  [r_onesf, r_idg], [pr])
                        act(G_bc[:, hf * 512:(hf + 1) * 512], pb[:, :], AF.Copy, [pr], [r_Gbc])
                ta, tr = tO[t % 2]
                sa, sr = stO[t % 5]
                pos = cO.pop(t)
                for hf in range(2):
                    po, por = pos[hf]
                    hs = slice(hf * 512, (hf + 1) * 512)
                    dve_stt(ta[0:R, hs], po[0:R, :], sa[0:R, 4:5], G_bc[0:R, hs], ALU.mult, ALU.mult,
                            [por, sr, r_Gbc], [tr])
                    punhold(por)

            def pO_add(t):
                c0, R = TILES[t]
                xa, xr = xo[t % NXO]
                ta, tr = tO[t % 2]
                S.op("pool", lambda g, o=xa[0:R, :], a=xa[0:R, :], b=ta[0:R, :]:
                     g.tensor_tensor(out=o, in0=a, in1=b, op=ALU.add), [xr, tr], [xr])
                S.dma("sp", x_dst[c0:c0 + R, :], xa[0:R, :], [xr], [r_x1s[t]])

            stages = [pO_ld, pO_nop, pO_mm, pO_sc, pO_gt, pO_add]
            if fuse:
                stages += p0_stages(l + 1, bO, lambda t: xo[t % NXO], bOs)
            pipeline(stages, NTILE)
            close(WL["wout"])

        S.final_wait("sp")

        keys = S.sem_keys()
        sems = {}
        for kx in keys:
            sems[kx] = es.enter_context(nc.semaphore("s_" + "_".join(str(p) for p in kx)))
        with nc.Block() as block:
            @block.sync
            def _(e):
                S.emit("sp", e, sems)

            @block.gpsimd
            def _(e):
                S.emit("pool", e, sems)

            @block.scalar
            def _(e):
                S.emit("act", e, sems)

            @block.vector
            def _(e):
                S.emit("dve", e, sems)

            @block.tensor
            def _(e):
                S.emit("pe", e, sems)
    return nc


def _consts():
    j = np.arange(128)[:, None]
    i = np.arange(128)[None, :]
    masku = (j <= i).astype(np.float32)
    ucum = (masku * (-1.0 / 16.0)).astype(np.float32)
    invc = np.ones((128, 64), np.float32)
    for g, w in enumerate(POOL_W):
        for t in range(16):
            invc[:, g * 16 + t] = 1.0 / min(w, t + 1)
    return {"k_ident": np.eye(128, dtype=np.float32), "k_masku": masku, "k_ucum": ucum, "k_invc": invc}


_NC_CACHE = {}


def kernel(x_prompt, x_sample, state_pool, state_gla, c_prompt, c_sample,
           ada_w, ada_b, pre_norm_g, post_norm_g, w_in, pool_w, pool_scale,
           sgu_norm_g, sgu_w, sgu_b, gla_wa2, gla_ba, gla_norm_g,
           w_oa, w_ob, w_oc, w_out):
    f = lambda a: np.ascontiguousarray(np.asarray(a, dtype=np.float32))
    x_prompt, x_sample, state_pool, state_gla, c_prompt, c_sample = map(
        f, (x_prompt, x_sample, state_pool, state_gla, c_prompt, c_sample))
    shared = {
        "ada_w": f(ada_w), "ada_b": f(ada_b), "pre_g": f(pre_norm_g), "post_g": f(post_norm_g), "w_in": f(w_in),
        "pool_w": f(pool_w), "pool_scale": f(pool_scale), "sgu_g": f(sgu_norm_g), "sgu_w": f(sgu_w),
        "sgu_b": f(sgu_b).reshape(DEPTH, 512), "gla_wa2": f(gla_wa2), "gla_ba": f(gla_ba), "gla_g": f(gla_norm_g),
        "w_oa": f(w_oa), "w_ob": f(w_ob), "w_oc": f(w_oc), "w_out": f(w_out),
    }
    shared.update(_consts())
    n = 8
    in_maps = []
    for b in range(n):
        m = dict(shared)
        m["xin"] = np.ascontiguousarray(np.concatenate([x_prompt[b], x_sample[b]], axis=0))
        m["c2"] = np.ascontiguousarray(np.stack([c_prompt[b], c_sample[b]], axis=0))
        m["spool"] = np.ascontiguousarray(state_pool[:, b])
        m["sgla"] = np.ascontiguousarray(state_gla[:, b])
        in_maps.append(m)
    if "nc" not in _NC_CACHE:
        _NC_CACHE["nc"] = build_program()
    nc = _NC_CACHE["nc"]
    res = run_bass_kernel_spmd(nc, in_maps, core_ids=list(range(n)))
    rs = res.results
    if DEBUG:
        kernel.dbg = rs
    y_prompt = np.stack([rs[b]["y"][:TP] for b in range(n)], axis=0)
    y_sample = np.stack([rs[b]["y"][TP:] for b in range(n)], axis=0)
    pool_p = np.stack([rs[b]["o_pool"][:, 0] for b in range(n)], axis=1)
    pool_s = np.stack([rs[b]["o_pool"][:, 1] for b in range(n)], axis=1)
    gla_p = np.stack([rs[b]["o_gla"][:, 0] for b in range(n)], axis=1)
    gla_s = np.stack([rs[b]["o_gla"][:, 1] for b in range(n)], axis=1)
    sgu_s = np.stack([rs[b]["o_sgu"] for b in range(n)], axis=1)
    return (y_prompt.astype(np.float32), y_sample.astype(np.float32), pool_p.astype(np.float32),
            gla_p.astype(np.float32), pool_s.astype(np.float32), gla_s.astype(np.float32),
            sgu_s.astype(np.float32))
```
